# Optimizing a Trainium2 kernel written in Bass

```python
import math
import jax
import jax.numpy as jnp
from jax import lax
import numpy as np

D_MODEL = 1024
BATCH = 16
SEQ = 2048
DEPTH = 2

GRID_W = 64
CTX_LEN = 256
HEAD_DIM = 64
ROPE_THETA = 10000.0
EPS = 1e-6
NEG_INF = -1e30
BLOCK = 128
CONV_WIDTH = 3
A_WIDTH = D_MODEL // 2
B_HEADS = D_MODEL // 256
B_QK_DIM = HEAD_DIM
B_V_DIM = 2 * HEAD_DIM
B_QK_WIDTH = B_HEADS * 2 * B_QK_DIM
B_V_WIDTH = B_HEADS * B_V_DIM
C_HEADS = D_MODEL // 128
C_KV_HEADS = C_HEADS // 4
C_GROUP = C_HEADS // C_KV_HEADS
C_Q_WIDTH = C_HEADS * HEAD_DIM
C_KV_WIDTH = C_KV_HEADS * HEAD_DIM
WINDOW = 128
POOL_SIZES = (2, 4, 8, 16)
D_WIDTH = D_MODEL // 2
POOL_GROUP = D_WIDTH // len(POOL_SIZES)
D_FF = ((8 * D_MODEL // 3 + 127) // 128) * 128
MIX_WIDTH = A_WIDTH + B_V_WIDTH
AB_COLS = 3 * A_WIDTH + 2 * B_QK_WIDTH + B_V_WIDTH
CD_COLS = C_Q_WIDTH + 2 * C_KV_WIDTH + D_WIDTH
N_EVEN = (DEPTH + 1) // 2
N_ODD = DEPTH // 2

kernel_name = 'hybrid_flow_backbone_block'


def rms_norm(x, w=None):
    xf = x.astype(jnp.float32)
    y = xf * lax.rsqrt(jnp.mean(xf * xf, axis=-1, keepdims=True) + EPS)
    if w is not None:
        y = y * w.astype(jnp.float32)
    return y.astype(x.dtype)


def modulate(xn, shift, scale):
    return xn * (1.0 + scale) + shift


def ada_mods(cvec, w, b):
    return jnp.split(jax.nn.silu(cvec) @ w + b, 6, axis=-1)


def lambda_init(layer):
    return 0.8 - 0.6 * math.exp(-0.3 * layer)


def dwconv3(x, w):
    xp = jnp.pad(x, ((0, 0), (1, 1), (0, 0)))
    return xp[:, :-2] * w[0] + xp[:, 1:-1] * w[1] + xp[:, 2:] * w[2]


def axial_rope_angles(n):
    rows = n // GRID_W
    row = jnp.repeat(jnp.arange(rows, dtype=jnp.int32), GRID_W)
    col = jnp.tile(jnp.arange(GRID_W, dtype=jnp.int32), rows)
    axis_dim = HEAD_DIM // 2
    inv = 1.0 / (ROPE_THETA ** (jnp.arange(0, axis_dim, 2, dtype=jnp.float32) / axis_dim))
    return row.astype(jnp.float32)[:, None] * inv, col.astype(jnp.float32)[:, None] * inv


def rope_1d(x, ang):
    x1, x2 = jnp.split(x, 2, axis=-1)
    cos = jnp.cos(ang).astype(x.dtype)
    sin = jnp.sin(ang).astype(x.dtype)
    return jnp.concatenate([x1 * cos - x2 * sin, x1 * sin + x2 * cos], axis=-1)


def apply_axial_rope(x, ang_row, ang_col):
    shape = (x.shape[1],) + (1,) * (x.ndim - 3) + (ang_row.shape[-1],)
    xr, xc = jnp.split(x, 2, axis=-1)
    return jnp.concatenate([rope_1d(xr, ang_row.reshape(shape)), rope_1d(xc, ang_col.reshape(shape))], axis=-1)


def diff_attend(q, k, v, lam):
    s = jnp.einsum('bqhmd,bkhmd->bhmqk', q, k).astype(jnp.float32) * (B_QK_DIM ** -0.5)
    p = jax.nn.softmax(s, axis=-1)
    w = (p[:, :, 0] - lam * p[:, :, 1]).astype(v.dtype)
    return jnp.einsum('bhqk,bkhe->bqhe', w, v)


def sink_attend(q, ks, vs, masks, sink):
    b, nq = q.shape[0], q.shape[1]
    logits = [jnp.broadcast_to(sink[None, :, :, None, None], (b, C_KV_HEADS, C_GROUP, nq, 1))]
    for k, m in zip(ks, masks):
        s = jnp.einsum('bqhgd,bkhd->bhgqk', q, k).astype(jnp.float32)
        if m is not None:
            s = jnp.where(m, s, NEG_INF)
        logits.append(s)
    p = jax.nn.softmax(jnp.concatenate(logits, axis=-1), axis=-1)
    terms = []
    start = 1
    for k, v in zip(ks, vs):
        pk = lax.slice_in_dim(p, start, start + k.shape[1], axis=-1).astype(v.dtype)
        terms.append(jnp.einsum('bhgqk,bkhd->bqhgd', pk, v))
        start += k.shape[1]
    return sum(terms)


def multiscale_pool(p, w_pool, scale):
    b, n, _ = p.shape
    pf = p.astype(jnp.float32)
    cs = jnp.concatenate([jnp.zeros((b, 1, D_WIDTH), jnp.float32), jnp.cumsum(pf, axis=1)], axis=1)
    t = jnp.arange(n, dtype=jnp.int32)
    outs = []
    for g, w in enumerate(POOL_SIZES):
        lo = jnp.clip(t - w // 2, 0, n)
        hi = jnp.clip(t + w // 2, 0, n)
        sl = slice(g * POOL_GROUP, (g + 1) * POOL_GROUP)
        csg = cs[:, :, sl]
        mean = (csg[:, hi] - csg[:, lo]) / (hi - lo).astype(jnp.float32)[:, None]
        outs.append(mean - pf[:, :, sl])
    z = jnp.stack(outs, axis=2).astype(p.dtype)
    z = jnp.einsum('bngc,gce->bnge', z, w_pool).reshape(b, n, D_WIDTH)
    return z * scale


def conv_ffn(h, w_up, conv_w, conv_b, w_down):
    u = dwconv3(h @ w_up, conv_w) + conv_b
    a, g = jnp.split(u, 2, axis=-1)
    return (jax.nn.silu(g) * a) @ w_down


def mixer_ab(hl, hc, w_in, conv_w, lam_qk, subln_w, w_out, lam_init, ang_row, ang_col, ctx_out):
    b, n, _ = hl.shape
    lc = hc.shape[1]
    splits = [A_WIDTH, 2 * A_WIDTH, 3 * A_WIDTH, 3 * A_WIDTH + B_QK_WIDTH, 3 * A_WIDTH + 2 * B_QK_WIDTH]
    gb, gc, hv, q, k, v = jnp.split(hl @ w_in, splits, axis=-1)
    gbc, gcc, hvc, qc, kc, vc = jnp.split(hc @ w_in, splits, axis=-1)
    lq = lam_qk.astype(jnp.float32)
    lam = jnp.exp(jnp.sum(lq[0] * lq[1])) - jnp.exp(jnp.sum(lq[2] * lq[3])) + lam_init
    ql = apply_axial_rope(q.reshape(b, n, B_HEADS, 2, B_QK_DIM), ang_row, ang_col)
    kl = apply_axial_rope(k.reshape(b, n, B_HEADS, 2, B_QK_DIM), ang_row, ang_col)
    vl = v.reshape(b, n, B_HEADS, B_V_DIM)
    kcx = kc.reshape(b, lc, B_HEADS, 2, B_QK_DIM)
    vcx = vc.reshape(b, lc, B_HEADS, B_V_DIM)
    k_all = jnp.concatenate([kcx, kl], axis=1)
    v_all = jnp.concatenate([vcx, vl], axis=1)
    nb = n // BLOCK
    qb = ql.reshape(b, nb, BLOCK, B_HEADS, 2, B_QK_DIM).swapaxes(0, 1)
    ob = lax.map(lambda qi: diff_attend(qi, k_all, v_all, lam), qb)
    yb = ob.swapaxes(0, 1).reshape(b, n, B_HEADS, B_V_DIM)
    yb = (rms_norm(yb, subln_w) * (1.0 - lam_init)).reshape(b, n, B_V_WIDTH)
    ya = gb * dwconv3(gc * hv, conv_w)
    yl = jnp.concatenate([ya, yb], axis=-1) @ w_out
    yc = None
    if ctx_out:
        ybc = diff_attend(qc.reshape(b, lc, B_HEADS, 2, B_QK_DIM), kcx, vcx, lam)
        ybc = (rms_norm(ybc, subln_w) * (1.0 - lam_init)).reshape(b, lc, B_V_WIDTH)
        yac = gbc * dwconv3(gcc * hvc, conv_w)
        yc = jnp.concatenate([yac, ybc], axis=-1) @ w_out
    return yl, yc


def mixer_cd(hl, hc, w_in, sink, pool_w, pool_scale, w_out, ang_row, ang_col, ctx_out):
    b, n, _ = hl.shape
    lc = hc.shape[1]
    splits = [C_Q_WIDTH, C_Q_WIDTH + C_KV_WIDTH, C_Q_WIDTH + 2 * C_KV_WIDTH]
    q, k, v, pl = jnp.split(hl @ w_in, splits, axis=-1)
    qc, kc, vc, pc = jnp.split(hc @ w_in, splits, axis=-1)
    scale = HEAD_DIM ** -0.5
    sink = sink.astype(jnp.float32).reshape(C_KV_HEADS, C_GROUP)
    ql = apply_axial_rope(q.reshape(b, n, C_KV_HEADS, C_GROUP, HEAD_DIM), ang_row, ang_col) * scale
    kl = apply_axial_rope(k.reshape(b, n, C_KV_HEADS, HEAD_DIM), ang_row, ang_col)
    vl = v.reshape(b, n, C_KV_HEADS, HEAD_DIM)
    kcx = kc.reshape(b, lc, C_KV_HEADS, HEAD_DIM)
    vcx = vc.reshape(b, lc, C_KV_HEADS, HEAD_DIM)
    pad = ((0, 0), (WINDOW, WINDOW), (0, 0), (0, 0))
    k_pad = jnp.pad(kl, pad)
    v_pad = jnp.pad(vl, pad)
    band = BLOCK + 2 * WINDOW
    nb = n // BLOCK
    qb = ql.reshape(b, nb, BLOCK, C_KV_HEADS, C_GROUP, HEAD_DIM).swapaxes(0, 1)

    def block(args):
        i, qi = args
        kb = lax.dynamic_slice_in_dim(k_pad, i * BLOCK, band, axis=1)
        vb = lax.dynamic_slice_in_dim(v_pad, i * BLOCK, band, axis=1)
        qpos = i * BLOCK + jnp.arange(BLOCK, dtype=jnp.int32)
        kpos = i * BLOCK - WINDOW + jnp.arange(band, dtype=jnp.int32)
        mask = (jnp.abs(kpos[None, :] - qpos[:, None]) <= WINDOW) & ((kpos >= 0) & (kpos < n))[None, :]
        return sink_attend(qi, [kcx, kb], [vcx, vb], [None, mask], sink)

    ob = lax.map(block, (jnp.arange(nb, dtype=jnp.int32), qb))
    ya = ob.swapaxes(0, 1).reshape(b, n, C_Q_WIDTH)
    yd = multiscale_pool(pl, pool_w, pool_scale)
    yl = jnp.concatenate([ya, yd], axis=-1) @ w_out
    yc = None
    if ctx_out:
        qcx = qc.reshape(b, lc, C_KV_HEADS, C_GROUP, HEAD_DIM) * scale
        yac = sink_attend(qcx, [kcx], [vcx], [None], sink).reshape(b, lc, C_Q_WIDTH)
        ydc = multiscale_pool(pc, pool_w, pool_scale)
        yc = jnp.concatenate([yac, ydc], axis=-1) @ w_out
    return yl, yc


def setup_inputs(seed: int = 0) -> dict:
    key = jax.random.key(seed)
    ks = jax.random.split(key, 24)

    def nrm(k, shape, s):
        return jax.random.normal(k, shape, jnp.float32) * s

    D = D_MODEL
    return {
        'x': nrm(ks[0], (BATCH, SEQ, D), 1.0),
        'c': nrm(ks[1], (BATCH, D), 1.0),
        'ctx': nrm(ks[2], (BATCH, CTX_LEN, D), 1.0),
        'c_ctx': nrm(ks[3], (D,), 1.0),
        'w_mod': nrm(ks[4], (DEPTH, D, 6 * D), 0.5 * D ** -0.5),
        'b_mod': nrm(ks[5], (DEPTH, 6 * D), 0.01),
        'w_in_ab': nrm(ks[6], (N_EVEN, D, AB_COLS), D ** -0.5),
        'conv_a': nrm(ks[7], (N_EVEN, CONV_WIDTH, A_WIDTH), CONV_WIDTH ** -0.5),
        'lam_qk': nrm(ks[8], (N_EVEN, 4, B_QK_DIM), 0.1),
        'subln_b': 1.0 + nrm(ks[9], (N_EVEN, B_V_DIM), 0.02),
        'w_out_ab': nrm(ks[10], (N_EVEN, MIX_WIDTH, D), MIX_WIDTH ** -0.5),
        'w_in_cd': nrm(ks[11], (N_ODD, D, CD_COLS), D ** -0.5),
        'sink_c': nrm(ks[12], (N_ODD, C_HEADS), 0.5),
        'pool_w': nrm(ks[13], (N_ODD, len(POOL_SIZES), POOL_GROUP, POOL_GROUP), POOL_GROUP ** -0.5),
        'pool_scale': 1.0 + nrm(ks[14], (N_ODD, D_WIDTH), 0.1),
        'w_out_cd': nrm(ks[15], (N_ODD, MIX_WIDTH, D), MIX_WIDTH ** -0.5),
        'w_up': nrm(ks[16], (DEPTH, D, 2 * D_FF), D ** -0.5),
        'conv_ffn_w': nrm(ks[17], (DEPTH, CONV_WIDTH, 2 * D_FF), CONV_WIDTH ** -0.5),
        'conv_ffn_b': nrm(ks[18], (DEPTH, 2 * D_FF), 0.01),
        'w_down': nrm(ks[19], (DEPTH, D_FF, D), D_FF ** -0.5),
        'final_norm_w': 1.0 + nrm(ks[20], (D,), 0.02),
    }


def reference(x, c, ctx, c_ctx, w_mod, b_mod, w_in_ab, conv_a, lam_qk, subln_b, w_out_ab, w_in_cd, sink_c, pool_w, pool_scale, w_out_cd, w_up, conv_ffn_w, conv_ffn_b, w_down, final_norm_w):
    ang_row, ang_col = axial_rope_angles(x.shape[1])
    xl, xc = x, ctx
    for l in range(DEPTH):
        ctx_out = l < DEPTH - 1
        ml = [m[:, None, :] for m in ada_mods(c, w_mod[l], b_mod[l])]
        mc = ada_mods(c_ctx, w_mod[l], b_mod[l])
        hl = modulate(rms_norm(xl), ml[0], ml[1])
        hc = modulate(rms_norm(xc), mc[0], mc[1])
        if l % 2 == 0:
            e = l // 2
            yl, yc = mixer_ab(hl, hc, w_in_ab[e], conv_a[e], lam_qk[e], subln_b[e], w_out_ab[e], lambda_init(l), ang_row, ang_col, ctx_out)
        else:
            o = l // 2
            yl, yc = mixer_cd(hl, hc, w_in_cd[o], sink_c[o], pool_w[o], pool_scale[o], w_out_cd[o], ang_row, ang_col, ctx_out)
        xl = xl + ml[2] * yl
        ffn = (w_up[l], conv_ffn_w[l], conv_ffn_b[l], w_down[l])
        xl = xl + ml[5] * conv_ffn(modulate(rms_norm(xl), ml[3], ml[4]), *ffn)
        if ctx_out:
            xc = xc + mc[2] * yc
            xc = xc + mc[5] * conv_ffn(modulate(rms_norm(xc), mc[3], mc[4]), *ffn)
    return rms_norm(xl, final_norm_w)
```

```python
import contextlib
import math
import numpy as np
import concourse.bass as bass
import concourse.mybir as mybir
from concourse.bass_utils import run_bass_kernel_spmd

F32 = mybir.dt.float32
BF16 = mybir.dt.bfloat16
AF = mybir.ActivationFunctionType
ALU = mybir.AluOpType
AX = mybir.AxisListType

D = 1024
NL = 2048
NCX = 256
NT = NL + NCX
DFF = 2816
EPS = 1e-6
NCORES = 8
BPC = 2
SEG = 256

DEBUG_STOP = None


class Eng:
    def __init__(self, name, e, sem, sid):
        self.name, self.e, self.sem, self.sid, self.cnt, self.seen = name, e, sem, sid, 0, {}


class Buf:
    __slots__ = ("name", "w", "r", "dsem", "dsid", "dcnt")

    def __init__(self, name):
        self.name, self.w, self.r = name, {}, {}
        self.dsem = None
        self.dcnt = 0


def _merge(dst, src):
    for k, v in src.items():
        if dst.get(k, 0) < v:
            dst[k] = v


class Sched:
    def __init__(self, nc, es):
        self.nc, self.es = nc, es
        self.sems = {}
        self.nsem = 0
        self.eng = {}
        for name, e in (("pe", nc.tensor), ("act", nc.scalar), ("dve", nc.vector), ("pool", nc.gpsimd), ("sp", nc.sync)):
            sem, sid = self.new_sem("e_" + name)
            self.eng[name] = Eng(name, e, sem, sid)

    def new_sem(self, name):
        sem = self.es.enter_context(self.nc.semaphore("%s_%d" % (name, self.nsem)))
        sid = self.nsem
        self.nsem += 1
        self.sems[sid] = sem
        return sem, sid

    def _need(self, reads, writes, acc, own_sid):
        need = {}
        for b in reads:
            _merge(need, b.w)
        for b in writes:
            if b.r:
                _merge(need, b.r)
            else:
                _merge(need, b.w)
        for b in acc:
            if b.r:
                _merge(need, b.r)
            else:
                for k, v in b.w.items():
                    if k != own_sid and need.get(k, 0) < v:
                        need[k] = v
        return need

    def _wait(self, E, need):
        for sid, val in need.items():
            if E.seen.get(sid, 0) < val:
                E.e.wait_ge(self.sems[sid], val)
                E.seen[sid] = val

    def op(self, en, fn, reads=(), writes=(), acc=()):
        E = self.eng[en]
        self._wait(E, self._need(reads, writes, acc, E.sid))
        ins = fn(E.e)
        E.cnt += 1
        ins.then_inc(E.sem, 1)
        for b in reads:
            if b.r.get(E.sid, 0) < E.cnt:
                b.r[E.sid] = E.cnt
        for b in writes:
            b.w = {E.sid: E.cnt}
            b.r = {}
        for b in acc:
            b.w = {E.sid: E.cnt}
            b.r = {}

    def dma(self, qn, out_ap, in_ap, owner, reads=(), writes=(), nowait=False):
        E = self.eng[qn]
        if owner.dsem is None:
            owner.dsem, owner.dsid = self.new_sem("d_" + owner.name)
        if not nowait:
            self._wait(E, self._need(reads, writes, (), -1))
        pairs = out_ap if isinstance(out_ap, list) else [(out_ap, in_ap)]
        for (o_, i_) in pairs:
            ins = E.e.dma_start(out=o_, in_=i_)
            owner.dcnt += 16
            ins.then_inc(owner.dsem, 16)
        for b in reads:
            if b.r.get(owner.dsid, 0) < owner.dcnt:
                b.r[owner.dsid] = owner.dcnt
        for b in writes:
            b.w = {owner.dsid: owner.dcnt}
            b.r = {}

    def sync_to(self, qn, names=("pe", "act", "dve")):
        self._wait(self.eng[qn], {self.eng[n].sid: self.eng[n].cnt for n in names if self.eng[n].cnt > 0})

    def barrier(self, names=("pe", "act", "dve", "sp")):
        clock = {self.eng[n].sid: self.eng[n].cnt for n in names if self.eng[n].cnt > 0}
        for n in names:
            E = self.eng[n]
            self._wait(E, {k: v for k, v in clock.items() if k != E.sid})


class Grid:
    def __init__(self, name, t):
        self.name, self.t, self.b = name, t, {}

    def R(self, c, a, b):
        out = []
        for k in range(a // SEG, (b - 1) // SEG + 1):
            key = (c, k)
            if key not in self.b:
                self.b[key] = Buf("%s_%s_%d" % (self.name, c, k))
            out.append(self.b[key])
        return out

    def RC(self, cs, a, b):
        out = []
        for c in cs:
            out += self.R(c, a, b)
        return out


def pc(t):
    return 1 + t if t < NL else 3 + t


def lam_init(layer):
    return 0.8 - 0.6 * math.exp(-0.3 * layer)


def build_program():
    nc = bass.Bass("TRN2", target_bir_lowering=False)
    di = {}

    def din(name, shape, dt=F32):
        di[name] = nc.dram_tensor(name, list(shape), dt, kind="ExternalInput").ap()
        return di[name]

    x_d = din("x", [BPC, NL, D])
    ctx_d = din("ctx", [BPC, NCX, D])
    cT_d = din("cT", [128, 8, 3])
    wmod_d = din("w_mod", [2, D, 6 * D])
    bmod_d = din("b_mod", [128, 2, 48])
    winab_d = din("w_in_ab", [D, 4096])
    conva_d = din("conv_a", [128, 4, 3])
    lamqk_d = din("lam_qk", [128, 4, 64])
    subln_d = din("subln", [128, 1])
    woutab_d = din("w_out_ab", [D, D])
    wincd_d = din("w_in_cd", [D, 1920])
    sink_d = din("sink", [128, 4])
    poolw_d = din("pool_w", [128, 4, 128])
    pscale_d = din("pool_scale", [128, 4])
    woutcd_d = din("w_out_cd", [D, D])
    wup_d = din("w_up", [2, D, 2 * DFF])
    cfw_d = din("cfw", [128, 2, 44, 3])
    cfb_d = din("cfb", [128, 2, 44])
    wdown_d = din("w_down", [2, DFF, D])
    fnw_d = din("fnw", [128, 8])
    ident_d = din("ident", [128, 128])
    cos_d = din("cos_t", [128, NL])
    sin_d = din("sin_t", [128, NL])
    mask_d = din("masks", [128, 2, 512])
    icnt_d = din("icnt", [128, 4, 16])
    onesab_d = din("onesab", [128, 2, 128])
    out_d = nc.dram_tensor("out", [BPC, NL, D], F32, kind="ExternalOutput").ap()
    dbg_d = None
    if DEBUG_STOP is not None:
        dbg_d = nc.dram_tensor("dbg", [128, 8, NT], F32, kind="ExternalOutput").ap()

    with contextlib.ExitStack() as es:
        S = Sched(nc, es)

        uniq = [0]

        def sb(name, shape, dt, stack=es):
            uniq[0] += 1
            return stack.enter_context(nc.sbuf_tensor("%s_%d" % (name, uniq[0]), list(shape), dt))

        XR_t = sb("XR", [128, 8, NT], F32)
        H_t = sb("H", [128, 8, NT], BF16)
        XR = Grid("XR", XR_t)
        H = Grid("H", H_t)
        PS_t = es.enter_context(nc.psum_tensor("PS", [128, 8, 512], F32))
        PS = [Buf("ps%d" % i) for i in range(8)]
        IDENT = sb("IDENT", [128, 128], F32)
        ONES = sb("ONES", [128, 128], BF16)
        ONESF = sb("ONESF", [128, 128], F32)
        MODS = sb("MODS", [128, 2, 6, 8, 3], F32)
        CT = sb("CT", [128, 8, 3], F32)
        SC = sb("SC", [128, 8, 3], F32)
        SCB = sb("SCB", [128, 8, 3], BF16)
        BM = sb("BM", [128, 2, 48], F32)
        CONVA = sb("CONVA", [128, 4, 3], F32)
        LS = sb("LS", [128, 4], F32)
        SUBW = sb("SUBW", [128, 1], F32)
        SINKE = sb("SINKE", [128, 4], F32)
        PSCALE = sb("PSCALE", [128, 4], F32)
        CFW = sb("CFW", [128, 2, 44, 3], F32)
        CFB = sb("CFB", [128, 2, 44], F32)
        FNW = sb("FNW", [128, 8], F32)
        CONST = Buf("const")
        NSLOT = 4
        SLOT_t = [sb("WS%d" % i, [128, 3072], BF16) for i in range(NSLOT)]
        SLOT = [Buf("ws%d" % i) for i in range(NSLOT)]
        slot_rr = [0]
        ps_rr = [0]

        def psb(pool=(0, 1, 2, 3)):
            i = pool[ps_rr[0] % len(pool)]
            ps_rr[0] += 1
            return i

        def cload(dst_ap, src_ap, q="sp"):
            S.dma(q, dst_ap, src_ap, CONST, writes=[CONST], nowait=True)

        cload(IDENT[:], ident_d)
        cload(CT[:], cT_d)
        cload(BM[:], bmod_d)
        cload(CONVA[:], conva_d)
        cload(SUBW[:], subln_d)
        cload(SINKE[:], sink_d)
        cload(PSCALE[:], pscale_d)
        cload(CFW[:], cfw_d)
        cload(CFB[:], cfb_d)
        cload(FNW[:], fnw_d)
        S.op("dve", lambda e: e.memset(ONES[:], 1.0), writes=[CONST])
        S.op("dve", lambda e: e.memset(ONESF[:], 1.0), writes=[CONST])

        def load_w(srcs, kc):
            i = slot_rr[0] % NSLOT
            slot_rr[0] += 1
            tot = sum(s.shape[1] for s in srcs)
            assert kc * tot <= 3072
            view = SLOT_t[i][:, 0:kc * tot].rearrange("p (k n) -> p k n", k=kc)
            off = 0
            pairs = []
            for s in srcs:
                n = s.shape[1]
                pairs.append((view[:, :, off:off + n], s.rearrange("(k p) n -> p k n", p=128)))
                off += n
            S.dma("pool", pairs, None, SLOT[i], writes=[SLOT[i]])
            return view, SLOT[i]

        def mm_group(bank, n, pairs, reads, m=128):
            def f(e):
                ins = None
                last = len(pairs) - 1
                for i, (lt, rh) in enumerate(pairs):
                    ins = e.matmul(PS_t[0:m, bank, 0:n], lhsT=lt, rhs=rh, start=(i == 0), stop=(i == last))
                return ins
            S.op("pe", f, reads=reads, writes=[PS[bank]])

        S.op("act", lambda e: e.activation(out=SC[:], in_=CT[:], func=AF.Silu), reads=[CONST], writes=[CONST])
        S.op("act", lambda e: e.activation(out=SCB[:], in_=CT[:], func=AF.Silu), reads=[CONST], writes=[CONST])

        def compute_mods(l):
            with contextlib.ExitStack() as st:
                S.barrier()
                MR_t = sb("MR", [3, 6144], F32, st)
                MR = Buf("MR")
                for blk in range(16):
                    wv, wb = load_w([wmod_d[l][:, blk * 384:(blk + 1) * 384]], 8)
                    bank = psb()
                    mm_group(bank, 384, [(SCB[:, kc, :], wv[:, kc, :]) for kc in range(8)], [wb, CONST], m=3)
                    S.op("act", lambda e, blk=blk, bank=bank: e.activation(out=MR_t[0:3, blk * 384:(blk + 1) * 384], in_=PS_t[0:3, bank, 0:384], func=AF.Copy), reads=[PS[bank]], writes=[MR])
                bank = psb()

                def tp(e, bank=bank):
                    ins = None
                    for j in range(48):
                        ins = e.transpose(out=PS_t[:, bank, j * 3:j * 3 + 3], in_=MR_t[0:3, j * 128:(j + 1) * 128], identity=IDENT[0:3, 0:3])
                    return ins
                S.op("pe", tp, reads=[MR, CONST], writes=[PS[bank]])
                for r in range(3):
                    S.op("dve", lambda e, r=r, l=l, bank=bank: e.tensor_tensor(
                        out=MODS[:, l].rearrange("p m c r -> p (m c) r")[:, :, r], in0=PS_t[:, bank, 0:144].rearrange("p (j r) -> p j r", r=3)[:, :, r], in1=BM[:, l, :], op=ALU.add),
                        reads=[PS[bank], CONST], writes=[CONST])
                for m in (1, 4):
                    S.op("dve", lambda e, l=l, m=m: e.tensor_scalar_add(out=MODS[:, l, m], in0=MODS[:, l, m], scalar1=1.0), reads=[CONST], writes=[CONST])
                S.barrier()
        compute_mods(0)
        with contextlib.ExitStack() as st:
            LQ = sb("LQ", [128, 4, 64], F32, st)
            LTMP = sb("LTMP", [128, 2, 64], F32, st)
            cload(LQ[:], lamqk_d)
            for q_ in range(2):
                S.op("dve", lambda e, q_=q_: e.tensor_tensor(out=LTMP[:, q_, :], in0=LQ[:, 2 * q_, :], in1=LQ[:, 2 * q_ + 1, :], op=ALU.mult), reads=[CONST], writes=[CONST])
            S.op("dve", lambda e: e.reduce_sum(out=LS[:, 0:2], in_=LTMP[:], axis=AX.X), reads=[CONST], writes=[CONST])
            S.op("act", lambda e: e.activation(out=LS[:, 0:2], in_=LS[:, 0:2], func=AF.Exp), reads=[CONST], writes=[CONST])
            S.op("dve", lambda e: e.tensor_tensor(out=LS[:, 2:3], in0=LS[:, 1:2], in1=LS[:, 0:1], op=ALU.subtract), reads=[CONST], writes=[CONST])
            S.op("dve", lambda e: e.tensor_scalar_add(out=LS[:, 2:3], in0=LS[:, 2:3], scalar1=-lam_init(0)), reads=[CONST], writes=[CONST])
            S.barrier()

        ddc = [0]

        def dd(name, src_ap, shape, dt):
            if DEBUG_STOP is None:
                return
            ddc[0] += 1
            dst = nc.dram_tensor(name, list(shape), dt, kind="ExternalOutput").ap()
            S.barrier()
            DB = Buf("dd%d" % ddc[0])
            S.dma("sp", dst, src_ap, DB, nowait=True)
            nc.sync.wait_ge(DB.dsem, DB.dcnt)
            for n_ in ("pe", "act", "dve"):
                S._wait(S.eng[n_], {DB.dsid: DB.dcnt})

        def mod(l, m, c, r):
            return MODS[:, l, m, c, r:r + 1]

        NEGLAM = LS[:, 2:3]
        S.op("dve", lambda e: e.tensor_scalar_mul(out=SUBW[:], in0=SUBW[:], scalar1=1.0 - lam_init(0)), reads=[CONST], writes=[CONST])
        S.op("act", lambda e: e.activation(out=SINKE[:], in_=SINKE[:], func=AF.Exp), reads=[CONST], writes=[CONST])

        def load_rope(st):
            COS = sb("COS", [128, NL], BF16, st)
            SIN = sb("SIN", [128, NL], BF16, st)
            RB = Buf("rope")
            S.sync_to("pool")
            S.dma("pool", [(COS[:], cos_d), (SIN[:], sin_d)], None, RB, writes=[RB])
            return COS, SIN, RB

        TILES_L = [(i * 512, (i + 1) * 512) for i in range(4)]
        TILES_ALL = TILES_L + [(NL, NT)]

        def load_x(b):
          with contextlib.ExitStack() as st:
            S.barrier()
            XS_t = [sb("XS%d" % i, [128, D], F32, st) for i in range(2)]
            XS = [Buf("xs%d" % i) for i in range(2)]
            for j in range(NT // 128):
                k = j % 2
                src = x_d[b, j * 128:(j + 1) * 128, :] if j < 16 else ctx_d[b, (j - 16) * 128:(j - 15) * 128, :]
                S.dma("sp", XS_t[k][:], src, XS[k], writes=[XS[k]])
                for hh in range(2):
                    bank = psb()

                    def tp(e, k=k, hh=hh, bank=bank):
                        ins = None
                        for q in range(4):
                            c = hh * 4 + q
                            ins = e.transpose(out=PS_t[:, bank, q * 128:(q + 1) * 128], in_=XS_t[k][:, c * 128:(c + 1) * 128], identity=IDENT[:])
                        return ins
                    S.op("pe", tp, reads=[XS[k], CONST], writes=[PS[bank]])
                    en = "act" if hh == 0 else "dve"
                    dst = XR_t[:, hh * 4:hh * 4 + 4, j * 128:(j + 1) * 128]
                    srcp = PS_t[:, bank, :].rearrange("p (q n) -> p q n", q=4)
                    wr = XR.RC(range(hh * 4, hh * 4 + 4), j * 128, (j + 1) * 128)
                    if en == "act":
                        S.op("act", lambda e, dst=dst, srcp=srcp: e.activation(out=dst, in_=srcp, func=AF.Copy), reads=[PS[bank]], writes=wr)
                    else:
                        S.op("dve", lambda e, dst=dst, srcp=srcp: e.tensor_copy(out=dst, in_=srcp), reads=[PS[bank]], writes=wr)

        def norm_stage(l, b, m_shift, m_scale, tiles, st):
            SQ_t = sb("SQ", [128, 8, 512], BF16, st)
            SQ = Buf("SQ")
            RS_t = [sb("RS%d" % i, [128, 512], F32, st) for i in range(2)]
            RS = [Buf("RS%d" % i) for i in range(2)]
            TM_t = [sb("TM%d" % i, [128, 512], F32, st) for i in range(2)]
            TM = [Buf("TM%d" % i) for i in range(2)]
            SQB = [Buf("SQ%d" % c) for c in range(8)]

            def sq(ti, a, bb, c):
                n = bb - a
                if c < 4:
                    S.op("act", lambda e: e.activation(out=SQ_t[:, c, 0:n], in_=XR_t[:, c, a:bb], func=AF.Square), reads=XR.R(c, a, bb), writes=[SQB[c]])
                else:
                    S.op("dve", lambda e: e.tensor_tensor(out=SQ_t[:, c, 0:n], in0=XR_t[:, c, a:bb], in1=XR_t[:, c, a:bb], op=ALU.mult), reads=XR.R(c, a, bb), writes=[SQB[c]])

            def stat(ti, a, bb):
                n = bb - a
                bank = psb()
                mm_group(bank, n, [(ONES[:], SQ_t[:, c, 0:n]) for c in range(8)], SQB + [CONST])
                k = ti % 2
                S.op("act", lambda e: e.activation(out=RS_t[k][:, 0:n], in_=PS_t[:, bank, 0:n], func=AF.Ln, bias=EPS, scale=1.0 / D), reads=[PS[bank]], writes=[RS[k]])
                S.op("act", lambda e: e.activation(out=RS_t[k][:, 0:n], in_=RS_t[k][:, 0:n], func=AF.Exp, scale=-0.5), reads=[RS[k]], writes=[RS[k]])

            def modl(ti, a, bb, c):
                n = bb - a
                r = b if a < NL else 2
                k = ti % 2
                kk = c % 2
                S.op("dve", lambda e: e.tensor_tensor(out=TM_t[kk][:, 0:n], in0=XR_t[:, c, a:bb], in1=RS_t[k][:, 0:n], op=ALU.mult), reads=XR.R(c, a, bb) + [RS[k]], writes=[TM[kk]])
                S.op("act", lambda e: e.activation(out=H_t[:, c, a:bb], in_=TM_t[kk][:, 0:n], func=AF.Identity, scale=mod(l, m_scale, c, r), bias=mod(l, m_shift, c, r)), reads=[TM[kk], CONST], writes=H.R(c, a, bb))
            for c in range(8):
                sq(0, *tiles[0], c)
            stat(0, *tiles[0])
            for ti in range(len(tiles)):
                nxt = ti + 1 < len(tiles)
                for c in range(8):
                    if nxt:
                        sq(ti + 1, *tiles[ti + 1], c)
                    modl(ti, *tiles[ti], c)
                if nxt:
                    stat(ti + 1, *tiles[ti + 1])

        def ktile_h(a, bb):
            return [H_t[:, k, a:bb] for k in range(8)], H.RC(range(8), a, bb)

        def ffn_stage(l, b, with_ctx):
            with contextlib.ExitStack() as st:
                S.barrier()
                tiles = TILES_ALL if with_ctx else TILES_L
                with contextlib.ExitStack() as st2:
                    norm_stage(l, b, 3, 4, tiles, st2)
                S.barrier()
                ACTS_t = sb("ACTS", [128, 22, 1024], BF16, st)
                ACTS = [Buf("acts%d" % i) for i in range(22)]
                U_t = [[sb("U%d%d" % (p, i), [128, 1028], BF16, st) for i in range(2)] for p in range(2)]
                U = [[Buf("U%d%d" % (p, i)) for i in range(2)] for p in range(2)]
                TT_t = [[sb("TT%d%d" % (p, i), [128, 1024], BF16, st) for i in range(2)] for p in range(2)]
                TT = [[Buf("TT%d%d" % (p, i)) for i in range(2)] for p in range(2)]
                wins = [(0, 1024, 0, NL), (1024, 2048, 0, NL)] + ([(NL, NT, NL, NT)] if with_ctx else [])
                for (a, bb, s0, s1) in wins:
                    W = bb - a
                    r = b if a < NL else 2
                    ua, ub = max(a - 1, s0), min(bb + 1, s1)
                    ncol = ub - ua
                    nt = (ncol + 511) // 512
                    base = (ncol + nt - 1) // nt
                    ctiles = [(ua + i * base, min(ua + (i + 1) * base, ub)) for i in range(nt)]
                    pend = [None]
                    for p_ in range(2):
                        for k_ in range(2):
                            if ua == a:
                                S.op("dve", lambda e, p_=p_, k_=k_: e.memset(U_t[p_][k_][:, 0:1], 0.0), writes=[U[p_][k_]])
                            if ub == bb:
                                S.op("dve", lambda e, p_=p_, k_=k_, W=W: e.memset(U_t[p_][k_][:, W + 1:W + 2], 0.0), writes=[U[p_][k_]])
                    for i in range(22):
                        wv, wb = load_w([wup_d[l][:, i * 128:(i + 1) * 128], wup_d[l][:, DFF + i * 128:DFF + (i + 1) * 128]], 8)
                        k = i % 2
                        for p in range(2):
                            ch = p * 22 + i
                            Ut, Ub = U_t[p][k], U[p][k]
                            for (ca, cb) in ctiles:
                                n = cb - ca
                                bank = psb()
                                rh, rb = ktile_h(ca, cb)
                                mm_group(bank, n, [(wv[:, kc, p * 128:(p + 1) * 128], rh[kc]) for kc in range(8)], rb + [wb])
                                o = 1 + ca - a
                                S.op("act", lambda e, Ut=Ut, o=o, n=n, bank=bank: e.activation(out=Ut[:, o:o + n], in_=PS_t[:, bank, 0:n], func=AF.Copy), reads=[PS[bank]], writes=[Ub])
                            Tt, Tb = TT_t[p][k], TT[p][k]
                            S.op("dve", lambda e, Tt=Tt, Ut=Ut, W=W, ch=ch: e.tensor_scalar(out=Tt[:, 0:W], in0=Ut[:, 1:W + 1], scalar1=CFW[:, l, ch, 1:2], scalar2=CFB[:, l, ch:ch + 1], op0=ALU.mult, op1=ALU.add), reads=[Ub, CONST], writes=[Tb])
                            S.op("dve", lambda e, Tt=Tt, Ut=Ut, W=W, ch=ch: e.scalar_tensor_tensor(out=Tt[:, 0:W], in0=Ut[:, 0:W], scalar=CFW[:, l, ch, 0:1], in1=Tt[:, 0:W], op0=ALU.mult, op1=ALU.add), reads=[Ub, Tb, CONST], writes=[Tb])
                            S.op("dve", lambda e, Tt=Tt, Ut=Ut, W=W, ch=ch: e.scalar_tensor_tensor(out=Tt[:, 0:W], in0=Ut[:, 2:W + 2], scalar=CFW[:, l, ch, 2:3], in1=Tt[:, 0:W], op0=ALU.mult, op1=ALU.add), reads=[Ub, Tb, CONST], writes=[Tb])
                        def fin(i=i, k=k, W=W):
                            S.op("act", lambda e: e.activation(out=TT_t[1][k][:, 0:W], in_=TT_t[1][k][:, 0:W], func=AF.Silu), reads=[TT[1][k]], writes=[TT[1][k]])
                            S.op("dve", lambda e: e.tensor_tensor(out=ACTS_t[:, i, 0:W], in0=TT_t[1][k][:, 0:W], in1=TT_t[0][k][:, 0:W], op=ALU.mult), reads=[TT[0][k], TT[1][k]], writes=[ACTS[i]])
                        if pend[0] is not None:
                            pend[0]()
                        pend[0] = fin
                    pend[0]()
                    pend[0] = None
                    otiles = [(a + i * 512, min(a + (i + 1) * 512, bb)) for i in range((W + 511) // 512)]
                    groups = [(oc, ca, cb) for oc in range(8) for (ca, cb) in otiles]
                    wts = {}
                    for gi in range(0, len(groups), 4):
                        batch = []
                        for (oc, ca, cb) in groups[gi:gi + 4]:
                            if oc not in wts:
                                wts[oc] = load_w([wdown_d[l][:, oc * 128:(oc + 1) * 128]], 22)
                            wv, wb = wts[oc]
                            n = cb - ca
                            bank = psb()

                            def fa(e, wv=wv, ca=ca, cb=cb, n=n, bank=bank):
                                ins = None
                                for kc in range(21):
                                    ins = e.matmul(PS_t[:, bank, 0:n], lhsT=wv[:, kc, :], rhs=ACTS_t[:, kc, ca - a:cb - a], start=(kc == 0), stop=False)
                                return ins
                            S.op("pe", fa, reads=ACTS[0:21] + [wb], writes=[PS[bank]])
                            batch.append((oc, ca, cb, n, bank, wv, wb))
                        for (oc, ca, cb, n, bank, wv, wb) in batch:
                            S.op("pe", lambda e, wv=wv, ca=ca, cb=cb, n=n, bank=bank: e.matmul(PS_t[:, bank, 0:n], lhsT=wv[:, 21, :], rhs=ACTS_t[:, 21, ca - a:cb - a], start=False, stop=True), reads=[ACTS[21], wb], acc=[PS[bank]])
                            S.op("dve", lambda e, oc=oc, ca=ca, cb=cb, n=n, bank=bank, r=r: e.scalar_tensor_tensor(out=XR_t[:, oc, ca:cb], in0=PS_t[:, bank, 0:n], scalar=mod(l, 5, oc, r), in1=XR_t[:, oc, ca:cb], op0=ALU.mult, op1=ALU.add), reads=[PS[bank], CONST] + XR.R(oc, ca, cb), writes=XR.R(oc, ca, cb))

        def wout_stage(l, b, w_d, kfun, tiles):
            for oc in range(8):
                wv, wb = load_w([w_d[:, oc * 128:(oc + 1) * 128]], 8)
                for (a, bb) in tiles:
                    n = bb - a
                    r = b if a < NL else 2
                    bank = psb()
                    rh, rb = kfun(a, bb)
                    mm_group(bank, n, [(wv[:, kc, :], rh[kc]) for kc in range(8)], rb + [wb])
                    S.op("dve", lambda e, oc=oc, a=a, bb=bb, n=n, bank=bank, r=r: e.scalar_tensor_tensor(out=XR_t[:, oc, a:bb], in0=PS_t[:, bank, 0:n], scalar=mod(l, 2, oc, r), in1=XR_t[:, oc, a:bb], op0=ALU.mult, op1=ALU.add), reads=[PS[bank], CONST] + XR.R(oc, a, bb), writes=XR.R(oc, a, bb))

        def mixer_ab(l, b):
            with contextlib.ExitStack() as st:
                S.barrier()
                with contextlib.ExitStack() as st2:
                    norm_stage(l, b, 0, 1, TILES_ALL, st2)
                S.barrier()
                COS, SIN, RB = load_rope(st)
                YAC_t = sb("YAC", [128, 4, 2312], BF16, st)
                YAC = [Buf("yac%d" % c) for c in range(4)]
                with contextlib.ExitStack() as st2:
                    PP2_t = [sb("PP%d" % i, [128, 2312], BF16, st2) for i in range(2)]
                    PP2 = [Buf("PP%d" % i) for i in range(2)]
                    GBB2_t = [sb("GBB%d" % i, [128, 2312], BF16, st2) for i in range(2)]
                    GBB2 = [Buf("GBB%d" % i) for i in range(2)]
                    T_t = sb("T", [128, 2306], F32, st2)
                    T = Buf("T")
                    GCS_t = [sb("GCS%d" % i, [128, 512], BF16, st2) for i in range(2)]
                    GCS = [Buf("GCS%d" % i) for i in range(2)]
                    for i_ in range(2):
                        S.op("dve", lambda e, i_=i_: e.memset(PP2_t[i_][:], 0.0), writes=[PP2[i_]])
                        S.op("dve", lambda e, i_=i_: e.memset(GBB2_t[i_][:], 0.0), writes=[GBB2[i_]])
                    for c in range(4):
                        PP_t, PP, GBB_t, GBB = PP2_t[c % 2], PP2[c % 2], GBB2_t[c % 2], GBB2[c % 2]
                        wv, wb = load_w([winab_d[:, g * 512 + c * 128:g * 512 + (c + 1) * 128] for g in range(3)], 8)
                        for ti, (a, bb) in enumerate(TILES_ALL):
                            n = bb - a
                            rh, rb = ktile_h(a, bb)
                            banks = [psb() for _ in range(3)]
                            for g in range(3):
                                mm_group(banks[g], n, [(wv[:, kc, g * 128:(g + 1) * 128], rh[kc]) for kc in range(8)], rb + [wb])
                            k = ti % 2
                            o = pc(a)
                            S.op("act", lambda e, k=k, n=n, bk=banks[1]: e.activation(out=GCS_t[k][:, 0:n], in_=PS_t[:, bk, 0:n], func=AF.Copy), reads=[PS[banks[1]]], writes=[GCS[k]])
                            S.op("dve", lambda e, k=k, n=n, o=o, bk=banks[2]: e.tensor_tensor(out=PP_t[:, o:o + n], in0=PS_t[:, bk, 0:n], in1=GCS_t[k][:, 0:n], op=ALU.mult), reads=[PS[banks[2]], GCS[k]], writes=[PP])
                            S.op("act", lambda e, n=n, o=o, bk=banks[0]: e.activation(out=GBB_t[:, o:o + n], in_=PS_t[:, bk, 0:n], func=AF.Copy), reads=[PS[banks[0]]], writes=[GBB])
                        S.op("dve", lambda e, c=c: e.tensor_scalar(out=T_t[:], in0=PP_t[:, 1:2307], scalar1=CONVA[:, c, 1:2], scalar2=None, op0=ALU.mult), reads=[PP, CONST], writes=[T])
                        S.op("dve", lambda e, c=c: e.scalar_tensor_tensor(out=T_t[:], in0=PP_t[:, 0:2306], scalar=CONVA[:, c, 0:1], in1=T_t[:], op0=ALU.mult, op1=ALU.add), reads=[PP, T, CONST], writes=[T])
                        S.op("dve", lambda e, c=c: e.scalar_tensor_tensor(out=T_t[:], in0=PP_t[:, 2:2308], scalar=CONVA[:, c, 2:3], in1=T_t[:], op0=ALU.mult, op1=ALU.add), reads=[PP, T, CONST], writes=[T])
                        S.op("dve", lambda e, c=c: e.tensor_tensor(out=YAC_t[:, c, 1:2307], in0=T_t[:], in1=GBB_t[:, 1:2307], op=ALU.mult), reads=[T, GBB], writes=[YAC[c]])
                    S.barrier()
                YB_t = sb("YB", [128, 4, NT], BF16, st)
                YB = Grid("YB", YB_t)
                QC_t = sb("QC", [128, NT], BF16, st)
                KC_t = sb("KC", [128, NT], BF16, st)
                VC_t = sb("VC", [128, 18, 128], BF16, st)
                QC, KC, VC = Grid("QC", QC_t), Grid("KC", KC_t), Buf("VC")
                PT_t = [sb("PT%d" % i, [128, 2, 512], BF16, st) for i in range(3)]
                PT = [Buf("PT%d" % i) for i in range(3)]
                RT_t = [sb("RT0", [128, 512], BF16, st), None]
                RT = [Buf("RT0"), None]
                ZA_t = [sb("ZA%d" % i, [128, 512], BF16, st) for i in range(2)]
                ZA = [Buf("ZA%d" % i) for i in range(2)]
                RT_t[1] = sb("RT1", [128, 512], BF16, st)
                RT[1] = Buf("RT1")
                QZ_t = [sb("QZ%d" % i, [128, 512], BF16, st) for i in range(2)]
                QZ = [Buf("QZ%d" % i) for i in range(2)]
                for m_ in range(2):
                    S.op("dve", lambda e, m_=m_: e.memset(QZ_t[m_][:], 0.0), writes=[QZ[m_]])
                YR_t, YR = RT_t[1], RT[1]
                SQ1_t = sb("SQ1", [128, 512], BF16, st)
                SQ1 = Buf("SQ1")
                pend_tail = [None]
                sp_rr = [0]
                for c in range(4):
                    wvq, wbq = load_w([winab_d[:, 1536 + c * 128:1536 + (c + 1) * 128], winab_d[:, 3072 + c * 128:3072 + (c + 1) * 128]], 8)
                    wvk, wbk = load_w([winab_d[:, 2048 + c * 128:2048 + (c + 1) * 128], winab_d[:, 3584 + c * 128:3584 + (c + 1) * 128]], 8)
                    wv2, wb2 = load_w([winab_d[:, 2560 + c * 128:2560 + (c + 1) * 128]], 8)
                    for (a, bb) in TILES_ALL:
                        n = bb - a
                        rh, rb = ktile_h(a, bb)
                        for qi, (dst_t, dst_g, wv, wb) in enumerate(((QC_t, QC, wvq, wbq), (KC_t, KC, wvk, wbk))):
                            b0 = psb()
                            mm_group(b0, n, [(wv[:, kc, 0:128], rh[kc]) for kc in range(8)], rb + [wb])
                            if a < NL:
                                b1 = psb()
                                mm_group(b1, n, [(wv[:, kc, 128:256], rh[kc]) for kc in range(8)], rb + [wb])
                                S.op("dve", lambda e, a=a, bb=bb, b0=b0: e.tensor_tensor(out=RT_t[0][:, 0:512], in0=PS_t[:, b0, 0:512], in1=COS[:, a:bb], op=ALU.mult), reads=[PS[b0], RB], writes=[RT[0]])
                                S.op("dve", lambda e, a=a, bb=bb, b1=b1: e.tensor_tensor(out=RT_t[1][:, 0:512], in0=PS_t[:, b1, 0:512], in1=SIN[:, a:bb], op=ALU.mult), reads=[PS[b1], RB], writes=[RT[1]])
                                S.op("dve", lambda e, a=a, bb=bb, dst_t=dst_t: e.tensor_tensor(out=dst_t[:, a:bb], in0=RT_t[0][:, 0:512], in1=RT_t[1][:, 0:512], op=ALU.add), reads=[RT[0], RT[1]], writes=dst_g.R(0, a, bb))
                            else:
                                S.op("act", lambda e, a=a, bb=bb, n=n, b0=b0, dst_t=dst_t: e.activation(out=dst_t[:, a:bb], in_=PS_t[:, b0, 0:n], func=AF.Copy), reads=[PS[b0]], writes=dst_g.R(0, a, bb))
                    for kcnk in range(18):
                        bank = psb()
                        a = kcnk * 128
                        mm_group(bank, 128, [(H_t[:, kc, a:a + 128], wv2[:, kc, :]) for kc in range(8)], H.RC(range(8), a, a + 128) + [wb2])
                        S.op("act", lambda e, kcnk=kcnk, bank=bank: e.activation(out=VC_t[:, kcnk, :], in_=PS_t[:, bank, 0:128], func=AF.Copy), reads=[PS[bank]], writes=[VC])
                    def qz0(a, bb):
                        S.op("act", lambda e: e.activation(out=QZ_t[0][0:64, 0:bb - a], in_=QC_t[0:64, a:bb], func=AF.Copy), reads=QC.R(0, a, bb), writes=[QZ[0]])
                    for tix, (a, bb) in enumerate(TILES_ALL):
                        n = bb - a
                        kchunks = list(range(18)) if a < NL else [16, 17]
                        kpairs = [(kchunks[i], kchunks[i + 1]) for i in range(0, len(kchunks), 2)]
                        its = [(m, pr) for m in range(2) for pr in kpairs]
                        sb_of = {}
                        if tix == 0:
                            qz0(a, bb)
                        def qz1(a=a, bb=bb, n=n):
                            S.op("dve", lambda e: e.tensor_copy(out=QZ_t[1][64:128, 0:n], in_=QC_t[64:128, a:bb]), reads=QC.R(0, a, bb), writes=[QZ[1]])

                        def comb(m, n=n):
                            S.op("act", lambda e: e.activation(out=RT_t[m][:, 0:n], in_=PS_t[:, 6 + m, 0:n], func=AF.Ln), reads=[PS[6 + m]], writes=[RT[m]])
                            S.op("act", lambda e: e.activation(out=RT_t[m][:, 0:n], in_=RT_t[m][:, 0:n], func=AF.Exp, scale=-1.0), reads=[RT[m]], writes=[RT[m]])
                            S.op("dve", lambda e: e.tensor_tensor(out=RT_t[m][:, 0:n], in0=PS_t[:, 4 + m, 0:n], in1=RT_t[m][:, 0:n], op=ALU.mult), reads=[PS[4 + m], RT[m]], writes=[RT[m]])
                        long_tile = len(its) > 12
                        if not long_tile:
                            qz1()
                        comb0_done = False

                        def emit_s(j):
                            m, pr = its[j]
                            b0 = 2 * (sp_rr[0] % 2)
                            sp_rr[0] += 1
                            sb_of[j] = b0
                            for q_, kk in enumerate(pr):
                                mm_group(b0 + q_, n, [(KC_t[:, kk * 128:(kk + 1) * 128], QZ_t[m][:, 0:n])], KC.R(0, kk * 128, (kk + 1) * 128) + [QZ[m]])
                        emit_s(0)
                        for j, (m, pr) in enumerate(its):
                            if j + 1 < len(its):
                                emit_s(j + 1)
                            if j == len(kpairs) and tix + 1 < len(TILES_ALL):
                                qz0(*TILES_ALL[tix + 1])
                            if long_tile and j == 3:
                                qz1()
                            if long_tile and j == len(kpairs) + 2:
                                comb(0)
                                comb0_done = True
                            if j == 6 and pend_tail[0] is not None:
                                pend_tail[0](7)
                                pend_tail[0] = None
                            b0 = sb_of[j]
                            pk = j % 3
                            S.op("act", lambda e, pk=pk, b0=b0, n=n: e.activation(out=PT_t[pk][:, :, 0:n], in_=PS_t[:, b0:b0 + 2, 0:n], func=AF.Exp, scale=0.125), reads=[PS[b0], PS[b0 + 1]], writes=[PT[pk]])
                            first, last = (pr == kpairs[0]), (pr == kpairs[-1])

                            def pv(e, m=m, pr=pr, pk=pk, n=n, first=first, last=last):
                                e.matmul(PS_t[:, 4 + m, 0:n], lhsT=VC_t[:, pr[0], :], rhs=PT_t[pk][:, 0, 0:n], start=first, stop=False)
                                return e.matmul(PS_t[:, 4 + m, 0:n], lhsT=VC_t[:, pr[1], :], rhs=PT_t[pk][:, 1, 0:n], start=False, stop=last)
                            if first:
                                S.op("pe", pv, reads=[PT[pk], VC], writes=[PS[4 + m]])
                                S.op("dve", lambda e, m=m, pk=pk, n=n: e.tensor_tensor(out=ZA_t[m][:, 0:n], in0=PT_t[pk][:, 0, 0:n], in1=PT_t[pk][:, 1, 0:n], op=ALU.add), reads=[PT[pk]], writes=[ZA[m]])
                            else:
                                S.op("pe", pv, reads=[PT[pk], VC], acc=[PS[4 + m]])
                                for q_ in range(2):
                                    S.op("dve", lambda e, m=m, pk=pk, n=n, q_=q_: e.tensor_tensor(out=ZA_t[m][:, 0:n], in0=ZA_t[m][:, 0:n], in1=PT_t[pk][:, q_, 0:n], op=ALU.add), reads=[PT[pk], ZA[m]], writes=[ZA[m]])
                            if last:
                                mm_group(6 + m, n, [(ONES[:], ZA_t[m][:, 0:n])], [ZA[m], CONST])
                        if pend_tail[0] is not None:
                            pend_tail[0](psb())
                            pend_tail[0] = None
                        if not comb0_done:
                            comb(0)
                        comb(1)

                        def tail(bk, n=n, a=a, bb=bb, c=c):
                            S.op("dve", lambda e: e.scalar_tensor_tensor(out=YR_t[:, 0:n], in0=RT_t[1][:, 0:n], scalar=NEGLAM, in1=RT_t[0][:, 0:n], op0=ALU.mult, op1=ALU.add), reads=[RT[0], RT[1], CONST], writes=[YR])
                            S.op("dve", lambda e: e.tensor_tensor(out=SQ1_t[:, 0:n], in0=YR_t[:, 0:n], in1=YR_t[:, 0:n], op=ALU.mult), reads=[YR], writes=[SQ1])
                            mm_group(bk, n, [(ONES[:], SQ1_t[:, 0:n])], [SQ1, CONST])
                            S.op("act", lambda e: e.activation(out=RT_t[0][:, 0:n], in_=PS_t[:, bk, 0:n], func=AF.Ln, bias=EPS, scale=1.0 / 128), reads=[PS[bk]], writes=[RT[0]])
                            S.op("act", lambda e: e.activation(out=RT_t[0][:, 0:n], in_=RT_t[0][:, 0:n], func=AF.Exp, scale=-0.5), reads=[RT[0]], writes=[RT[0]])
                            S.op("dve", lambda e: e.scalar_tensor_tensor(out=YB_t[:, c, a:bb], in0=YR_t[:, 0:n], scalar=SUBW[:, 0:1], in1=RT_t[0][:, 0:n], op0=ALU.mult, op1=ALU.mult), reads=[YR, RT[0], CONST], writes=YB.R(c, a, bb))
                        pend_tail[0] = tail
                    pend_tail[0](psb())
                    pend_tail[0] = None
                    if c == 0 and b == 0:
                        dd("d_qc", QC_t[:], [128, NT], BF16)
                        dd("d_kc", KC_t[:], [128, NT], BF16)
                        dd("d_cos", COS[:], [128, NL], BF16)
                        dd("d_sin", SIN[:], [128, NL], BF16)

                def kfun(a, bb):
                    o = pc(a)
                    return ([YAC_t[:, c, o:o + (bb - a)] for c in range(4)] + [YB_t[:, c, a:bb] for c in range(4)],
                            YAC + YB.RC(range(4), a, bb))
                wout_stage(l, b, woutab_d, kfun, TILES_ALL)

        def mixer_cd(l, b):
            with contextlib.ExitStack() as st:
                S.barrier()
                with contextlib.ExitStack() as st2:
                    norm_stage(l, b, 0, 1, TILES_ALL, st2)
                S.barrier()
                COS, SIN, RB = load_rope(st)
                MASKS = sb("MASKS", [128, 2, 512], BF16, st)
                ONESAB = sb("ONESAB", [128, 2, 128], BF16, st)
                POOLW = sb("POOLW", [128, 4, 128], BF16, st)
                ICNT = sb("ICNT", [128, 4, 16], F32, st)
                CB = Buf("cdconst")
                S.sync_to("pool")
                S.dma("pool", [(MASKS[:], mask_d), (ONESAB[:], onesab_d), (POOLW[:], poolw_d)], None, CB, writes=[CB])
                CB2 = Buf("cdconst2")
                S.dma("sp", ICNT[:], icnt_d, CB2, writes=[CB2], nowait=True)
                YD_t = sb("YD", [128, 4, NL], BF16, st)
                YD = Grid("YD", YD_t)
                with contextlib.ExitStack() as st2:
                    PX_t = sb("PX", [128, 2080], F32, st2)
                    PA_t = sb("PA", [128, 2080], F32, st2)
                    PB_t = sb("PB", [128, 2080], F32, st2)
                    PX, PA, PB = Buf("PX"), Buf("PA"), Buf("PB")
                    ZG_t = sb("ZG", [128, NL], BF16, st2)
                    ZG = Buf("ZG")
                    ET_t = sb("ET", [128, 16], F32, st2)
                    ET = Buf("ET")
                    S.op("dve", lambda e: e.memset(PX_t[:], 0.0), writes=[PX])
                    S.op("dve", lambda e: e.memset(PA_t[:], 0.0), writes=[PA])
                    S.op("dve", lambda e: e.memset(PB_t[:], 0.0), writes=[PB])
                    for g in range(4):
                        wv, wb = load_w([wincd_d[:, 1408 + g * 128:1408 + (g + 1) * 128]], 8)
                        for (a, bb) in TILES_L:
                            rh, rb = ktile_h(a, bb)
                            bank = psb()
                            mm_group(bank, 512, [(wv[:, kc, :], rh[kc]) for kc in range(8)], rb + [wb])
                            S.op("act", lambda e, a=a, bank=bank: e.activation(out=PX_t[:, 16 + a:16 + a + 512], in_=PS_t[:, bank, 0:512], func=AF.Copy), reads=[PS[bank]], writes=[PX])
                        w = (2, 4, 8, 16)[g]
                        cur_t, cur = PX_t, PX
                        sh = 1
                        dsts = [(PA_t, PA), (PB_t, PB)]
                        di_ = 0
                        while sh < w:
                            dt_, db_ = dsts[di_ % 2]
                            di_ += 1
                            S.op("dve", lambda e, dt_=dt_, cur_t=cur_t, sh=sh: e.tensor_tensor(out=dt_[:, 15:2080], in0=cur_t[:, 15:2080], in1=cur_t[:, 15 - sh:2080 - sh], op=ALU.add), reads=[cur], writes=[db_])
                            cur_t, cur = dt_, db_
                            sh *= 2
                        o = 16 + w // 2 - 1
                        S.op("dve", lambda e, cur_t=cur_t, o=o, w=w: e.scalar_tensor_tensor(out=ZG_t[:], in0=cur_t[:, o:o + NL], scalar=1.0 / w, in1=PX_t[:, 16:16 + NL], op0=ALU.mult, op1=ALU.subtract), reads=[cur, PX], writes=[ZG])
                        for side in range(2):
                            t0 = 0 if side == 0 else NL - 8
                            S.op("dve", lambda e, cur_t=cur_t, o=o, t0=t0, g=g, side=side: e.tensor_tensor(out=ET_t[:, 0:8], in0=cur_t[:, o + t0:o + t0 + 8], in1=ICNT[:, g, side * 8:side * 8 + 8], op=ALU.mult), reads=[cur, CONST, CB, CB2, RB], writes=[ET])
                            S.op("dve", lambda e, t0=t0: e.tensor_tensor(out=ZG_t[:, t0:t0 + 8], in0=ET_t[:, 0:8], in1=PX_t[:, 16 + t0:16 + t0 + 8], op=ALU.subtract), reads=[ET, PX, ZG], writes=[ZG])
                        for (a, bb) in TILES_L:
                            bank = psb()
                            mm_group(bank, 512, [(POOLW[:, g, :], ZG_t[:, a:bb])], [ZG, CONST, CB, RB])
                            S.op("act", lambda e, a=a, bb=bb, g=g, bank=bank: e.activation(out=YD_t[:, g, a:bb], in_=PS_t[:, bank, 0:512], func=AF.Identity, scale=PSCALE[:, g:g + 1]), reads=[PS[bank], CONST, CB, RB], writes=YD.R(g, a, bb))
                    S.barrier()
                QT_t = sb("QT", [128, 4, NL], BF16, st)
                QT = Grid("QT", QT_t)
                KT_t = sb("KT", [128, 2, NT], BF16, st)
                KT = Grid("KT", KT_t)
                S.op("dve", lambda e: e.memset(KT_t[:], 0.0), writes=KT.RC(range(2), 0, NT))
                VAB_t = sb("VAB", [128, 2, 18, 128], BF16, st)
                VAB = Buf("VAB")
                YA1_t, YA1 = QT_t, QT
                RT_t = [sb("RTc%d" % i, [128, 512], F32, st) for i in range(2)]
                RT = [Buf("RTc%d" % i) for i in range(2)]
                PT_t = [sb("PTc%d" % i, [128, 512], BF16, st) for i in range(4)]
                PT = [Buf("PTc%d" % i) for i in range(4)]
                S.op("dve", lambda e: e.memset(VAB_t[:], 0.0), writes=[VAB])

                def rope_proj(wv, wb, ci, dst_fn, dst_bufs_fn, tiles):
                    for (a, bb) in tiles:
                        n = bb - a
                        rh, rb = ktile_h(a, bb)
                        b0 = psb()
                        mm_group(b0, n, [(wv[:, kc, ci * 128:(ci + 1) * 128], rh[kc]) for kc in range(8)], rb + [wb])
                        if a < NL:
                            b1 = psb()
                            mm_group(b1, n, [(wv[:, kc, (ci + 1) * 128:(ci + 2) * 128], rh[kc]) for kc in range(8)], rb + [wb])
                            S.op("dve", lambda e, a=a, bb=bb, b0=b0: e.tensor_tensor(out=RT_t[0][:, 0:512], in0=PS_t[:, b0, 0:512], in1=COS[:, a:bb], op=ALU.mult), reads=[PS[b0], CONST, CB, RB], writes=[RT[0]])
                            S.op("dve", lambda e, a=a, bb=bb, b1=b1: e.tensor_tensor(out=RT_t[1][:, 0:512], in0=PS_t[:, b1, 0:512], in1=SIN[:, a:bb], op=ALU.mult), reads=[PS[b1], CONST, CB, RB], writes=[RT[1]])
                            if dst_fn is not None:
                                S.op("dve", lambda e, a=a, bb=bb: e.tensor_tensor(out=dst_fn(a, bb), in0=RT_t[0][:, 0:512], in1=RT_t[1][:, 0:512], op=ALU.add), reads=[RT[0], RT[1]], writes=dst_bufs_fn(a, bb))
                            else:
                                for kv_ in range(2):
                                    S.op("dve", lambda e, a=a, bb=bb, kv_=kv_: e.tensor_tensor(out=KT_t[64 * kv_:64 * kv_ + 64, kv_, a:bb], in0=RT_t[0][64 * kv_:64 * kv_ + 64, 0:512], in1=RT_t[1][64 * kv_:64 * kv_ + 64, 0:512], op=ALU.add), reads=[RT[0], RT[1]], writes=dst_bufs_fn(a, bb))
                        else:
                            if dst_fn is not None:
                                S.op("act", lambda e, a=a, bb=bb, n=n, b0=b0: e.activation(out=dst_fn(a, bb), in_=PS_t[:, b0, 0:n], func=AF.Copy), reads=[PS[b0]], writes=dst_bufs_fn(a, bb))
                            else:
                                for kv_ in range(2):
                                    S.op("act", lambda e, a=a, bb=bb, n=n, b0=b0, kv_=kv_: e.activation(out=KT_t[64 * kv_:64 * kv_ + 64, kv_, a:bb], in_=PS_t[64 * kv_:64 * kv_ + 64, b0, 0:n], func=AF.Copy), reads=[PS[b0]], writes=dst_bufs_fn(a, bb))
                for g in range(4):
                    wv, wb = load_w([wincd_d[:, g * 128:(g + 1) * 128], wincd_d[:, 512 + g * 128:512 + (g + 1) * 128]], 8)
                    rope_proj(wv, wb, 0, lambda a, bb, g=g: QT_t[:, g, a:bb], lambda a, bb, g=g: QT.R(g, a, bb), TILES_L)
                wv, wb = load_w([wincd_d[:, 1024:1152], wincd_d[:, 1152:1280], wincd_d[:, 1280:1408]], 8)
                rope_proj(wv, wb, 0, None, lambda a, bb: KT.RC(range(2), a, bb), TILES_ALL)
                for kcnk in range(18):
                    bank = psb()
                    a = kcnk * 128
                    mm_group(bank, 128, [(H_t[:, kc, a:a + 128], wv[:, kc, 256:384]) for kc in range(8)], H.RC(range(8), a, a + 128) + [wb])
                    S.op("act", lambda e, kcnk=kcnk, bank=bank: e.activation(out=VAB_t[:, 0, kcnk, 0:64], in_=PS_t[:, bank, 0:64], func=AF.Copy), reads=[PS[bank]], writes=[VAB])
                    S.op("dve", lambda e, kcnk=kcnk, bank=bank: e.tensor_copy(out=VAB_t[:, 1, kcnk, 64:128], in_=PS_t[:, bank, 64:128]), reads=[PS[bank]], writes=[VAB])
                for i in range(16):
                    qa, qb = i * 128, (i + 1) * 128
                    kl = [(16, None), (17, None)]
                    if i > 0:
                        kl.append((i - 1, 0))
                    kl.append((i, None))
                    if i < 15:
                        kl.append((i + 1, 1))
                    its = [(kv, kk, mk) for kv in range(2) for (kk, mk) in kl]
                    sbanks = {}

                    def emit_s(j):
                        kv, kk, mk = its[j]
                        bk = psb()
                        sbanks[j] = bk
                        mm_group(bk, 512, [(KT_t[:, kv, kk * 128:(kk + 1) * 128], QT_t[:, :, qa:qb])], KT.R(kv, kk * 128, (kk + 1) * 128) + QT.RC(range(4), qa, qb))
                    emit_s(0)
                    emit_s(1)
                    for j, (kv, kk, mk) in enumerate(its):
                        if j + 2 < len(its):
                            emit_s(j + 2)
                        bk = sbanks[j]
                        pk = j % 4
                        S.op("act", lambda e, pk=pk, bk=bk: e.activation(out=PT_t[pk][:], in_=PS_t[:, bk, :], func=AF.Exp, scale=0.125), reads=[PS[bk]], writes=[PT[pk]])
                        if mk is not None:
                            S.op("dve", lambda e, pk=pk, mk=mk: e.tensor_tensor(out=PT_t[pk][:], in0=PT_t[pk][:], in1=MASKS[:, mk, :], op=ALU.mult), reads=[PT[pk], CONST, CB, RB], writes=[PT[pk]])
                        first, last = (j == 0), (j == len(its) - 1)

                        ob = 4 + 2 * (i % 2)

                        def pv(e, kv=kv, kk=kk, pk=pk, first=first, last=last, ob=ob):
                            e.matmul(PS_t[:, ob, :], lhsT=VAB_t[:, kv, kk, :], rhs=PT_t[pk][:], start=first, stop=last)
                            return e.matmul(PS_t[:, ob + 1, :], lhsT=ONESAB[:, kv, :], rhs=PT_t[pk][:], start=first, stop=last)
                        if first:
                            S.op("pe", pv, reads=[PT[pk], VAB, CONST, CB, RB], writes=[PS[ob], PS[ob + 1]])
                        else:
                            S.op("pe", pv, reads=[PT[pk], VAB, CONST, CB, RB], acc=[PS[ob], PS[ob + 1]])
                    ob = 4 + 2 * (i % 2)
                    rk = i % 2
                    for g in range(4):
                        S.op("act", lambda e, g=g, ob=ob, rk=rk: e.activation(out=RT_t[rk][:, g * 128:(g + 1) * 128], in_=PS_t[:, ob + 1, g * 128:(g + 1) * 128], func=AF.Ln, bias=SINKE[:, g:g + 1]), reads=[PS[ob + 1], CONST, CB, RB], writes=[RT[rk]])
                    S.op("act", lambda e, rk=rk: e.activation(out=RT_t[rk][:], in_=RT_t[rk][:], func=AF.Exp, scale=-1.0), reads=[RT[rk]], writes=[RT[rk]])
                    S.op("dve", lambda e, qa=qa, qb=qb, ob=ob, rk=rk: e.tensor_tensor(out=YA1_t[:, :, qa:qb], in0=PS_t[:, ob, :].rearrange("p (g n) -> p g n", g=4), in1=RT_t[rk][:].rearrange("p (g n) -> p g n", g=4), op=ALU.mult), reads=[PS[ob], RT[rk]], writes=YA1.RC(range(4), qa, qb))

                def kfun(a, bb):
                    return ([YA1_t[:, g, a:bb] for g in range(4)] + [YD_t[:, g, a:bb] for g in range(4)],
                            YA1.RC(range(4), a, bb) + YD.RC(range(4), a, bb))
                wout_stage(l, b, woutcd_d, kfun, TILES_L)

        def final_stage(b):
            with contextlib.ExitStack() as st:
                S.barrier()
                SQ_t = sb("SQf", [128, 8, 512], BF16, st)
                SQ = Buf("SQf")
                RS_t = sb("RSf", [128, 512], F32, st)
                RS = Buf("RSf")
                YF_t = sb("YF", [128, 8, 512], F32, st)
                YF = [Buf("YF%d" % c) for c in range(8)]
                OS_t = [sb("OS%d" % i, [128, D], F32, st) for i in range(2)]
                OS = [Buf("OS%d" % i) for i in range(2)]
                for (a, bb) in TILES_L:
                    for c in range(8):
                        S.op("act", lambda e, c=c, a=a, bb=bb: e.activation(out=SQ_t[:, c, :], in_=XR_t[:, c, a:bb], func=AF.Square), reads=XR.R(c, a, bb), writes=[SQ])
                    bank = psb()
                    mm_group(bank, 512, [(ONES[:], SQ_t[:, c, :]) for c in range(8)], [SQ, CONST])
                    S.op("act", lambda e, bank=bank: e.activation(out=RS_t[:], in_=PS_t[:, bank, :], func=AF.Sqrt, bias=EPS, scale=1.0 / D), reads=[PS[bank]], writes=[RS])
                    S.op("dve", lambda e: e.reciprocal(out=RS_t[:], in_=RS_t[:]), reads=[RS], writes=[RS])
                    for c in range(8):
                        S.op("dve", lambda e, c=c, a=a, bb=bb: e.scalar_tensor_tensor(out=YF_t[:, c, :], in0=XR_t[:, c, a:bb], scalar=FNW[:, c:c + 1], in1=RS_t[:], op0=ALU.mult, op1=ALU.mult), reads=XR.R(c, a, bb) + [RS, CONST], writes=[YF[c]])
                    for s in range(4):
                        tok = a + s * 128
                        k = (tok // 128) % 2
                        for hh in range(2):
                            bank = psb()

                            def tp(e, hh=hh, bank=bank, s=s):
                                ins = None
                                for q in range(4):
                                    c = hh * 4 + q
                                    ins = e.transpose(out=PS_t[:, bank, q * 128:(q + 1) * 128], in_=YF_t[:, c, s * 128:(s + 1) * 128], identity=IDENT[:])
                                return ins
                            S.op("pe", tp, reads=[YF[hh * 4 + q] for q in range(4)] + [CONST], writes=[PS[bank]])
                            if hh == 0:
                                S.op("act", lambda e, k=k, bank=bank: e.activation(out=OS_t[k][:, 0:512], in_=PS_t[:, bank, :], func=AF.Copy), reads=[PS[bank]], writes=[OS[k]])
                            else:
                                S.op("dve", lambda e, k=k, bank=bank: e.tensor_copy(out=OS_t[k][:, 512:1024], in_=PS_t[:, bank, :]), reads=[PS[bank]], writes=[OS[k]])
                        S.dma("sp", out_d[b, tok:tok + 128, :], OS_t[k][:], OS[k], reads=[OS[k]])
                S.barrier()
                for k in range(2):
                    if OS[k].dsem is not None:
                        nc.sync.wait_ge(OS[k].dsem, OS[k].dcnt)

        def dump_dbg():
            S.barrier()
            DB = Buf("dbg")
            S.dma("sp", dbg_d, XR_t[:], DB, reads=[bb_ for v in XR.b.values() for bb_ in [v]])
            nc.sync.wait_ge(DB.dsem, DB.dcnt)

        stages = [("mixer0", lambda b: mixer_ab(0, b)), ("ffn0", lambda b: ffn_stage(0, b, True)),
                  ("mixer1", lambda b: mixer_cd(1, b)), ("ffn1", lambda b: ffn_stage(1, b, False))]
        for b in range(BPC):
            load_x(b)
            stop = False
            for name, fn in stages:
                fn(b)
                if b == 0 and name == "mixer0":
                    compute_mods(1)
                if DEBUG_STOP == name:
                    stop = True
                    break
            if stop:
                dump_dbg()
                break
            final_stage(b)
    return nc


def _fm(v):
    v = np.asarray(v, np.float32)
    n = v.shape[-1] // 128
    return np.ascontiguousarray(np.moveaxis(v.reshape(v.shape[:-1] + (n, 128)), -1, 0))


def _rope_tables():
    pos = np.arange(NL)
    row = (pos // 64).astype(np.float32)
    col = (pos % 64).astype(np.float32)
    inv = (1.0 / (np.float32(10000.0) ** (np.arange(0, 32, 2, dtype=np.float32) / np.float32(32)))).astype(np.float32)
    cos_t = np.zeros((128, NL), np.float32)
    sin_t = np.zeros((128, NL), np.float32)
    for p in range(128):
        d = p % 64
        ax = row if d < 32 else col
        dd = d % 32
        ang = (ax * inv[dd % 16]).astype(np.float32)
        cos_t[p] = np.cos(ang)
        sin_t[p] = -np.sin(ang) if dd < 16 else np.sin(ang)
    return cos_t, sin_t


def _partner_perm(ncols):
    idx = np.arange(ncols)
    d = idx % 64
    dd = d % 32
    return np.where(dd < 16, idx + 16, idx - 16)


_PROG = {}


def _prepare(inputs):
    f = lambda k: np.asarray(inputs[k], np.float32)
    x, c, ctx, c_ctx = f("x"), f("c"), f("ctx"), f("c_ctx")
    shared = {}
    shared["w_mod"] = f("w_mod")
    shared["b_mod"] = np.ascontiguousarray(_fm(f("b_mod")))
    w = f("w_in_ab")[0]
    q, k = w[:, 1536:2048], w[:, 2048:2560]
    pp = _partner_perm(512)
    shared["w_in_ab"] = np.ascontiguousarray(np.concatenate([w, q[:, pp], k[:, pp]], axis=1))
    shared["conv_a"] = np.ascontiguousarray(np.transpose(f("conv_a")[0].reshape(3, 4, 128), (2, 1, 0)))
    shared["lam_qk"] = np.ascontiguousarray(np.broadcast_to(f("lam_qk")[0][None], (128, 4, 64)))
    shared["subln"] = np.ascontiguousarray(f("subln_b")[0].reshape(128, 1))
    shared["w_out_ab"] = f("w_out_ab")[0]
    w = f("w_in_cd")[0]
    qh = w[:, 0:512].reshape(D, 8, 64)
    qg = np.concatenate([np.concatenate([qh[:, g], qh[:, 4 + g]], axis=1) for g in range(4)], axis=1)
    kk = w[:, 512:640]
    shared["w_in_cd"] = np.ascontiguousarray(np.concatenate([qg, qg[:, _partner_perm(512)], kk, kk[:, _partner_perm(128)], w[:, 640:768], w[:, 768:1280]], axis=1))
    sk = f("sink_c")[0]
    shared["sink"] = np.ascontiguousarray(np.concatenate([np.broadcast_to(sk[None, 0:4], (64, 4)), np.broadcast_to(sk[None, 4:8], (64, 4))], axis=0))
    shared["pool_w"] = np.ascontiguousarray(np.transpose(f("pool_w")[0], (1, 0, 2)))
    shared["pool_scale"] = np.ascontiguousarray(f("pool_scale")[0].reshape(4, 128).T)
    wo = f("w_out_cd")[0]
    rows = []
    for g in range(4):
        rows += list(range(g * 64, g * 64 + 64)) + list(range((4 + g) * 64, (4 + g) * 64 + 64))
    rows += list(range(512, 1024))
    shared["w_out_cd"] = np.ascontiguousarray(wo[np.array(rows)])
    shared["w_up"] = f("w_up")
    shared["cfw"] = np.ascontiguousarray(np.transpose(f("conv_ffn_w").reshape(2, 3, 44, 128), (3, 0, 2, 1)))
    shared["cfb"] = np.ascontiguousarray(np.transpose(f("conv_ffn_b").reshape(2, 44, 128), (2, 0, 1)))
    shared["w_down"] = f("w_down")
    shared["fnw"] = np.ascontiguousarray(f("final_norm_w").reshape(8, 128).T)
    shared["ident"] = np.eye(128, dtype=np.float32)
    shared["cos_t"], shared["sin_t"] = _rope_tables()
    kk_, qq_ = np.meshgrid(np.arange(128), np.arange(128), indexing="ij")
    m_prev = (kk_ >= qq_).astype(np.float32)
    m_next = (kk_ <= qq_).astype(np.float32)
    shared["masks"] = np.ascontiguousarray(np.stack([np.tile(m_prev, (1, 4)), np.tile(m_next, (1, 4))], axis=1))
    ic = np.zeros((4, 16), np.float32)
    for g, wdt in enumerate((2, 4, 8, 16)):
        for j in range(8):
            t = j
            ic[g, j] = 1.0 / (min(t + wdt // 2, NL) - max(t - wdt // 2, 0))
            t = NL - 8 + j
            ic[g, 8 + j] = 1.0 / (min(t + wdt // 2, NL) - max(t - wdt // 2, 0))
    shared["icnt"] = np.ascontiguousarray(np.broadcast_to(ic[None], (128, 4, 16)))
    oab = np.zeros((128, 2, 128), np.float32)
    oab[:, 0, 0:64] = 1.0
    oab[:, 1, 64:128] = 1.0
    shared["onesab"] = oab
    in_maps = []
    for i in range(NCORES):
        m = dict(shared)
        m["x"] = np.ascontiguousarray(x[BPC * i:BPC * (i + 1)])
        m["ctx"] = np.ascontiguousarray(ctx[BPC * i:BPC * (i + 1)])
        cv = np.stack([c[BPC * i], c[BPC * i + 1], c_ctx], axis=0)
        m["cT"] = np.ascontiguousarray(np.transpose(cv.reshape(3, 8, 128), (2, 1, 0)))
        in_maps.append(m)
    return in_maps


def kernel(**inputs):
    if "nc" not in _PROG:
        _PROG["nc"] = build_program()
    in_maps = _prepare(inputs)
    res = run_bass_kernel_spmd(_PROG["nc"], in_maps, core_ids=list(range(NCORES)))
    return np.concatenate([r["out"] for r in res.results], axis=0).astype(np.float32)
```

```python
import contextlib
import math
import numpy as np
import concourse.bass as bass
import concourse.mybir as mybir
from concourse.bass_utils import run_bass_kernel_spmd

F32 = mybir.dt.float32
BF16 = mybir.dt.bfloat16
AF = mybir.ActivationFunctionType
ALU = mybir.AluOpType
AX = mybir.AxisListType

D = 1024
NL = 2048
NCX = 256
NT = NL + NCX
DFF = 2816
EPS = 1e-6
NCORES = 8
BPC = 2
SEG = 256

DEBUG_STOP = None


class Eng:
    def __init__(self, name, e, sem, sid):
        self.name, self.e, self.sem, self.sid, self.cnt, self.seen = name, e, sem, sid, 0, {}


class Buf:
    __slots__ = ("name", "w", "r", "dsem", "dsid", "dcnt")

    def __init__(self, name):
        self.name, self.w, self.r = name, {}, {}
        self.dsem = None
        self.dcnt = 0


def _merge(dst, src):
    for k, v in src.items():
        if dst.get(k, 0) < v:
            dst[k] = v


class Sched:
    def __init__(self, nc, es):
        self.nc, self.es = nc, es
        self.sems = {}
        self.nsem = 0
        self.eng = {}
        for name, e in (("pe", nc.tensor), ("act", nc.scalar), ("dve", nc.vector), ("pool", nc.gpsimd), ("sp", nc.sync)):
            sem, sid = self.new_sem("e_" + name)
            self.eng[name] = Eng(name, e, sem, sid)

    def new_sem(self, name):
        sem = self.es.enter_context(self.nc.semaphore("%s_%d" % (name, self.nsem)))
        sid = self.nsem
        self.nsem += 1
        self.sems[sid] = sem
        return sem, sid

    def _need(self, reads, writes, acc, own_sid):
        need = {}
        for b in reads:
            _merge(need, b.w)
        for b in writes:
            if b.r:
                _merge(need, b.r)
            else:
                _merge(need, b.w)
        for b in acc:
            if b.r:
                _merge(need, b.r)
            else:
                for k, v in b.w.items():
                    if k != own_sid and need.get(k, 0) < v:
                        need[k] = v
        return need

    def _wait(self, E, need):
        for sid, val in need.items():
            if E.seen.get(sid, 0) < val:
                E.e.wait_ge(self.sems[sid], val)
                E.seen[sid] = val

    def op(self, en, fn, reads=(), writes=(), acc=()):
        E = self.eng[en]
        self._wait(E, self._need(reads, writes, acc, E.sid))
        ins = fn(E.e)
        E.cnt += 1
        ins.then_inc(E.sem, 1)
        for b in reads:
            if b.r.get(E.sid, 0) < E.cnt:
                b.r[E.sid] = E.cnt
        for b in writes:
            b.w = {E.sid: E.cnt}
            b.r = {}
        for b in acc:
            b.w = {E.sid: E.cnt}
            b.r = {}

    def dma(self, qn, out_ap, in_ap, owner, reads=(), writes=(), nowait=False):
        E = self.eng[qn]
        if owner.dsem is None:
            owner.dsem, owner.dsid = self.new_sem("d_" + owner.name)
        if not nowait:
            self._wait(E, self._need(reads, writes, (), -1))
        pairs = out_ap if isinstance(out_ap, list) else [(out_ap, in_ap)]
        for (o_, i_) in pairs:
            ins = E.e.dma_start(out=o_, in_=i_)
            owner.dcnt += 16
            ins.then_inc(owner.dsem, 16)
        for b in reads:
            if b.r.get(owner.dsid, 0) < owner.dcnt:
                b.r[owner.dsid] = owner.dcnt
        for b in writes:
            b.w = {owner.dsid: owner.dcnt}
            b.r = {}

    def sync_to(self, qn, names=("pe", "act", "dve")):
        self._wait(self.eng[qn], {self.eng[n].sid: self.eng[n].cnt for n in names if self.eng[n].cnt > 0})

    def barrier(self, names=("pe", "act", "dve", "sp")):
        clock = {self.eng[n].sid: self.eng[n].cnt for n in names if self.eng[n].cnt > 0}
        for n in names:
            E = self.eng[n]
            self._wait(E, {k: v for k, v in clock.items() if k != E.sid})


class Grid:
    def __init__(self, name, t):
        self.name, self.t, self.b = name, t, {}

    def R(self, c, a, b):
        out = []
        for k in range(a // SEG, (b - 1) // SEG + 1):
            key = (c, k)
            if key not in self.b:
                self.b[key] = Buf("%s_%s_%d" % (self.name, c, k))
            out.append(self.b[key])
        return out

    def RC(self, cs, a, b):
        out = []
        for c in cs:
            out += self.R(c, a, b)
        return out


def pc(t):
    return 1 + t if t < NL else 3 + t


def lam_init(layer):
    return 0.8 - 0.6 * math.exp(-0.3 * layer)


def build_program():
    nc = bass.Bass("TRN2", target_bir_lowering=False)
    di = {}

    def din(name, shape, dt=F32):
        di[name] = nc.dram_tensor(name, list(shape), dt, kind="ExternalInput").ap()
        return di[name]

    x_d = din("x", [BPC, NL, D])
    ctx_d = din("ctx", [BPC, NCX, D])
    cT_d = din("cT", [128, 8, 3])
    wmod_d = din("w_mod", [2, D, 6 * D])
    bmod_d = din("b_mod", [128, 2, 48])
    winab_d = din("w_in_ab", [D, 4096])
    conva_d = din("conv_a", [128, 4, 3])
    lamqk_d = din("lam_qk", [128, 4, 64])
    subln_d = din("subln", [128, 1])
    woutab_d = din("w_out_ab", [D, D])
    wincd_d = din("w_in_cd", [D, 1920])
    sink_d = din("sink", [128, 4])
    poolw_d = din("pool_w", [128, 4, 128])
    pscale_d = din("pool_scale", [128, 4])
    woutcd_d = din("w_out_cd", [D, D])
    wup_d = din("w_up", [2, D, 2 * DFF])
    cfw_d = din("cfw", [128, 2, 44, 3])
    cfb_d = din("cfb", [128, 2, 44])
    wdown_d = din("w_down", [2, DFF, D])
    fnw_d = din("fnw", [128, 8])
    ident_d = din("ident", [128, 128])
    cos_d = din("cos_t", [128, NL])
    sin_d = din("sin_t", [128, NL])
    mask_d = din("masks", [128, 2, 512])
    icnt_d = din("icnt", [128, 4, 16])
    onesab_d = din("onesab", [128, 2, 128])
    out_d = nc.dram_tensor("out", [BPC, NL, D], F32, kind="ExternalOutput").ap()
    dbg_d = None
    if DEBUG_STOP is not None:
        dbg_d = nc.dram_tensor("dbg", [128, 8, NT], F32, kind="ExternalOutput").ap()

    with contextlib.ExitStack() as es:
        S = Sched(nc, es)

        uniq = [0]

        def sb(name, shape, dt, stack=es):
            uniq[0] += 1
            return stack.enter_context(nc.sbuf_tensor("%s_%d" % (name, uniq[0]), list(shape), dt))

        XR_t = sb("XR", [128, 8, NT], F32)
        H_t = sb("H", [128, 8, NT], BF16)
        XR = Grid("XR", XR_t)
        H = Grid("H", H_t)
        PS_t = es.enter_context(nc.psum_tensor("PS", [128, 8, 512], F32))
        PS = [Buf("ps%d" % i) for i in range(8)]
        IDENT = sb("IDENT", [128, 128], F32)
        ONES = sb("ONES", [128, 128], BF16)
        ONESF = sb("ONESF", [128, 128], F32)
        MODS = sb("MODS", [128, 2, 6, 8, 3], F32)
        CT = sb("CT", [128, 8, 3], F32)
        SC = sb("SC", [128, 8, 3], F32)
        SCB = sb("SCB", [128, 8, 3], BF16)
        BM = sb("BM", [128, 2, 48], F32)
        CONVA = sb("CONVA", [128, 4, 3], F32)
        LS = sb("LS", [128, 4], F32)
        SUBW = sb("SUBW", [128, 1], F32)
        SINKE = sb("SINKE", [128, 4], F32)
        PSCALE = sb("PSCALE", [128, 4], F32)
        CFW = sb("CFW", [128, 2, 44, 3], F32)
        CFB = sb("CFB", [128, 2, 44], F32)
        FNW = sb("FNW", [128, 8], F32)
        CONST = Buf("const")
        NSLOT = 4
        SLOT_t = [sb("WS%d" % i, [128, 3072], BF16) for i in range(NSLOT)]
        SLOT = [Buf("ws%d" % i) for i in range(NSLOT)]
        slot_rr = [0]
        ps_rr = [0]

        def psb(pool=(0, 1, 2, 3)):
            i = pool[ps_rr[0] % len(pool)]
            ps_rr[0] += 1
            return i

        def cload(dst_ap, src_ap, q="sp"):
            S.dma(q, dst_ap, src_ap, CONST, writes=[CONST], nowait=True)

        cload(IDENT[:], ident_d)
        cload(CT[:], cT_d)
        cload(BM[:], bmod_d)
        cload(CONVA[:], conva_d)
        cload(SUBW[:], subln_d)
        cload(SINKE[:], sink_d)
        cload(PSCALE[:], pscale_d)
        cload(CFW[:], cfw_d)
        cload(CFB[:], cfb_d)
        cload(FNW[:], fnw_d)
        S.op("dve", lambda e: e.memset(ONES[:], 1.0), writes=[CONST])
        S.op("dve", lambda e: e.memset(ONESF[:], 1.0), writes=[CONST])

        def load_w(srcs, kc):
            i = slot_rr[0] % NSLOT
            slot_rr[0] += 1
            tot = sum(s.shape[1] for s in srcs)
            assert kc * tot <= 3072
            view = SLOT_t[i][:, 0:kc * tot].rearrange("p (k n) -> p k n", k=kc)
            off = 0
            pairs = []
            for s in srcs:
                n = s.shape[1]
                pairs.append((view[:, :, off:off + n], s.rearrange("(k p) n -> p k n", p=128)))
                off += n
            S.dma("pool", pairs, None, SLOT[i], writes=[SLOT[i]])
            return view, SLOT[i]

        def mm_group(bank, n, pairs, reads, m=128):
            def f(e):
                ins = None
                last = len(pairs) - 1
                for i, (lt, rh) in enumerate(pairs):
                    ins = e.matmul(PS_t[0:m, bank, 0:n], lhsT=lt, rhs=rh, start=(i == 0), stop=(i == last))
                return ins
            S.op("pe", f, reads=reads, writes=[PS[bank]])

        S.op("act", lambda e: e.activation(out=SC[:], in_=CT[:], func=AF.Silu), reads=[CONST], writes=[CONST])
        S.op("act", lambda e: e.activation(out=SCB[:], in_=CT[:], func=AF.Silu), reads=[CONST], writes=[CONST])

        def compute_mods(l):
            with contextlib.ExitStack() as st:
                S.barrier()
                MR_t = sb("MR", [3, 6144], F32, st)
                MR = Buf("MR")
                for blk in range(16):
                    wv, wb = load_w([wmod_d[l][:, blk * 384:(blk + 1) * 384]], 8)
                    bank = psb()
                    mm_group(bank, 384, [(SCB[:, kc, :], wv[:, kc, :]) for kc in range(8)], [wb, CONST], m=3)
                    S.op("act", lambda e, blk=blk, bank=bank: e.activation(out=MR_t[0:3, blk * 384:(blk + 1) * 384], in_=PS_t[0:3, bank, 0:384], func=AF.Copy), reads=[PS[bank]], writes=[MR])
                bank = psb()

                def tp(e, bank=bank):
                    ins = None
                    for j in range(48):
                        ins = e.transpose(out=PS_t[:, bank, j * 3:j * 3 + 3], in_=MR_t[0:3, j * 128:(j + 1) * 128], identity=IDENT[0:3, 0:3])
                    return ins
                S.op("pe", tp, reads=[MR, CONST], writes=[PS[bank]])
                for r in range(3):
                    S.op("dve", lambda e, r=r, l=l, bank=bank: e.tensor_tensor(
                        out=MODS[:, l].rearrange("p m c r -> p (m c) r")[:, :, r], in0=PS_t[:, bank, 0:144].rearrange("p (j r) -> p j r", r=3)[:, :, r], in1=BM[:, l, :], op=ALU.add),
                        reads=[PS[bank], CONST], writes=[CONST])
                for m in (1, 4):
                    S.op("dve", lambda e, l=l, m=m: e.tensor_scalar_add(out=MODS[:, l, m], in0=MODS[:, l, m], scalar1=1.0), reads=[CONST], writes=[CONST])
                S.barrier()
        compute_mods(0)
        with contextlib.ExitStack() as st:
            LQ = sb("LQ", [128, 4, 64], F32, st)
            LTMP = sb("LTMP", [128, 2, 64], F32, st)
            cload(LQ[:], lamqk_d)
            for q_ in range(2):
                S.op("dve", lambda e, q_=q_: e.tensor_tensor(out=LTMP[:, q_, :], in0=LQ[:, 2 * q_, :], in1=LQ[:, 2 * q_ + 1, :], op=ALU.mult), reads=[CONST], writes=[CONST])
            S.op("dve", lambda e: e.reduce_sum(out=LS[:, 0:2], in_=LTMP[:], axis=AX.X), reads=[CONST], writes=[CONST])
            S.op("act", lambda e: e.activation(out=LS[:, 0:2], in_=LS[:, 0:2], func=AF.Exp), reads=[CONST], writes=[CONST])
            S.op("dve", lambda e: e.tensor_tensor(out=LS[:, 2:3], in0=LS[:, 1:2], in1=LS[:, 0:1], op=ALU.subtract), reads=[CONST], writes=[CONST])
            S.op("dve", lambda e: e.tensor_scalar_add(out=LS[:, 2:3], in0=LS[:, 2:3], scalar1=-lam_init(0)), reads=[CONST], writes=[CONST])
            S.barrier()

        ddc = [0]

        def dd(name, src_ap, shape, dt):
            if DEBUG_STOP is None:
                return
            ddc[0] += 1
            dst = nc.dram_tensor(name, list(shape), dt, kind="ExternalOutput").ap()
            S.barrier()
            DB = Buf("dd%d" % ddc[0])
            S.dma("sp", dst, src_ap, DB, nowait=True)
            nc.sync.wait_ge(DB.dsem, DB.dcnt)
            for n_ in ("pe", "act", "dve"):
                S._wait(S.eng[n_], {DB.dsid: DB.dcnt})

        def mod(l, m, c, r):
            return MODS[:, l, m, c, r:r + 1]

        NEGLAM = LS[:, 2:3]
        S.op("dve", lambda e: e.tensor_scalar_mul(out=SUBW[:], in0=SUBW[:], scalar1=1.0 - lam_init(0)), reads=[CONST], writes=[CONST])
        S.op("act", lambda e: e.activation(out=SINKE[:], in_=SINKE[:], func=AF.Exp), reads=[CONST], writes=[CONST])

        def load_rope(st):
            COS = sb("COS", [128, NL], BF16, st)
            SIN = sb("SIN", [128, NL], BF16, st)
            RB = Buf("rope")
            S.sync_to("pool")
            S.dma("pool", [(COS[:], cos_d), (SIN[:], sin_d)], None, RB, writes=[RB])
            return COS, SIN, RB

        TILES_L = [(i * 512, (i + 1) * 512) for i in range(4)]
        TILES_ALL = TILES_L + [(NL, NT)]

        def load_x(b):
          with contextlib.ExitStack() as st:
            S.barrier()
            XS_t = [sb("XS%d" % i, [128, D], F32, st) for i in range(2)]
            XS = [Buf("xs%d" % i) for i in range(2)]
            for j in range(NT // 128):
                k = j % 2
                src = x_d[b, j * 128:(j + 1) * 128, :] if j < 16 else ctx_d[b, (j - 16) * 128:(j - 15) * 128, :]
                S.dma("sp", XS_t[k][:], src, XS[k], writes=[XS[k]])
                for hh in range(2):
                    bank = psb()

                    def tp(e, k=k, hh=hh, bank=bank):
                        ins = None
                        for q in range(4):
                            c = hh * 4 + q
                            ins = e.transpose(out=PS_t[:, bank, q * 128:(q + 1) * 128], in_=XS_t[k][:, c * 128:(c + 1) * 128], identity=IDENT[:])
                        return ins
                    S.op("pe", tp, reads=[XS[k], CONST], writes=[PS[bank]])
                    en = "act" if hh == 0 else "dve"
                    dst = XR_t[:, hh * 4:hh * 4 + 4, j * 128:(j + 1) * 128]
                    srcp = PS_t[:, bank, :].rearrange("p (q n) -> p q n", q=4)
                    wr = XR.RC(range(hh * 4, hh * 4 + 4), j * 128, (j + 1) * 128)
                    if en == "act":
                        S.op("act", lambda e, dst=dst, srcp=srcp: e.activation(out=dst, in_=srcp, func=AF.Copy), reads=[PS[bank]], writes=wr)
                    else:
                        S.op("dve", lambda e, dst=dst, srcp=srcp: e.tensor_copy(out=dst, in_=srcp), reads=[PS[bank]], writes=wr)

        def norm_stage(l, b, m_shift, m_scale, tiles, st):
            SQ_t = sb("SQ", [128, 8, 512], BF16, st)
            SQ = Buf("SQ")
            RS_t = [sb("RS%d" % i, [128, 512], F32, st) for i in range(2)]
            RS = [Buf("RS%d" % i) for i in range(2)]
            TM_t = [sb("TM%d" % i, [128, 512], F32, st) for i in range(2)]
            TM = [Buf("TM%d" % i) for i in range(2)]
            SQB = [Buf("SQ%d" % c) for c in range(8)]

            def sq(ti, a, bb, c):
                n = bb - a
                if c < 4:
                    S.op("act", lambda e: e.activation(out=SQ_t[:, c, 0:n], in_=XR_t[:, c, a:bb], func=AF.Square), reads=XR.R(c, a, bb), writes=[SQB[c]])
                else:
                    S.op("dve", lambda e: e.tensor_tensor(out=SQ_t[:, c, 0:n], in0=XR_t[:, c, a:bb], in1=XR_t[:, c, a:bb], op=ALU.mult), reads=XR.R(c, a, bb), writes=[SQB[c]])

            def stat(ti, a, bb):
                n = bb - a
                bank = psb()
                mm_group(bank, n, [(ONES[:], SQ_t[:, c, 0:n]) for c in range(8)], SQB + [CONST])
                k = ti % 2
                S.op("act", lambda e: e.activation(out=RS_t[k][:, 0:n], in_=PS_t[:, bank, 0:n], func=AF.Ln, bias=EPS, scale=1.0 / D), reads=[PS[bank]], writes=[RS[k]])
                S.op("act", lambda e: e.activation(out=RS_t[k][:, 0:n], in_=RS_t[k][:, 0:n], func=AF.Exp, scale=-0.5), reads=[RS[k]], writes=[RS[k]])

            def modl(ti, a, bb, c):
                n = bb - a
                r = b if a < NL else 2
                k = ti % 2
                kk = c % 2
                S.op("dve", lambda e: e.tensor_tensor(out=TM_t[kk][:, 0:n], in0=XR_t[:, c, a:bb], in1=RS_t[k][:, 0:n], op=ALU.mult), reads=XR.R(c, a, bb) + [RS[k]], writes=[TM[kk]])
                S.op("act", lambda e: e.activation(out=H_t[:, c, a:bb], in_=TM_t[kk][:, 0:n], func=AF.Identity, scale=mod(l, m_scale, c, r), bias=mod(l, m_shift, c, r)), reads=[TM[kk], CONST], writes=H.R(c, a, bb))
            for c in range(8):
                sq(0, *tiles[0], c)
            stat(0, *tiles[0])
            for ti in range(len(tiles)):
                nxt = ti + 1 < len(tiles)
                for c in range(8):
                    if nxt:
                        sq(ti + 1, *tiles[ti + 1], c)
                    modl(ti, *tiles[ti], c)
                if nxt:
                    stat(ti + 1, *tiles[ti + 1])

        def ktile_h(a, bb):
            return [H_t[:, k, a:bb] for k in range(8)], H.RC(range(8), a, bb)

        def ffn_stage(l, b, with_ctx):
            with contextlib.ExitStack() as st:
                S.barrier()
                tiles = TILES_ALL if with_ctx else TILES_L
                with contextlib.ExitStack() as st2:
                    norm_stage(l, b, 3, 4, tiles, st2)
                S.barrier()
                ACTS_t = sb("ACTS", [128, 22, 1024], BF16, st)
                ACTS = [Buf("acts%d" % i) for i in range(22)]
                U_t = [[sb("U%d%d" % (p, i), [128, 1028], BF16, st) for i in range(2)] for p in range(2)]
                U = [[Buf("U%d%d" % (p, i)) for i in range(2)] for p in range(2)]
                TT_t = [[sb("TT%d%d" % (p, i), [128, 1024], BF16, st) for i in range(2)] for p in range(2)]
                TT = [[Buf("TT%d%d" % (p, i)) for i in range(2)] for p in range(2)]
                wins = [(0, 1024, 0, NL), (1024, 2048, 0, NL)] + ([(NL, NT, NL, NT)] if with_ctx else [])
                for (a, bb, s0, s1) in wins:
                    W = bb - a
                    r = b if a < NL else 2
                    ua, ub = max(a - 1, s0), min(bb + 1, s1)
                    ncol = ub - ua
                    nt = (ncol + 511) // 512
                    base = (ncol + nt - 1) // nt
                    ctiles = [(ua + i * base, min(ua + (i + 1) * base, ub)) for i in range(nt)]
                    pend = [None]
                    for p_ in range(2):
                        for k_ in range(2):
                            if ua == a:
                                S.op("dve", lambda e, p_=p_, k_=k_: e.memset(U_t[p_][k_][:, 0:1], 0.0), writes=[U[p_][k_]])
                            if ub == bb:
                                S.op("dve", lambda e, p_=p_, k_=k_, W=W: e.memset(U_t[p_][k_][:, W + 1:W + 2], 0.0), writes=[U[p_][k_]])
                    for i in range(22):
                        wv, wb = load_w([wup_d[l][:, i * 128:(i + 1) * 128], wup_d[l][:, DFF + i * 128:DFF + (i + 1) * 128]], 8)
                        k = i % 2
                        for p in range(2):
                            ch = p * 22 + i
                            Ut, Ub = U_t[p][k], U[p][k]
                            for (ca, cb) in ctiles:
                                n = cb - ca
                                bank = psb()
                                rh, rb = ktile_h(ca, cb)
                                mm_group(bank, n, [(wv[:, kc, p * 128:(p + 1) * 128], rh[kc]) for kc in range(8)], rb + [wb])
                                o = 1 + ca - a
                                S.op("act", lambda e, Ut=Ut, o=o, n=n, bank=bank: e.activation(out=Ut[:, o:o + n], in_=PS_t[:, bank, 0:n], func=AF.Copy), reads=[PS[bank]], writes=[Ub])
                            Tt, Tb = TT_t[p][k], TT[p][k]
                            S.op("dve", lambda e, Tt=Tt, Ut=Ut, W=W, ch=ch: e.tensor_scalar(out=Tt[:, 0:W], in0=Ut[:, 1:W + 1], scalar1=CFW[:, l, ch, 1:2], scalar2=CFB[:, l, ch:ch + 1], op0=ALU.mult, op1=ALU.add), reads=[Ub, CONST], writes=[Tb])
                            S.op("dve", lambda e, Tt=Tt, Ut=Ut, W=W, ch=ch: e.scalar_tensor_tensor(out=Tt[:, 0:W], in0=Ut[:, 0:W], scalar=CFW[:, l, ch, 0:1], in1=Tt[:, 0:W], op0=ALU.mult, op1=ALU.add), reads=[Ub, Tb, CONST], writes=[Tb])
                            S.op("dve", lambda e, Tt=Tt, Ut=Ut, W=W, ch=ch: e.scalar_tensor_tensor(out=Tt[:, 0:W], in0=Ut[:, 2:W + 2], scalar=CFW[:, l, ch, 2:3], in1=Tt[:, 0:W], op0=ALU.mult, op1=ALU.add), reads=[Ub, Tb, CONST], writes=[Tb])
                        def fin(i=i, k=k, W=W):
                            S.op("act", lambda e: e.activation(out=TT_t[1][k][:, 0:W], in_=TT_t[1][k][:, 0:W], func=AF.Silu), reads=[TT[1][k]], writes=[TT[1][k]])
                            S.op("dve", lambda e: e.tensor_tensor(out=ACTS_t[:, i, 0:W], in0=TT_t[1][k][:, 0:W], in1=TT_t[0][k][:, 0:W], op=ALU.mult), reads=[TT[0][k], TT[1][k]], writes=[ACTS[i]])
                        if pend[0] is not None:
                            pend[0]()
                        pend[0] = fin
                    pend[0]()
                    pend[0] = None
                    otiles = [(a + i * 512, min(a + (i + 1) * 512, bb)) for i in range((W + 511) // 512)]
                    groups = [(oc, ca, cb) for oc in range(8) for (ca, cb) in otiles]
                    wts = {}
                    KS = 20
                    for gi in range(0, len(groups), 4):
                        batch = []
                        for (oc, ca, cb) in groups[gi:gi + 4]:
                            if oc not in wts:
                                wts[oc] = load_w([wdown_d[l][:, oc * 128:(oc + 1) * 128]], 22)
                            wv, wb = wts[oc]
                            n = cb - ca
                            bank = psb()

                            def fa(e, wv=wv, ca=ca, cb=cb, n=n, bank=bank):
                                ins = None
                                for kc in range(KS):
                                    ins = e.matmul(PS_t[:, bank, 0:n], lhsT=wv[:, kc, :], rhs=ACTS_t[:, kc, ca - a:cb - a], start=(kc == 0), stop=False)
                                return ins
                            S.op("pe", fa, reads=ACTS[0:KS] + [wb], writes=[PS[bank]])
                            batch.append((oc, ca, cb, n, bank, wv, wb))
                        for (oc, ca, cb, n, bank, wv, wb) in batch:
                            def fb(e, wv=wv, ca=ca, cb=cb, n=n, bank=bank):
                                ins = None
                                for kc in range(KS, 22):
                                    ins = e.matmul(PS_t[:, bank, 0:n], lhsT=wv[:, kc, :], rhs=ACTS_t[:, kc, ca - a:cb - a], start=False, stop=(kc == 21))
                                return ins
                            S.op("pe", fb, reads=ACTS[KS:22] + [wb], acc=[PS[bank]])
                            S.op("dve", lambda e, oc=oc, ca=ca, cb=cb, n=n, bank=bank, r=r: e.scalar_tensor_tensor(out=XR_t[:, oc, ca:cb], in0=PS_t[:, bank, 0:n], scalar=mod(l, 5, oc, r), in1=XR_t[:, oc, ca:cb], op0=ALU.mult, op1=ALU.add), reads=[PS[bank], CONST] + XR.R(oc, ca, cb), writes=XR.R(oc, ca, cb))

        def wout_stage(l, b, w_d, kfun, tiles):
            for oc in range(8):
                wv, wb = load_w([w_d[:, oc * 128:(oc + 1) * 128]], 8)
                for (a, bb) in tiles:
                    n = bb - a
                    r = b if a < NL else 2
                    bank = psb()
                    rh, rb = kfun(a, bb)
                    mm_group(bank, n, [(wv[:, kc, :], rh[kc]) for kc in range(8)], rb + [wb])
                    S.op("dve", lambda e, oc=oc, a=a, bb=bb, n=n, bank=bank, r=r: e.scalar_tensor_tensor(out=XR_t[:, oc, a:bb], in0=PS_t[:, bank, 0:n], scalar=mod(l, 2, oc, r), in1=XR_t[:, oc, a:bb], op0=ALU.mult, op1=ALU.add), reads=[PS[bank], CONST] + XR.R(oc, a, bb), writes=XR.R(oc, a, bb))

        def mixer_ab(l, b):
            with contextlib.ExitStack() as st:
                S.barrier()
                with contextlib.ExitStack() as st2:
                    norm_stage(l, b, 0, 1, TILES_ALL, st2)
                S.barrier()
                COS, SIN, RB = load_rope(st)
                YAC_t = sb("YAC", [128, 4, 2312], BF16, st)
                YAC = [Buf("yac%d" % c) for c in range(4)]
                with contextlib.ExitStack() as st2:
                    PP2_t = [sb("PP%d" % i, [128, 2312], BF16, st2) for i in range(2)]
                    PP2 = [Buf("PP%d" % i) for i in range(2)]
                    GBB2_t = [sb("GBB%d" % i, [128, 2312], BF16, st2) for i in range(2)]
                    GBB2 = [Buf("GBB%d" % i) for i in range(2)]
                    T_t = sb("T", [128, 2306], F32, st2)
                    T = Buf("T")
                    GCS_t = [sb("GCS%d" % i, [128, 512], BF16, st2) for i in range(2)]
                    GCS = [Buf("GCS%d" % i) for i in range(2)]
                    for i_ in range(2):
                        S.op("dve", lambda e, i_=i_: e.memset(PP2_t[i_][:], 0.0), writes=[PP2[i_]])
                        S.op("dve", lambda e, i_=i_: e.memset(GBB2_t[i_][:], 0.0), writes=[GBB2[i_]])
                    for c in range(4):
                        PP_t, PP, GBB_t, GBB = PP2_t[c % 2], PP2[c % 2], GBB2_t[c % 2], GBB2[c % 2]
                        wv, wb = load_w([winab_d[:, g * 512 + c * 128:g * 512 + (c + 1) * 128] for g in range(3)], 8)
                        for ti, (a, bb) in enumerate(TILES_ALL):
                            n = bb - a
                            rh, rb = ktile_h(a, bb)
                            banks = [psb() for _ in range(3)]
                            for g in range(3):
                                mm_group(banks[g], n, [(wv[:, kc, g * 128:(g + 1) * 128], rh[kc]) for kc in range(8)], rb + [wb])
                            k = ti % 2
                            o = pc(a)
                            S.op("act", lambda e, k=k, n=n, bk=banks[1]: e.activation(out=GCS_t[k][:, 0:n], in_=PS_t[:, bk, 0:n], func=AF.Copy), reads=[PS[banks[1]]], writes=[GCS[k]])
                            S.op("dve", lambda e, k=k, n=n, o=o, bk=banks[2]: e.tensor_tensor(out=PP_t[:, o:o + n], in0=PS_t[:, bk, 0:n], in1=GCS_t[k][:, 0:n], op=ALU.mult), reads=[PS[banks[2]], GCS[k]], writes=[PP])
                            S.op("act", lambda e, n=n, o=o, bk=banks[0]: e.activation(out=GBB_t[:, o:o + n], in_=PS_t[:, bk, 0:n], func=AF.Copy), reads=[PS[banks[0]]], writes=[GBB])
                        S.op("dve", lambda e, c=c: e.tensor_scalar(out=T_t[:], in0=PP_t[:, 1:2307], scalar1=CONVA[:, c, 1:2], scalar2=None, op0=ALU.mult), reads=[PP, CONST], writes=[T])
                        S.op("dve", lambda e, c=c: e.scalar_tensor_tensor(out=T_t[:], in0=PP_t[:, 0:2306], scalar=CONVA[:, c, 0:1], in1=T_t[:], op0=ALU.mult, op1=ALU.add), reads=[PP, T, CONST], writes=[T])
                        S.op("dve", lambda e, c=c: e.scalar_tensor_tensor(out=T_t[:], in0=PP_t[:, 2:2308], scalar=CONVA[:, c, 2:3], in1=T_t[:], op0=ALU.mult, op1=ALU.add), reads=[PP, T, CONST], writes=[T])
                        S.op("dve", lambda e, c=c: e.tensor_tensor(out=YAC_t[:, c, 1:2307], in0=T_t[:], in1=GBB_t[:, 1:2307], op=ALU.mult), reads=[T, GBB], writes=[YAC[c]])
                    S.barrier()
                YB_t = sb("YB", [128, 4, NT], BF16, st)
                YB = Grid("YB", YB_t)
                QC_t = sb("QC", [128, NT], BF16, st)
                KC_t = sb("KC", [128, NT], BF16, st)
                VC_t = sb("VC", [128, 18, 128], BF16, st)
                QC, KC, VC = Grid("QC", QC_t), Grid("KC", KC_t), Buf("VC")
                PT_t = [sb("PT%d" % i, [128, 2, 512], BF16, st) for i in range(3)]
                PT = [Buf("PT%d" % i) for i in range(3)]
                RT_t = [sb("RT0", [128, 512], BF16, st), None]
                RT = [Buf("RT0"), None]
                ZA_t = [sb("ZA%d" % i, [128, 512], BF16, st) for i in range(2)]
                ZA = [Buf("ZA%d" % i) for i in range(2)]
                RT_t[1] = sb("RT1", [128, 512], BF16, st)
                RT[1] = Buf("RT1")
                QZ_t = [sb("QZ%d" % i, [128, 512], BF16, st) for i in range(2)]
                QZ = [Buf("QZ%d" % i) for i in range(2)]
                for m_ in range(2):
                    S.op("dve", lambda e, m_=m_: e.memset(QZ_t[m_][:], 0.0), writes=[QZ[m_]])
                YR_t, YR = RT_t[1], RT[1]
                SQ1_t = sb("SQ1", [128, 512], BF16, st)
                SQ1 = Buf("SQ1")
                pend_tail = [None]
                sp_rr = [0]
                for c in range(4):
                    wvq, wbq = load_w([winab_d[:, 1536 + c * 128:1536 + (c + 1) * 128], winab_d[:, 3072 + c * 128:3072 + (c + 1) * 128]], 8)
                    wvk, wbk = load_w([winab_d[:, 2048 + c * 128:2048 + (c + 1) * 128], winab_d[:, 3584 + c * 128:3584 + (c + 1) * 128]], 8)
                    wv2, wb2 = load_w([winab_d[:, 2560 + c * 128:2560 + (c + 1) * 128]], 8)
                    for (a, bb) in TILES_ALL:
                        n = bb - a
                        rh, rb = ktile_h(a, bb)
                        for qi, (dst_t, dst_g, wv, wb) in enumerate(((QC_t, QC, wvq, wbq), (KC_t, KC, wvk, wbk))):
                            b0 = psb()
                            mm_group(b0, n, [(wv[:, kc, 0:128], rh[kc]) for kc in range(8)], rb + [wb])
                            if a < NL:
                                b1 = psb()
                                mm_group(b1, n, [(wv[:, kc, 128:256], rh[kc]) for kc in range(8)], rb + [wb])
                                S.op("dve", lambda e, a=a, bb=bb, b0=b0: e.tensor_tensor(out=RT_t[0][:, 0:512], in0=PS_t[:, b0, 0:512], in1=COS[:, a:bb], op=ALU.mult), reads=[PS[b0], RB], writes=[RT[0]])
                                S.op("dve", lambda e, a=a, bb=bb, b1=b1: e.tensor_tensor(out=RT_t[1][:, 0:512], in0=PS_t[:, b1, 0:512], in1=SIN[:, a:bb], op=ALU.mult), reads=[PS[b1], RB], writes=[RT[1]])
                                S.op("dve", lambda e, a=a, bb=bb, dst_t=dst_t: e.tensor_tensor(out=dst_t[:, a:bb], in0=RT_t[0][:, 0:512], in1=RT_t[1][:, 0:512], op=ALU.add), reads=[RT[0], RT[1]], writes=dst_g.R(0, a, bb))
                            else:
                                S.op("act", lambda e, a=a, bb=bb, n=n, b0=b0, dst_t=dst_t: e.activation(out=dst_t[:, a:bb], in_=PS_t[:, b0, 0:n], func=AF.Copy), reads=[PS[b0]], writes=dst_g.R(0, a, bb))
                    for kcnk in range(18):
                        bank = psb()
                        a = kcnk * 128
                        mm_group(bank, 128, [(H_t[:, kc, a:a + 128], wv2[:, kc, :]) for kc in range(8)], H.RC(range(8), a, a + 128) + [wb2])
                        S.op("act", lambda e, kcnk=kcnk, bank=bank: e.activation(out=VC_t[:, kcnk, :], in_=PS_t[:, bank, 0:128], func=AF.Copy), reads=[PS[bank]], writes=[VC])
                    def qz0(a, bb):
                        S.op("act", lambda e: e.activation(out=QZ_t[0][0:64, 0:bb - a], in_=QC_t[0:64, a:bb], func=AF.Copy), reads=QC.R(0, a, bb), writes=[QZ[0]])
                    for tix, (a, bb) in enumerate(TILES_ALL):
                        n = bb - a
                        kchunks = list(range(18)) if a < NL else [16, 17]
                        kpairs = [(kchunks[i], kchunks[i + 1]) for i in range(0, len(kchunks), 2)]
                        its = [(m, pr) for m in range(2) for pr in kpairs]
                        sb_of = {}
                        if tix == 0:
                            qz0(a, bb)
                        S.op("dve", lambda e, a=a, bb=bb, n=n: e.tensor_copy(out=QZ_t[1][64:128, 0:n], in_=QC_t[64:128, a:bb]), reads=QC.R(0, a, bb), writes=[QZ[1]])

                        def emit_s(j):
                            m, pr = its[j]
                            b0 = 2 * (sp_rr[0] % 2)
                            sp_rr[0] += 1
                            sb_of[j] = b0
                            for q_, kk in enumerate(pr):
                                mm_group(b0 + q_, n, [(KC_t[:, kk * 128:(kk + 1) * 128], QZ_t[m][:, 0:n])], KC.R(0, kk * 128, (kk + 1) * 128) + [QZ[m]])
                        emit_s(0)
                        for j, (m, pr) in enumerate(its):
                            if j + 1 < len(its):
                                emit_s(j + 1)
                            if j == len(kpairs) and tix + 1 < len(TILES_ALL):
                                qz0(*TILES_ALL[tix + 1])
                            if j == 6 and pend_tail[0] is not None:
                                pend_tail[0](7)
                                pend_tail[0] = None
                            b0 = sb_of[j]
                            pk = j % 3
                            S.op("act", lambda e, pk=pk, b0=b0, n=n: e.activation(out=PT_t[pk][:, :, 0:n], in_=PS_t[:, b0:b0 + 2, 0:n], func=AF.Exp, scale=0.125), reads=[PS[b0], PS[b0 + 1]], writes=[PT[pk]])
                            first, last = (pr == kpairs[0]), (pr == kpairs[-1])

                            def pv(e, m=m, pr=pr, pk=pk, n=n, first=first, last=last):
                                e.matmul(PS_t[:, 4 + m, 0:n], lhsT=VC_t[:, pr[0], :], rhs=PT_t[pk][:, 0, 0:n], start=first, stop=False)
                                return e.matmul(PS_t[:, 4 + m, 0:n], lhsT=VC_t[:, pr[1], :], rhs=PT_t[pk][:, 1, 0:n], start=False, stop=last)
                            if first:
                                S.op("pe", pv, reads=[PT[pk], VC], writes=[PS[4 + m]])
                                S.op("dve", lambda e, m=m, pk=pk, n=n: e.tensor_tensor(out=ZA_t[m][:, 0:n], in0=PT_t[pk][:, 0, 0:n], in1=PT_t[pk][:, 1, 0:n], op=ALU.add), reads=[PT[pk]], writes=[ZA[m]])
                            else:
                                S.op("pe", pv, reads=[PT[pk], VC], acc=[PS[4 + m]])
                                for q_ in range(2):
                                    S.op("dve", lambda e, m=m, pk=pk, n=n, q_=q_: e.tensor_tensor(out=ZA_t[m][:, 0:n], in0=ZA_t[m][:, 0:n], in1=PT_t[pk][:, q_, 0:n], op=ALU.add), reads=[PT[pk], ZA[m]], writes=[ZA[m]])
                            if last:
                                mm_group(6 + m, n, [(ONES[:], ZA_t[m][:, 0:n])], [ZA[m], CONST])
                        if pend_tail[0] is not None:
                            pend_tail[0](psb())
                            pend_tail[0] = None
                        for m in range(2):
                            S.op("act", lambda e, m=m, n=n: e.activation(out=RT_t[m][:, 0:n], in_=PS_t[:, 6 + m, 0:n], func=AF.Ln), reads=[PS[6 + m]], writes=[RT[m]])
                            S.op("act", lambda e, m=m, n=n: e.activation(out=RT_t[m][:, 0:n], in_=RT_t[m][:, 0:n], func=AF.Exp, scale=-1.0), reads=[RT[m]], writes=[RT[m]])
                            S.op("dve", lambda e, m=m, n=n: e.tensor_tensor(out=RT_t[m][:, 0:n], in0=PS_t[:, 4 + m, 0:n], in1=RT_t[m][:, 0:n], op=ALU.mult), reads=[PS[4 + m], RT[m]], writes=[RT[m]])
                        S.op("dve", lambda e, n=n: e.scalar_tensor_tensor(out=YR_t[:, 0:n], in0=RT_t[1][:, 0:n], scalar=NEGLAM, in1=RT_t[0][:, 0:n], op0=ALU.mult, op1=ALU.add), reads=[RT[0], RT[1], CONST], writes=[YR])
                        S.op("dve", lambda e, n=n: e.tensor_tensor(out=SQ1_t[:, 0:n], in0=YR_t[:, 0:n], in1=YR_t[:, 0:n], op=ALU.mult), reads=[YR], writes=[SQ1])
                        def tail(bk, n=n, a=a, bb=bb, c=c):
                            mm_group(bk, n, [(ONES[:], SQ1_t[:, 0:n])], [SQ1, CONST])
                            S.op("act", lambda e: e.activation(out=RT_t[0][:, 0:n], in_=PS_t[:, bk, 0:n], func=AF.Ln, bias=EPS, scale=1.0 / 128), reads=[PS[bk]], writes=[RT[0]])
                            S.op("act", lambda e: e.activation(out=RT_t[0][:, 0:n], in_=RT_t[0][:, 0:n], func=AF.Exp, scale=-0.5), reads=[RT[0]], writes=[RT[0]])
                            S.op("dve", lambda e: e.scalar_tensor_tensor(out=YB_t[:, c, a:bb], in0=YR_t[:, 0:n], scalar=SUBW[:, 0:1], in1=RT_t[0][:, 0:n], op0=ALU.mult, op1=ALU.mult), reads=[YR, RT[0], CONST], writes=YB.R(c, a, bb))
                        pend_tail[0] = tail
                    pend_tail[0](psb())
                    pend_tail[0] = None
                    if c == 0 and b == 0:
                        dd("d_qc", QC_t[:], [128, NT], BF16)
                        dd("d_kc", KC_t[:], [128, NT], BF16)
                        dd("d_cos", COS[:], [128, NL], BF16)
                        dd("d_sin", SIN[:], [128, NL], BF16)

                def kfun(a, bb):
                    o = pc(a)
                    return ([YAC_t[:, c, o:o + (bb - a)] for c in range(4)] + [YB_t[:, c, a:bb] for c in range(4)],
                            YAC + YB.RC(range(4), a, bb))
                wout_stage(l, b, woutab_d, kfun, TILES_ALL)

        def mixer_cd(l, b):
            with contextlib.ExitStack() as st:
                S.barrier()
                with contextlib.ExitStack() as st2:
                    norm_stage(l, b, 0, 1, TILES_ALL, st2)
                S.barrier()
                COS, SIN, RB = load_rope(st)
                MASKS = sb("MASKS", [128, 2, 512], BF16, st)
                ONESAB = sb("ONESAB", [128, 2, 128], BF16, st)
                POOLW = sb("POOLW", [128, 4, 128], BF16, st)
                ICNT = sb("ICNT", [128, 4, 16], F32, st)
                CB = Buf("cdconst")
                S.sync_to("pool")
                S.dma("pool", [(MASKS[:], mask_d), (ONESAB[:], onesab_d), (POOLW[:], poolw_d)], None, CB, writes=[CB])
                CB2 = Buf("cdconst2")
                S.dma("sp", ICNT[:], icnt_d, CB2, writes=[CB2], nowait=True)
                YD_t = sb("YD", [128, 4, NL], BF16, st)
                YD = Grid("YD", YD_t)
                with contextlib.ExitStack() as st2:
                    PX_t = sb("PX", [128, 2080], F32, st2)
                    PA_t = sb("PA", [128, 2080], F32, st2)
                    PB_t = sb("PB", [128, 2080], F32, st2)
                    PX, PA, PB = Buf("PX"), Buf("PA"), Buf("PB")
                    ZG_t = sb("ZG", [128, NL], BF16, st2)
                    ZG = Buf("ZG")
                    ET_t = sb("ET", [128, 16], F32, st2)
                    ET = Buf("ET")
                    S.op("dve", lambda e: e.memset(PX_t[:], 0.0), writes=[PX])
                    S.op("dve", lambda e: e.memset(PA_t[:], 0.0), writes=[PA])
                    S.op("dve", lambda e: e.memset(PB_t[:], 0.0), writes=[PB])
                    for g in range(4):
                        wv, wb = load_w([wincd_d[:, 1408 + g * 128:1408 + (g + 1) * 128]], 8)
                        for (a, bb) in TILES_L:
                            rh, rb = ktile_h(a, bb)
                            bank = psb()
                            mm_group(bank, 512, [(wv[:, kc, :], rh[kc]) for kc in range(8)], rb + [wb])
                            S.op("act", lambda e, a=a, bank=bank: e.activation(out=PX_t[:, 16 + a:16 + a + 512], in_=PS_t[:, bank, 0:512], func=AF.Copy), reads=[PS[bank]], writes=[PX])
                        w = (2, 4, 8, 16)[g]
                        cur_t, cur = PX_t, PX
                        sh = 1
                        dsts = [(PA_t, PA), (PB_t, PB)]
                        di_ = 0
                        while sh < w:
                            dt_, db_ = dsts[di_ % 2]
                            di_ += 1
                            S.op("dve", lambda e, dt_=dt_, cur_t=cur_t, sh=sh: e.tensor_tensor(out=dt_[:, 15:2080], in0=cur_t[:, 15:2080], in1=cur_t[:, 15 - sh:2080 - sh], op=ALU.add), reads=[cur], writes=[db_])
                            cur_t, cur = dt_, db_
                            sh *= 2
                        o = 16 + w // 2 - 1
                        S.op("dve", lambda e, cur_t=cur_t, o=o, w=w: e.scalar_tensor_tensor(out=ZG_t[:], in0=cur_t[:, o:o + NL], scalar=1.0 / w, in1=PX_t[:, 16:16 + NL], op0=ALU.mult, op1=ALU.subtract), reads=[cur, PX], writes=[ZG])
                        for side in range(2):
                            t0 = 0 if side == 0 else NL - 8
                            S.op("dve", lambda e, cur_t=cur_t, o=o, t0=t0, g=g, side=side: e.tensor_tensor(out=ET_t[:, 0:8], in0=cur_t[:, o + t0:o + t0 + 8], in1=ICNT[:, g, side * 8:side * 8 + 8], op=ALU.mult), reads=[cur, CONST, CB, CB2, RB], writes=[ET])
                            S.op("dve", lambda e, t0=t0: e.tensor_tensor(out=ZG_t[:, t0:t0 + 8], in0=ET_t[:, 0:8], in1=PX_t[:, 16 + t0:16 + t0 + 8], op=ALU.subtract), reads=[ET, PX, ZG], writes=[ZG])
                        for (a, bb) in TILES_L:
                            bank = psb()
                            mm_group(bank, 512, [(POOLW[:, g, :], ZG_t[:, a:bb])], [ZG, CONST, CB, RB])
                            S.op("act", lambda e, a=a, bb=bb, g=g, bank=bank: e.activation(out=YD_t[:, g, a:bb], in_=PS_t[:, bank, 0:512], func=AF.Identity, scale=PSCALE[:, g:g + 1]), reads=[PS[bank], CONST, CB, RB], writes=YD.R(g, a, bb))
                    S.barrier()
                QT_t = sb("QT", [128, 4, NL], BF16, st)
                QT = Grid("QT", QT_t)
                KT_t = sb("KT", [128, 2, NT], BF16, st)
                KT = Grid("KT", KT_t)
                S.op("dve", lambda e: e.memset(KT_t[:], 0.0), writes=KT.RC(range(2), 0, NT))
                VAB_t = sb("VAB", [128, 2, 18, 128], BF16, st)
                VAB = Buf("VAB")
                YA1_t, YA1 = QT_t, QT
                RT_t = [sb("RTc%d" % i, [128, 512], F32, st) for i in range(2)]
                RT = [Buf("RTc%d" % i) for i in range(2)]
                PT_t = [sb("PTc%d" % i, [128, 512], BF16, st) for i in range(4)]
                PT = [Buf("PTc%d" % i) for i in range(4)]
                S.op("dve", lambda e: e.memset(VAB_t[:], 0.0), writes=[VAB])

                def rope_proj(wv, wb, ci, dst_fn, dst_bufs_fn, tiles):
                    for (a, bb) in tiles:
                        n = bb - a
                        rh, rb = ktile_h(a, bb)
                        b0 = psb()
                        mm_group(b0, n, [(wv[:, kc, ci * 128:(ci + 1) * 128], rh[kc]) for kc in range(8)], rb + [wb])
                        if a < NL:
                            b1 = psb()
                            mm_group(b1, n, [(wv[:, kc, (ci + 1) * 128:(ci + 2) * 128], rh[kc]) for kc in range(8)], rb + [wb])
                            S.op("dve", lambda e, a=a, bb=bb, b0=b0: e.tensor_tensor(out=RT_t[0][:, 0:512], in0=PS_t[:, b0, 0:512], in1=COS[:, a:bb], op=ALU.mult), reads=[PS[b0], CONST, CB, RB], writes=[RT[0]])
                            S.op("dve", lambda e, a=a, bb=bb, b1=b1: e.tensor_tensor(out=RT_t[1][:, 0:512], in0=PS_t[:, b1, 0:512], in1=SIN[:, a:bb], op=ALU.mult), reads=[PS[b1], CONST, CB, RB], writes=[RT[1]])
                            if dst_fn is not None:
                                S.op("dve", lambda e, a=a, bb=bb: e.tensor_tensor(out=dst_fn(a, bb), in0=RT_t[0][:, 0:512], in1=RT_t[1][:, 0:512], op=ALU.add), reads=[RT[0], RT[1]], writes=dst_bufs_fn(a, bb))
                            else:
                                for kv_ in range(2):
                                    S.op("dve", lambda e, a=a, bb=bb, kv_=kv_: e.tensor_tensor(out=KT_t[64 * kv_:64 * kv_ + 64, kv_, a:bb], in0=RT_t[0][64 * kv_:64 * kv_ + 64, 0:512], in1=RT_t[1][64 * kv_:64 * kv_ + 64, 0:512], op=ALU.add), reads=[RT[0], RT[1]], writes=dst_bufs_fn(a, bb))
                        else:
                            if dst_fn is not None:
                                S.op("act", lambda e, a=a, bb=bb, n=n, b0=b0: e.activation(out=dst_fn(a, bb), in_=PS_t[:, b0, 0:n], func=AF.Copy), reads=[PS[b0]], writes=dst_bufs_fn(a, bb))
                            else:
                                for kv_ in range(2):
                                    S.op("act", lambda e, a=a, bb=bb, n=n, b0=b0, kv_=kv_: e.activation(out=KT_t[64 * kv_:64 * kv_ + 64, kv_, a:bb], in_=PS_t[64 * kv_:64 * kv_ + 64, b0, 0:n], func=AF.Copy), reads=[PS[b0]], writes=dst_bufs_fn(a, bb))
                for g in range(4):
                    wv, wb = load_w([wincd_d[:, g * 128:(g + 1) * 128], wincd_d[:, 512 + g * 128:512 + (g + 1) * 128]], 8)
                    rope_proj(wv, wb, 0, lambda a, bb, g=g: QT_t[:, g, a:bb], lambda a, bb, g=g: QT.R(g, a, bb), TILES_L)
                wv, wb = load_w([wincd_d[:, 1024:1152], wincd_d[:, 1152:1280], wincd_d[:, 1280:1408]], 8)
                rope_proj(wv, wb, 0, None, lambda a, bb: KT.RC(range(2), a, bb), TILES_ALL)
                for kcnk in range(18):
                    bank = psb()
                    a = kcnk * 128
                    mm_group(bank, 128, [(H_t[:, kc, a:a + 128], wv[:, kc, 256:384]) for kc in range(8)], H.RC(range(8), a, a + 128) + [wb])
                    S.op("act", lambda e, kcnk=kcnk, bank=bank: e.activation(out=VAB_t[:, 0, kcnk, 0:64], in_=PS_t[:, bank, 0:64], func=AF.Copy), reads=[PS[bank]], writes=[VAB])
                    S.op("dve", lambda e, kcnk=kcnk, bank=bank: e.tensor_copy(out=VAB_t[:, 1, kcnk, 64:128], in_=PS_t[:, bank, 64:128]), reads=[PS[bank]], writes=[VAB])
                for i in range(16):
                    qa, qb = i * 128, (i + 1) * 128
                    kl = [(16, None), (17, None)]
                    if i > 0:
                        kl.append((i - 1, 0))
                    kl.append((i, None))
                    if i < 15:
                        kl.append((i + 1, 1))
                    its = [(kv, kk, mk) for kv in range(2) for (kk, mk) in kl]
                    sbanks = {}

                    def emit_s(j):
                        kv, kk, mk = its[j]
                        bk = psb()
                        sbanks[j] = bk
                        mm_group(bk, 512, [(KT_t[:, kv, kk * 128:(kk + 1) * 128], QT_t[:, :, qa:qb])], KT.R(kv, kk * 128, (kk + 1) * 128) + QT.RC(range(4), qa, qb))
                    emit_s(0)
                    emit_s(1)
                    for j, (kv, kk, mk) in enumerate(its):
                        if j + 2 < len(its):
                            emit_s(j + 2)
                        bk = sbanks[j]
                        pk = j % 4
                        S.op("act", lambda e, pk=pk, bk=bk: e.activation(out=PT_t[pk][:], in_=PS_t[:, bk, :], func=AF.Exp, scale=0.125), reads=[PS[bk]], writes=[PT[pk]])
                        if mk is not None:
                            S.op("dve", lambda e, pk=pk, mk=mk: e.tensor_tensor(out=PT_t[pk][:], in0=PT_t[pk][:], in1=MASKS[:, mk, :], op=ALU.mult), reads=[PT[pk], CONST, CB, RB], writes=[PT[pk]])
                        first, last = (j == 0), (j == len(its) - 1)

                        ob = 4 + 2 * (i % 2)

                        def pv(e, kv=kv, kk=kk, pk=pk, first=first, last=last, ob=ob):
                            e.matmul(PS_t[:, ob, :], lhsT=VAB_t[:, kv, kk, :], rhs=PT_t[pk][:], start=first, stop=last)
                            return e.matmul(PS_t[:, ob + 1, :], lhsT=ONESAB[:, kv, :], rhs=PT_t[pk][:], start=first, stop=last)
                        if first:
                            S.op("pe", pv, reads=[PT[pk], VAB, CONST, CB, RB], writes=[PS[ob], PS[ob + 1]])
                        else:
                            S.op("pe", pv, reads=[PT[pk], VAB, CONST, CB, RB], acc=[PS[ob], PS[ob + 1]])
                    ob = 4 + 2 * (i % 2)
                    rk = i % 2
                    for g in range(4):
                        S.op("act", lambda e, g=g, ob=ob, rk=rk: e.activation(out=RT_t[rk][:, g * 128:(g + 1) * 128], in_=PS_t[:, ob + 1, g * 128:(g + 1) * 128], func=AF.Ln, bias=SINKE[:, g:g + 1]), reads=[PS[ob + 1], CONST, CB, RB], writes=[RT[rk]])
                    S.op("act", lambda e, rk=rk: e.activation(out=RT_t[rk][:], in_=RT_t[rk][:], func=AF.Exp, scale=-1.0), reads=[RT[rk]], writes=[RT[rk]])
                    S.op("dve", lambda e, qa=qa, qb=qb, ob=ob, rk=rk: e.tensor_tensor(out=YA1_t[:, :, qa:qb], in0=PS_t[:, ob, :].rearrange("p (g n) -> p g n", g=4), in1=RT_t[rk][:].rearrange("p (g n) -> p g n", g=4), op=ALU.mult), reads=[PS[ob], RT[rk]], writes=YA1.RC(range(4), qa, qb))

                def kfun(a, bb):
                    return ([YA1_t[:, g, a:bb] for g in range(4)] + [YD_t[:, g, a:bb] for g in range(4)],
                            YA1.RC(range(4), a, bb) + YD.RC(range(4), a, bb))
                wout_stage(l, b, woutcd_d, kfun, TILES_L)

        def final_stage(b):
            with contextlib.ExitStack() as st:
                S.barrier()
                SQ_t = sb("SQf", [128, 8, 512], BF16, st)
                SQ = Buf("SQf")
                RS_t = sb("RSf", [128, 512], F32, st)
                RS = Buf("RSf")
                YF_t = sb("YF", [128, 8, 512], F32, st)
                YF = [Buf("YF%d" % c) for c in range(8)]
                OS_t = [sb("OS%d" % i, [128, D], F32, st) for i in range(2)]
                OS = [Buf("OS%d" % i) for i in range(2)]
                for (a, bb) in TILES_L:
                    for c in range(8):
                        S.op("act", lambda e, c=c, a=a, bb=bb: e.activation(out=SQ_t[:, c, :], in_=XR_t[:, c, a:bb], func=AF.Square), reads=XR.R(c, a, bb), writes=[SQ])
                    bank = psb()
                    mm_group(bank, 512, [(ONES[:], SQ_t[:, c, :]) for c in range(8)], [SQ, CONST])
                    S.op("act", lambda e, bank=bank: e.activation(out=RS_t[:], in_=PS_t[:, bank, :], func=AF.Sqrt, bias=EPS, scale=1.0 / D), reads=[PS[bank]], writes=[RS])
                    S.op("dve", lambda e: e.reciprocal(out=RS_t[:], in_=RS_t[:]), reads=[RS], writes=[RS])
                    for c in range(8):
                        S.op("dve", lambda e, c=c, a=a, bb=bb: e.scalar_tensor_tensor(out=YF_t[:, c, :], in0=XR_t[:, c, a:bb], scalar=FNW[:, c:c + 1], in1=RS_t[:], op0=ALU.mult, op1=ALU.mult), reads=XR.R(c, a, bb) + [RS, CONST], writes=[YF[c]])
                    for s in range(4):
                        tok = a + s * 128
                        k = (tok // 128) % 2
                        for hh in range(2):
                            bank = psb()

                            def tp(e, hh=hh, bank=bank, s=s):
                                ins = None
                                for q in range(4):
                                    c = hh * 4 + q
                                    ins = e.transpose(out=PS_t[:, bank, q * 128:(q + 1) * 128], in_=YF_t[:, c, s * 128:(s + 1) * 128], identity=IDENT[:])
                                return ins
                            S.op("pe", tp, reads=[YF[hh * 4 + q] for q in range(4)] + [CONST], writes=[PS[bank]])
                            if hh == 0:
                                S.op("act", lambda e, k=k, bank=bank: e.activation(out=OS_t[k][:, 0:512], in_=PS_t[:, bank, :], func=AF.Copy), reads=[PS[bank]], writes=[OS[k]])
                            else:
                                S.op("dve", lambda e, k=k, bank=bank: e.tensor_copy(out=OS_t[k][:, 512:1024], in_=PS_t[:, bank, :]), reads=[PS[bank]], writes=[OS[k]])
                        S.dma("sp", out_d[b, tok:tok + 128, :], OS_t[k][:], OS[k], reads=[OS[k]])
                S.barrier()
                for k in range(2):
                    if OS[k].dsem is not None:
                        nc.sync.wait_ge(OS[k].dsem, OS[k].dcnt)

        def dump_dbg():
            S.barrier()
            DB = Buf("dbg")
            S.dma("sp", dbg_d, XR_t[:], DB, reads=[bb_ for v in XR.b.values() for bb_ in [v]])
            nc.sync.wait_ge(DB.dsem, DB.dcnt)

        stages = [("mixer0", lambda b: mixer_ab(0, b)), ("ffn0", lambda b: ffn_stage(0, b, True)),
                  ("mixer1", lambda b: mixer_cd(1, b)), ("ffn1", lambda b: ffn_stage(1, b, False))]
        for b in range(BPC):
            load_x(b)
            stop = False
            for name, fn in stages:
                fn(b)
                if b == 0 and name == "mixer0":
                    compute_mods(1)
                if DEBUG_STOP == name:
                    stop = True
                    break
            if stop:
                dump_dbg()
                break
            final_stage(b)
    return nc


def _fm(v):
    v = np.asarray(v, np.float32)
    n = v.shape[-1] // 128
    return np.ascontiguousarray(np.moveaxis(v.reshape(v.shape[:-1] + (n, 128)), -1, 0))


def _rope_tables():
    pos = np.arange(NL)
    row = (pos // 64).astype(np.float32)
    col = (pos % 64).astype(np.float32)
    inv = (1.0 / (np.float32(10000.0) ** (np.arange(0, 32, 2, dtype=np.float32) / np.float32(32)))).astype(np.float32)
    cos_t = np.zeros((128, NL), np.float32)
    sin_t = np.zeros((128, NL), np.float32)
    for p in range(128):
        d = p % 64
        ax = row if d < 32 else col
        dd = d % 32
        ang = (ax * inv[dd % 16]).astype(np.float32)
        cos_t[p] = np.cos(ang)
        sin_t[p] = -np.sin(ang) if dd < 16 else np.sin(ang)
    return cos_t, sin_t


def _partner_perm(ncols):
    idx = np.arange(ncols)
    d = idx % 64
    dd = d % 32
    return np.where(dd < 16, idx + 16, idx - 16)


_PROG = {}


def _prepare(inputs):
    f = lambda k: np.asarray(inputs[k], np.float32)
    x, c, ctx, c_ctx = f("x"), f("c"), f("ctx"), f("c_ctx")
    shared = {}
    shared["w_mod"] = f("w_mod")
    shared["b_mod"] = np.ascontiguousarray(_fm(f("b_mod")))
    w = f("w_in_ab")[0]
    q, k = w[:, 1536:2048], w[:, 2048:2560]
    pp = _partner_perm(512)
    shared["w_in_ab"] = np.ascontiguousarray(np.concatenate([w, q[:, pp], k[:, pp]], axis=1))
    shared["conv_a"] = np.ascontiguousarray(np.transpose(f("conv_a")[0].reshape(3, 4, 128), (2, 1, 0)))
    shared["lam_qk"] = np.ascontiguousarray(np.broadcast_to(f("lam_qk")[0][None], (128, 4, 64)))
    shared["subln"] = np.ascontiguousarray(f("subln_b")[0].reshape(128, 1))
    shared["w_out_ab"] = f("w_out_ab")[0]
    w = f("w_in_cd")[0]
    qh = w[:, 0:512].reshape(D, 8, 64)
    qg = np.concatenate([np.concatenate([qh[:, g], qh[:, 4 + g]], axis=1) for g in range(4)], axis=1)
    kk = w[:, 512:640]
    shared["w_in_cd"] = np.ascontiguousarray(np.concatenate([qg, qg[:, _partner_perm(512)], kk, kk[:, _partner_perm(128)], w[:, 640:768], w[:, 768:1280]], axis=1))
    sk = f("sink_c")[0]
    shared["sink"] = np.ascontiguousarray(np.concatenate([np.broadcast_to(sk[None, 0:4], (64, 4)), np.broadcast_to(sk[None, 4:8], (64, 4))], axis=0))
    shared["pool_w"] = np.ascontiguousarray(np.transpose(f("pool_w")[0], (1, 0, 2)))
    shared["pool_scale"] = np.ascontiguousarray(f("pool_scale")[0].reshape(4, 128).T)
    wo = f("w_out_cd")[0]
    rows = []
    for g in range(4):
        rows += list(range(g * 64, g * 64 + 64)) + list(range((4 + g) * 64, (4 + g) * 64 + 64))
    rows += list(range(512, 1024))
    shared["w_out_cd"] = np.ascontiguousarray(wo[np.array(rows)])
    shared["w_up"] = f("w_up")
    shared["cfw"] = np.ascontiguousarray(np.transpose(f("conv_ffn_w").reshape(2, 3, 44, 128), (3, 0, 2, 1)))
    shared["cfb"] = np.ascontiguousarray(np.transpose(f("conv_ffn_b").reshape(2, 44, 128), (2, 0, 1)))
    shared["w_down"] = f("w_down")
    shared["fnw"] = np.ascontiguousarray(f("final_norm_w").reshape(8, 128).T)
    shared["ident"] = np.eye(128, dtype=np.float32)
    shared["cos_t"], shared["sin_t"] = _rope_tables()
    kk_, qq_ = np.meshgrid(np.arange(128), np.arange(128), indexing="ij")
    m_prev = (kk_ >= qq_).astype(np.float32)
    m_next = (kk_ <= qq_).astype(np.float32)
    shared["masks"] = np.ascontiguousarray(np.stack([np.tile(m_prev, (1, 4)), np.tile(m_next, (1, 4))], axis=1))
    ic = np.zeros((4, 16), np.float32)
    for g, wdt in enumerate((2, 4, 8, 16)):
        for j in range(8):
            t = j
            ic[g, j] = 1.0 / (min(t + wdt // 2, NL) - max(t - wdt // 2, 0))
            t = NL - 8 + j
            ic[g, 8 + j] = 1.0 / (min(t + wdt // 2, NL) - max(t - wdt // 2, 0))
    shared["icnt"] = np.ascontiguousarray(np.broadcast_to(ic[None], (128, 4, 16)))
    oab = np.zeros((128, 2, 128), np.float32)
    oab[:, 0, 0:64] = 1.0
    oab[:, 1, 64:128] = 1.0
    shared["onesab"] = oab
    in_maps = []
    for i in range(NCORES):
        m = dict(shared)
        m["x"] = np.ascontiguousarray(x[BPC * i:BPC * (i + 1)])
        m["ctx"] = np.ascontiguousarray(ctx[BPC * i:BPC * (i + 1)])
        cv = np.stack([c[BPC * i], c[BPC * i + 1], c_ctx], axis=0)
        m["cT"] = np.ascontiguousarray(np.transpose(cv.reshape(3, 8, 128), (2, 1, 0)))
        in_maps.append(m)
    return in_maps


def kernel(**inputs):
    if "nc" not in _PROG:
        _PROG["nc"] = build_program()
    in_maps = _prepare(inputs)
    res = run_bass_kernel_spmd(_PROG["nc"], in_maps, core_ids=list(range(NCORES)))
    return np.concatenate([r["out"] for r in res.results], axis=0).astype(np.float32)
```

```python
import contextlib
import math
import numpy as np
import concourse.bass as bass
import concourse.mybir as mybir
from concourse.bass_utils import run_bass_kernel_spmd

F32 = mybir.dt.float32
BF16 = mybir.dt.bfloat16
AF = mybir.ActivationFunctionType
ALU = mybir.AluOpType
AX = mybir.AxisListType

D = 1024
NL = 2048
NCX = 256
NT = NL + NCX
DFF = 2816
EPS = 1e-6
NCORES = 8
BPC = 2
SEG = 256

DEBUG_STOP = None


class Eng:
    def __init__(self, name, e, sem, sid):
        self.name, self.e, self.sem, self.sid, self.cnt, self.seen = name, e, sem, sid, 0, {}


class Buf:
    __slots__ = ("name", "w", "r", "dsem", "dsid", "dcnt")

    def __init__(self, name):
        self.name, self.w, self.r = name, {}, {}
        self.dsem = None
        self.dcnt = 0


def _merge(dst, src):
    for k, v in src.items():
        if dst.get(k, 0) < v:
            dst[k] = v


class Sched:
    def __init__(self, nc, es):
        self.nc, self.es = nc, es
        self.sems = {}
        self.nsem = 0
        self.eng = {}
        for name, e in (("pe", nc.tensor), ("act", nc.scalar), ("dve", nc.vector), ("pool", nc.gpsimd), ("sp", nc.sync)):
            sem, sid = self.new_sem("e_" + name)
            self.eng[name] = Eng(name, e, sem, sid)

    def new_sem(self, name):
        sem = self.es.enter_context(self.nc.semaphore("%s_%d" % (name, self.nsem)))
        sid = self.nsem
        self.nsem += 1
        self.sems[sid] = sem
        return sem, sid

    def _need(self, reads, writes, acc, own_sid):
        need = {}
        for b in reads:
            _merge(need, b.w)
        for b in writes:
            if b.r:
                _merge(need, b.r)
            else:
                _merge(need, b.w)
        for b in acc:
            if b.r:
                _merge(need, b.r)
            else:
                for k, v in b.w.items():
                    if k != own_sid and need.get(k, 0) < v:
                        need[k] = v
        return need

    def _wait(self, E, need):
        for sid, val in need.items():
            if E.seen.get(sid, 0) < val:
                E.e.wait_ge(self.sems[sid], val)
                E.seen[sid] = val

    def op(self, en, fn, reads=(), writes=(), acc=()):
        E = self.eng[en]
        self._wait(E, self._need(reads, writes, acc, E.sid))
        ins = fn(E.e)
        E.cnt += 1
        ins.then_inc(E.sem, 1)
        for b in reads:
            if b.r.get(E.sid, 0) < E.cnt:
                b.r[E.sid] = E.cnt
        for b in writes:
            b.w = {E.sid: E.cnt}
            b.r = {}
        for b in acc:
            b.w = {E.sid: E.cnt}
            b.r = {}

    def dma(self, qn, out_ap, in_ap, owner, reads=(), writes=(), nowait=False):
        E = self.eng[qn]
        if owner.dsem is None:
            owner.dsem, owner.dsid = self.new_sem("d_" + owner.name)
        if not nowait:
            self._wait(E, self._need(reads, writes, (), -1))
        pairs = out_ap if isinstance(out_ap, list) else [(out_ap, in_ap)]
        for (o_, i_) in pairs:
            ins = E.e.dma_start(out=o_, in_=i_)
            owner.dcnt += 16
            ins.then_inc(owner.dsem, 16)
        for b in reads:
            if b.r.get(owner.dsid, 0) < owner.dcnt:
                b.r[owner.dsid] = owner.dcnt
        for b in writes:
            b.w = {owner.dsid: owner.dcnt}
            b.r = {}

    def sync_to(self, qn, names=("pe", "act", "dve")):
        self._wait(self.eng[qn], {self.eng[n].sid: self.eng[n].cnt for n in names if self.eng[n].cnt > 0})

    def barrier(self, names=("pe", "act", "dve", "sp")):
        clock = {self.eng[n].sid: self.eng[n].cnt for n in names if self.eng[n].cnt > 0}
        for n in names:
            E = self.eng[n]
            self._wait(E, {k: v for k, v in clock.items() if k != E.sid})


class Grid:
    def __init__(self, name, t):
        self.name, self.t, self.b = name, t, {}

    def R(self, c, a, b):
        out = []
        for k in range(a // SEG, (b - 1) // SEG + 1):
            key = (c, k)
            if key not in self.b:
                self.b[key] = Buf("%s_%s_%d" % (self.name, c, k))
            out.append(self.b[key])
        return out

    def RC(self, cs, a, b):
        out = []
        for c in cs:
            out += self.R(c, a, b)
        return out


def pc(t):
    return 1 + t if t < NL else 3 + t


def lam_init(layer):
    return 0.8 - 0.6 * math.exp(-0.3 * layer)


def build_program():
    nc = bass.Bass("TRN2", target_bir_lowering=False)
    di = {}

    def din(name, shape, dt=F32):
        di[name] = nc.dram_tensor(name, list(shape), dt, kind="ExternalInput").ap()
        return di[name]

    x_d = din("x", [BPC, NL, D])
    ctx_d = din("ctx", [BPC, NCX, D])
    cT_d = din("cT", [128, 8, 3])
    wmod_d = din("w_mod", [2, D, 6 * D])
    bmod_d = din("b_mod", [128, 2, 48])
    winab_d = din("w_in_ab", [D, 4096])
    conva_d = din("conv_a", [128, 4, 3])
    lamqk_d = din("lam_qk", [128, 4, 64])
    subln_d = din("subln", [128, 1])
    woutab_d = din("w_out_ab", [D, D])
    wincd_d = din("w_in_cd", [D, 1920])
    sink_d = din("sink", [128, 4])
    poolw_d = din("pool_w", [128, 4, 128])
    pscale_d = din("pool_scale", [128, 4])
    woutcd_d = din("w_out_cd", [D, D])
    wup_d = din("w_up", [2, D, 2 * DFF])
    cfw_d = din("cfw", [128, 2, 44, 3])
    cfb_d = din("cfb", [128, 2, 44])
    wdown_d = din("w_down", [2, DFF, D])
    fnw_d = din("fnw", [128, 8])
    ident_d = din("ident", [128, 128])
    cos_d = din("cos_t", [128, NL])
    sin_d = din("sin_t", [128, NL])
    mask_d = din("masks", [128, 2, 512])
    icnt_d = din("icnt", [128, 4, 16])
    onesab_d = din("onesab", [128, 2, 128])
    out_d = nc.dram_tensor("out", [BPC, NL, D], F32, kind="ExternalOutput").ap()
    dbg_d = None
    if DEBUG_STOP is not None:
        dbg_d = nc.dram_tensor("dbg", [128, 8, NT], F32, kind="ExternalOutput").ap()

    with contextlib.ExitStack() as es:
        S = Sched(nc, es)

        uniq = [0]

        def sb(name, shape, dt, stack=es):
            uniq[0] += 1
            return stack.enter_context(nc.sbuf_tensor("%s_%d" % (name, uniq[0]), list(shape), dt))

        XR_t = sb("XR", [128, 8, NT], F32)
        H_t = sb("H", [128, 8, NT], BF16)
        XR = Grid("XR", XR_t)
        H = Grid("H", H_t)
        PS_t = es.enter_context(nc.psum_tensor("PS", [128, 8, 512], F32))
        PS = [Buf("ps%d" % i) for i in range(8)]
        IDENT = sb("IDENT", [128, 128], F32)
        ONES = sb("ONES", [128, 128], BF16)
        ONESF = sb("ONESF", [128, 128], F32)
        MODS = sb("MODS", [128, 2, 6, 8, 3], F32)
        CT = sb("CT", [128, 8, 3], F32)
        SC = sb("SC", [128, 8, 3], F32)
        SCB = sb("SCB", [128, 8, 3], BF16)
        BM = sb("BM", [128, 2, 48], F32)
        CONVA = sb("CONVA", [128, 4, 3], F32)
        LS = sb("LS", [128, 4], F32)
        SUBW = sb("SUBW", [128, 1], F32)
        SINKE = sb("SINKE", [128, 4], F32)
        PSCALE = sb("PSCALE", [128, 4], F32)
        CFW = sb("CFW", [128, 2, 44, 3], F32)
        CFB = sb("CFB", [128, 2, 44], F32)
        FNW = sb("FNW", [128, 8], F32)
        CONST = Buf("const")
        NSLOT = 4
        SLOT_t = [sb("WS%d" % i, [128, 3072], BF16) for i in range(NSLOT)]
        SLOT = [Buf("ws%d" % i) for i in range(NSLOT)]
        slot_rr = [0]
        ps_rr = [0]

        def psb(pool=(0, 1, 2, 3)):
            i = pool[ps_rr[0] % len(pool)]
            ps_rr[0] += 1
            return i

        def cload(dst_ap, src_ap, q="sp"):
            S.dma(q, dst_ap, src_ap, CONST, writes=[CONST], nowait=True)

        cload(IDENT[:], ident_d)
        cload(CT[:], cT_d)
        cload(BM[:], bmod_d)
        cload(CONVA[:], conva_d)
        cload(SUBW[:], subln_d)
        cload(SINKE[:], sink_d)
        cload(PSCALE[:], pscale_d)
        cload(CFW[:], cfw_d)
        cload(CFB[:], cfb_d)
        cload(FNW[:], fnw_d)
        S.op("dve", lambda e: e.memset(ONES[:], 1.0), writes=[CONST])
        S.op("dve", lambda e: e.memset(ONESF[:], 1.0), writes=[CONST])

        def load_w(srcs, kc):
            i = slot_rr[0] % NSLOT
            slot_rr[0] += 1
            tot = sum(s.shape[1] for s in srcs)
            assert kc * tot <= 3072
            view = SLOT_t[i][:, 0:kc * tot].rearrange("p (k n) -> p k n", k=kc)
            off = 0
            pairs = []
            for s in srcs:
                n = s.shape[1]
                pairs.append((view[:, :, off:off + n], s.rearrange("(k p) n -> p k n", p=128)))
                off += n
            S.dma("pool", pairs, None, SLOT[i], writes=[SLOT[i]])
            return view, SLOT[i]

        def mm_group(bank, n, pairs, reads, m=128):
            def f(e):
                ins = None
                last = len(pairs) - 1
                for i, (lt, rh) in enumerate(pairs):
                    ins = e.matmul(PS_t[0:m, bank, 0:n], lhsT=lt, rhs=rh, start=(i == 0), stop=(i == last))
                return ins
            S.op("pe", f, reads=reads, writes=[PS[bank]])

        S.op("act", lambda e: e.activation(out=SC[:], in_=CT[:], func=AF.Silu), reads=[CONST], writes=[CONST])
        S.op("act", lambda e: e.activation(out=SCB[:], in_=CT[:], func=AF.Silu), reads=[CONST], writes=[CONST])

        def compute_mods(l):
            with contextlib.ExitStack() as st:
                S.barrier()
                MR_t = sb("MR", [3, 6144], F32, st)
                MR = Buf("MR")
                for blk in range(16):
                    wv, wb = load_w([wmod_d[l][:, blk * 384:(blk + 1) * 384]], 8)
                    bank = psb()
                    mm_group(bank, 384, [(SCB[:, kc, :], wv[:, kc, :]) for kc in range(8)], [wb, CONST], m=3)
                    S.op("act", lambda e, blk=blk, bank=bank: e.activation(out=MR_t[0:3, blk * 384:(blk + 1) * 384], in_=PS_t[0:3, bank, 0:384], func=AF.Copy), reads=[PS[bank]], writes=[MR])
                bank = psb()

                def tp(e, bank=bank):
                    ins = None
                    for j in range(48):
                        ins = e.transpose(out=PS_t[:, bank, j * 3:j * 3 + 3], in_=MR_t[0:3, j * 128:(j + 1) * 128], identity=IDENT[0:3, 0:3])
                    return ins
                S.op("pe", tp, reads=[MR, CONST], writes=[PS[bank]])
                for r in range(3):
                    S.op("dve", lambda e, r=r, l=l, bank=bank: e.tensor_tensor(
                        out=MODS[:, l].rearrange("p m c r -> p (m c) r")[:, :, r], in0=PS_t[:, bank, 0:144].rearrange("p (j r) -> p j r", r=3)[:, :, r], in1=BM[:, l, :], op=ALU.add),
                        reads=[PS[bank], CONST], writes=[CONST])
                for m in (1, 4):
                    S.op("dve", lambda e, l=l, m=m: e.tensor_scalar_add(out=MODS[:, l, m], in0=MODS[:, l, m], scalar1=1.0), reads=[CONST], writes=[CONST])
                S.barrier()
        compute_mods(0)
        with contextlib.ExitStack() as st:
            LQ = sb("LQ", [128, 4, 64], F32, st)
            LTMP = sb("LTMP", [128, 2, 64], F32, st)
            cload(LQ[:], lamqk_d)
            for q_ in range(2):
                S.op("dve", lambda e, q_=q_: e.tensor_tensor(out=LTMP[:, q_, :], in0=LQ[:, 2 * q_, :], in1=LQ[:, 2 * q_ + 1, :], op=ALU.mult), reads=[CONST], writes=[CONST])
            S.op("dve", lambda e: e.reduce_sum(out=LS[:, 0:2], in_=LTMP[:], axis=AX.X), reads=[CONST], writes=[CONST])
            S.op("act", lambda e: e.activation(out=LS[:, 0:2], in_=LS[:, 0:2], func=AF.Exp), reads=[CONST], writes=[CONST])
            S.op("dve", lambda e: e.tensor_tensor(out=LS[:, 2:3], in0=LS[:, 1:2], in1=LS[:, 0:1], op=ALU.subtract), reads=[CONST], writes=[CONST])
            S.op("dve", lambda e: e.tensor_scalar_add(out=LS[:, 2:3], in0=LS[:, 2:3], scalar1=-lam_init(0)), reads=[CONST], writes=[CONST])
            S.barrier()

        ddc = [0]

        def dd(name, src_ap, shape, dt):
            if DEBUG_STOP is None:
                return
            ddc[0] += 1
            dst = nc.dram_tensor(name, list(shape), dt, kind="ExternalOutput").ap()
            S.barrier()
            DB = Buf("dd%d" % ddc[0])
            S.dma("sp", dst, src_ap, DB, nowait=True)
            nc.sync.wait_ge(DB.dsem, DB.dcnt)
            for n_ in ("pe", "act", "dve"):
                S._wait(S.eng[n_], {DB.dsid: DB.dcnt})

        def mod(l, m, c, r):
            return MODS[:, l, m, c, r:r + 1]

        NEGLAM = LS[:, 2:3]
        S.op("dve", lambda e: e.tensor_scalar_mul(out=SUBW[:], in0=SUBW[:], scalar1=1.0 - lam_init(0)), reads=[CONST], writes=[CONST])
        S.op("act", lambda e: e.activation(out=SINKE[:], in_=SINKE[:], func=AF.Exp), reads=[CONST], writes=[CONST])

        def load_rope(st):
            COS = sb("COS", [128, NL], BF16, st)
            SIN = sb("SIN", [128, NL], BF16, st)
            RB = Buf("rope")
            S.sync_to("pool")
            S.dma("pool", [(COS[:], cos_d), (SIN[:], sin_d)], None, RB, writes=[RB])
            return COS, SIN, RB

        TILES_L = [(i * 512, (i + 1) * 512) for i in range(4)]
        TILES_ALL = TILES_L + [(NL, NT)]

        def load_x(b):
          with contextlib.ExitStack() as st:
            S.barrier()
            XS_t = [sb("XS%d" % i, [128, D], F32, st) for i in range(2)]
            XS = [Buf("xs%d" % i) for i in range(2)]
            for j in range(NT // 128):
                k = j % 2
                src = x_d[b, j * 128:(j + 1) * 128, :] if j < 16 else ctx_d[b, (j - 16) * 128:(j - 15) * 128, :]
                S.dma("sp", XS_t[k][:], src, XS[k], writes=[XS[k]])
                for hh in range(2):
                    bank = psb()

                    def tp(e, k=k, hh=hh, bank=bank):
                        ins = None
                        for q in range(4):
                            c = hh * 4 + q
                            ins = e.transpose(out=PS_t[:, bank, q * 128:(q + 1) * 128], in_=XS_t[k][:, c * 128:(c + 1) * 128], identity=IDENT[:])
                        return ins
                    S.op("pe", tp, reads=[XS[k], CONST], writes=[PS[bank]])
                    en = "act" if hh == 0 else "dve"
                    dst = XR_t[:, hh * 4:hh * 4 + 4, j * 128:(j + 1) * 128]
                    srcp = PS_t[:, bank, :].rearrange("p (q n) -> p q n", q=4)
                    wr = XR.RC(range(hh * 4, hh * 4 + 4), j * 128, (j + 1) * 128)
                    if en == "act":
                        S.op("act", lambda e, dst=dst, srcp=srcp: e.activation(out=dst, in_=srcp, func=AF.Copy), reads=[PS[bank]], writes=wr)
                    else:
                        S.op("dve", lambda e, dst=dst, srcp=srcp: e.tensor_copy(out=dst, in_=srcp), reads=[PS[bank]], writes=wr)

        def norm_stage(l, b, m_shift, m_scale, tiles, st):
            SQ_t = sb("SQ", [128, 8, 512], BF16, st)
            SQ = Buf("SQ")
            RS_t = [sb("RS%d" % i, [128, 512], F32, st) for i in range(2)]
            RS = [Buf("RS%d" % i) for i in range(2)]
            TM_t = [sb("TM%d" % i, [128, 512], F32, st) for i in range(2)]
            TM = [Buf("TM%d" % i) for i in range(2)]
            SQB = [Buf("SQ%d" % c) for c in range(8)]

            def sq(ti, a, bb, c):
                n = bb - a
                if c < 4:
                    S.op("act", lambda e: e.activation(out=SQ_t[:, c, 0:n], in_=XR_t[:, c, a:bb], func=AF.Square), reads=XR.R(c, a, bb), writes=[SQB[c]])
                else:
                    S.op("dve", lambda e: e.tensor_tensor(out=SQ_t[:, c, 0:n], in0=XR_t[:, c, a:bb], in1=XR_t[:, c, a:bb], op=ALU.mult), reads=XR.R(c, a, bb), writes=[SQB[c]])

            def stat(ti, a, bb):
                n = bb - a
                bank = psb()
                mm_group(bank, n, [(ONES[:], SQ_t[:, c, 0:n]) for c in range(8)], SQB + [CONST])
                k = ti % 2
                S.op("act", lambda e: e.activation(out=RS_t[k][:, 0:n], in_=PS_t[:, bank, 0:n], func=AF.Ln, bias=EPS, scale=1.0 / D), reads=[PS[bank]], writes=[RS[k]])
                S.op("act", lambda e: e.activation(out=RS_t[k][:, 0:n], in_=RS_t[k][:, 0:n], func=AF.Exp, scale=-0.5), reads=[RS[k]], writes=[RS[k]])

            def modl(ti, a, bb, c):
                n = bb - a
                r = b if a < NL else 2
                k = ti % 2
                kk = c % 2
                S.op("dve", lambda e: e.tensor_tensor(out=TM_t[kk][:, 0:n], in0=XR_t[:, c, a:bb], in1=RS_t[k][:, 0:n], op=ALU.mult), reads=XR.R(c, a, bb) + [RS[k]], writes=[TM[kk]])
                S.op("act", lambda e: e.activation(out=H_t[:, c, a:bb], in_=TM_t[kk][:, 0:n], func=AF.Identity, scale=mod(l, m_scale, c, r), bias=mod(l, m_shift, c, r)), reads=[TM[kk], CONST], writes=H.R(c, a, bb))
            for c in range(8):
                sq(0, *tiles[0], c)
            stat(0, *tiles[0])
            for ti in range(len(tiles)):
                nxt = ti + 1 < len(tiles)
                for c in range(8):
                    if nxt:
                        sq(ti + 1, *tiles[ti + 1], c)
                    modl(ti, *tiles[ti], c)
                if nxt:
                    stat(ti + 1, *tiles[ti + 1])

        def ktile_h(a, bb):
            return [H_t[:, k, a:bb] for k in range(8)], H.RC(range(8), a, bb)

        def ffn_stage(l, b, with_ctx):
            with contextlib.ExitStack() as st:
                S.barrier()
                tiles = TILES_ALL if with_ctx else TILES_L
                with contextlib.ExitStack() as st2:
                    norm_stage(l, b, 3, 4, tiles, st2)
                S.barrier()
                ACTS_t = sb("ACTS", [128, 22, 1024], BF16, st)
                ACTS = [Buf("acts%d" % i) for i in range(22)]
                U_t = [[sb("U%d%d" % (p, i), [128, 1028], BF16, st) for i in range(2)] for p in range(2)]
                U = [[Buf("U%d%d" % (p, i)) for i in range(2)] for p in range(2)]
                TT_t = [[sb("TT%d%d" % (p, i), [128, 1024], BF16, st) for i in range(2)] for p in range(2)]
                TT = [[Buf("TT%d%d" % (p, i)) for i in range(2)] for p in range(2)]
                wins = [(0, 1024, 0, NL), (1024, 2048, 0, NL)] + ([(NL, NT, NL, NT)] if with_ctx else [])
                for (a, bb, s0, s1) in wins:
                    W = bb - a
                    r = b if a < NL else 2
                    ua, ub = max(a - 1, s0), min(bb + 1, s1)
                    ncol = ub - ua
                    nt = (ncol + 511) // 512
                    base = (ncol + nt - 1) // nt
                    ctiles = [(ua + i * base, min(ua + (i + 1) * base, ub)) for i in range(nt)]
                    pend = [None]
                    for p_ in range(2):
                        for k_ in range(2):
                            if ua == a:
                                S.op("dve", lambda e, p_=p_, k_=k_: e.memset(U_t[p_][k_][:, 0:1], 0.0), writes=[U[p_][k_]])
                            if ub == bb:
                                S.op("dve", lambda e, p_=p_, k_=k_, W=W: e.memset(U_t[p_][k_][:, W + 1:W + 2], 0.0), writes=[U[p_][k_]])
                    for i in range(22):
                        wv, wb = load_w([wup_d[l][:, i * 128:(i + 1) * 128], wup_d[l][:, DFF + i * 128:DFF + (i + 1) * 128]], 8)
                        k = i % 2
                        for p in range(2):
                            ch = p * 22 + i
                            Ut, Ub = U_t[p][k], U[p][k]
                            for (ca, cb) in ctiles:
                                n = cb - ca
                                bank = psb()
                                rh, rb = ktile_h(ca, cb)
                                mm_group(bank, n, [(wv[:, kc, p * 128:(p + 1) * 128], rh[kc]) for kc in range(8)], rb + [wb])
                                o = 1 + ca - a
                                S.op("act", lambda e, Ut=Ut, o=o, n=n, bank=bank: e.activation(out=Ut[:, o:o + n], in_=PS_t[:, bank, 0:n], func=AF.Copy), reads=[PS[bank]], writes=[Ub])
                            Tt, Tb = TT_t[p][k], TT[p][k]
                            S.op("dve", lambda e, Tt=Tt, Ut=Ut, W=W, ch=ch: e.tensor_scalar(out=Tt[:, 0:W], in0=Ut[:, 1:W + 1], scalar1=CFW[:, l, ch, 1:2], scalar2=CFB[:, l, ch:ch + 1], op0=ALU.mult, op1=ALU.add), reads=[Ub, CONST], writes=[Tb])
                            S.op("dve", lambda e, Tt=Tt, Ut=Ut, W=W, ch=ch: e.scalar_tensor_tensor(out=Tt[:, 0:W], in0=Ut[:, 0:W], scalar=CFW[:, l, ch, 0:1], in1=Tt[:, 0:W], op0=ALU.mult, op1=ALU.add), reads=[Ub, Tb, CONST], writes=[Tb])
                            S.op("dve", lambda e, Tt=Tt, Ut=Ut, W=W, ch=ch: e.scalar_tensor_tensor(out=Tt[:, 0:W], in0=Ut[:, 2:W + 2], scalar=CFW[:, l, ch, 2:3], in1=Tt[:, 0:W], op0=ALU.mult, op1=ALU.add), reads=[Ub, Tb, CONST], writes=[Tb])
                        def fin(i=i, k=k, W=W):
                            S.op("act", lambda e: e.activation(out=TT_t[1][k][:, 0:W], in_=TT_t[1][k][:, 0:W], func=AF.Silu), reads=[TT[1][k]], writes=[TT[1][k]])
                            S.op("dve", lambda e: e.tensor_tensor(out=ACTS_t[:, i, 0:W], in0=TT_t[1][k][:, 0:W], in1=TT_t[0][k][:, 0:W], op=ALU.mult), reads=[TT[0][k], TT[1][k]], writes=[ACTS[i]])
                        if pend[0] is not None:
                            pend[0]()
                        pend[0] = fin
                    pend[0]()
                    pend[0] = None
                    otiles = [(a + i * 512, min(a + (i + 1) * 512, bb)) for i in range((W + 511) // 512)]
                    groups = [(oc, ca, cb) for oc in range(8) for (ca, cb) in otiles]
                    wts = {}
                    KS = 20
                    for gi in range(0, len(groups), 4):
                        batch = []
                        for (oc, ca, cb) in groups[gi:gi + 4]:
                            if oc not in wts:
                                wts[oc] = load_w([wdown_d[l][:, oc * 128:(oc + 1) * 128]], 22)
                            wv, wb = wts[oc]
                            n = cb - ca
                            bank = psb()

                            def fa(e, wv=wv, ca=ca, cb=cb, n=n, bank=bank):
                                ins = None
                                for kc in range(KS):
                                    ins = e.matmul(PS_t[:, bank, 0:n], lhsT=wv[:, kc, :], rhs=ACTS_t[:, kc, ca - a:cb - a], start=(kc == 0), stop=False)
                                return ins
                            S.op("pe", fa, reads=ACTS[0:KS] + [wb], writes=[PS[bank]])
                            batch.append((oc, ca, cb, n, bank, wv, wb))
                        for (oc, ca, cb, n, bank, wv, wb) in batch:
                            def fb(e, wv=wv, ca=ca, cb=cb, n=n, bank=bank):
                                ins = None
                                for kc in range(KS, 22):
                                    ins = e.matmul(PS_t[:, bank, 0:n], lhsT=wv[:, kc, :], rhs=ACTS_t[:, kc, ca - a:cb - a], start=False, stop=(kc == 21))
                                return ins
                            S.op("pe", fb, reads=ACTS[KS:22] + [wb], acc=[PS[bank]])
                            S.op("dve", lambda e, oc=oc, ca=ca, cb=cb, n=n, bank=bank, r=r: e.scalar_tensor_tensor(out=XR_t[:, oc, ca:cb], in0=PS_t[:, bank, 0:n], scalar=mod(l, 5, oc, r), in1=XR_t[:, oc, ca:cb], op0=ALU.mult, op1=ALU.add), reads=[PS[bank], CONST] + XR.R(oc, ca, cb), writes=XR.R(oc, ca, cb))

        def wout_stage(l, b, w_d, kfun, tiles):
            for oc in range(8):
                wv, wb = load_w([w_d[:, oc * 128:(oc + 1) * 128]], 8)
                for (a, bb) in tiles:
                    n = bb - a
                    r = b if a < NL else 2
                    bank = psb()
                    rh, rb = kfun(a, bb)
                    mm_group(bank, n, [(wv[:, kc, :], rh[kc]) for kc in range(8)], rb + [wb])
                    S.op("dve", lambda e, oc=oc, a=a, bb=bb, n=n, bank=bank, r=r: e.scalar_tensor_tensor(out=XR_t[:, oc, a:bb], in0=PS_t[:, bank, 0:n], scalar=mod(l, 2, oc, r), in1=XR_t[:, oc, a:bb], op0=ALU.mult, op1=ALU.add), reads=[PS[bank], CONST] + XR.R(oc, a, bb), writes=XR.R(oc, a, bb))

        def mixer_ab(l, b):
            with contextlib.ExitStack() as st:
                S.barrier()
                with contextlib.ExitStack() as st2:
                    norm_stage(l, b, 0, 1, TILES_ALL, st2)
                S.barrier()
                COS, SIN, RB = load_rope(st)
                YAC_t = sb("YAC", [128, 4, 2312], BF16, st)
                YAC = [Buf("yac%d" % c) for c in range(4)]
                with contextlib.ExitStack() as st2:
                    PP2_t = [sb("PP%d" % i, [128, 2312], BF16, st2) for i in range(2)]
                    PP2 = [Buf("PP%d" % i) for i in range(2)]
                    GBB2_t = [sb("GBB%d" % i, [128, 2312], BF16, st2) for i in range(2)]
                    GBB2 = [Buf("GBB%d" % i) for i in range(2)]
                    T_t = sb("T", [128, 2306], F32, st2)
                    T = Buf("T")
                    GCS_t = [sb("GCS%d" % i, [128, 512], BF16, st2) for i in range(2)]
                    GCS = [Buf("GCS%d" % i) for i in range(2)]
                    for i_ in range(2):
                        S.op("dve", lambda e, i_=i_: e.memset(PP2_t[i_][:], 0.0), writes=[PP2[i_]])
                        S.op("dve", lambda e, i_=i_: e.memset(GBB2_t[i_][:], 0.0), writes=[GBB2[i_]])
                    for c in range(4):
                        PP_t, PP, GBB_t, GBB = PP2_t[c % 2], PP2[c % 2], GBB2_t[c % 2], GBB2[c % 2]
                        wv, wb = load_w([winab_d[:, g * 512 + c * 128:g * 512 + (c + 1) * 128] for g in range(3)], 8)
                        for ti, (a, bb) in enumerate(TILES_ALL):
                            n = bb - a
                            rh, rb = ktile_h(a, bb)
                            banks = [psb() for _ in range(3)]
                            for g in range(3):
                                mm_group(banks[g], n, [(wv[:, kc, g * 128:(g + 1) * 128], rh[kc]) for kc in range(8)], rb + [wb])
                            k = ti % 2
                            o = pc(a)
                            S.op("act", lambda e, k=k, n=n, bk=banks[1]: e.activation(out=GCS_t[k][:, 0:n], in_=PS_t[:, bk, 0:n], func=AF.Copy), reads=[PS[banks[1]]], writes=[GCS[k]])
                            S.op("dve", lambda e, k=k, n=n, o=o, bk=banks[2]: e.tensor_tensor(out=PP_t[:, o:o + n], in0=PS_t[:, bk, 0:n], in1=GCS_t[k][:, 0:n], op=ALU.mult), reads=[PS[banks[2]], GCS[k]], writes=[PP])
                            S.op("act", lambda e, n=n, o=o, bk=banks[0]: e.activation(out=GBB_t[:, o:o + n], in_=PS_t[:, bk, 0:n], func=AF.Copy), reads=[PS[banks[0]]], writes=[GBB])
                        S.op("dve", lambda e, c=c: e.tensor_scalar(out=T_t[:], in0=PP_t[:, 1:2307], scalar1=CONVA[:, c, 1:2], scalar2=None, op0=ALU.mult), reads=[PP, CONST], writes=[T])
                        S.op("dve", lambda e, c=c: e.scalar_tensor_tensor(out=T_t[:], in0=PP_t[:, 0:2306], scalar=CONVA[:, c, 0:1], in1=T_t[:], op0=ALU.mult, op1=ALU.add), reads=[PP, T, CONST], writes=[T])
                        S.op("dve", lambda e, c=c: e.scalar_tensor_tensor(out=T_t[:], in0=PP_t[:, 2:2308], scalar=CONVA[:, c, 2:3], in1=T_t[:], op0=ALU.mult, op1=ALU.add), reads=[PP, T, CONST], writes=[T])
                        S.op("dve", lambda e, c=c: e.tensor_tensor(out=YAC_t[:, c, 1:2307], in0=T_t[:], in1=GBB_t[:, 1:2307], op=ALU.mult), reads=[T, GBB], writes=[YAC[c]])
                    S.barrier()
                YB_t = sb("YB", [128, 4, NT], BF16, st)
                YB = Grid("YB", YB_t)
                QC_t = sb("QC", [128, NT], BF16, st)
                KC_t = sb("KC", [128, NT], BF16, st)
                VC_t = sb("VC", [128, 18, 128], BF16, st)
                QC, KC, VC = Grid("QC", QC_t), Grid("KC", KC_t), Buf("VC")
                PT_t = [sb("PT%d" % i, [128, 2, 512], BF16, st) for i in range(3)]
                PT = [Buf("PT%d" % i) for i in range(3)]
                RT_t = [sb("RT0", [128, 512], BF16, st), None]
                RT = [Buf("RT0"), None]
                ZA_t = [sb("ZA%d" % i, [128, 512], BF16, st) for i in range(2)]
                ZA = [Buf("ZA%d" % i) for i in range(2)]
                RT_t[1] = sb("RT1", [128, 512], BF16, st)
                RT[1] = Buf("RT1")
                QZ_t = [sb("QZ%d" % i, [128, 512], BF16, st) for i in range(2)]
                QZ = [Buf("QZ%d" % i) for i in range(2)]
                for m_ in range(2):
                    S.op("dve", lambda e, m_=m_: e.memset(QZ_t[m_][:], 0.0), writes=[QZ[m_]])
                YR_t, YR = RT_t[1], RT[1]
                SQ1_t = sb("SQ1", [128, 512], BF16, st)
                SQ1 = Buf("SQ1")
                pend_tail = [None]
                sp_rr = [0]
                for c in range(4):
                    wvq, wbq = load_w([winab_d[:, 1536 + c * 128:1536 + (c + 1) * 128], winab_d[:, 3072 + c * 128:3072 + (c + 1) * 128]], 8)
                    wvk, wbk = load_w([winab_d[:, 2048 + c * 128:2048 + (c + 1) * 128], winab_d[:, 3584 + c * 128:3584 + (c + 1) * 128]], 8)
                    wv2, wb2 = load_w([winab_d[:, 2560 + c * 128:2560 + (c + 1) * 128]], 8)
                    for (a, bb) in TILES_ALL:
                        n = bb - a
                        rh, rb = ktile_h(a, bb)
                        for qi, (dst_t, dst_g, wv, wb) in enumerate(((QC_t, QC, wvq, wbq), (KC_t, KC, wvk, wbk))):
                            b0 = psb()
                            mm_group(b0, n, [(wv[:, kc, 0:128], rh[kc]) for kc in range(8)], rb + [wb])
                            if a < NL:
                                b1 = psb()
                                mm_group(b1, n, [(wv[:, kc, 128:256], rh[kc]) for kc in range(8)], rb + [wb])
                                S.op("dve", lambda e, a=a, bb=bb, b0=b0: e.tensor_tensor(out=RT_t[0][:, 0:512], in0=PS_t[:, b0, 0:512], in1=COS[:, a:bb], op=ALU.mult), reads=[PS[b0], RB], writes=[RT[0]])
                                S.op("dve", lambda e, a=a, bb=bb, b1=b1: e.tensor_tensor(out=RT_t[1][:, 0:512], in0=PS_t[:, b1, 0:512], in1=SIN[:, a:bb], op=ALU.mult), reads=[PS[b1], RB], writes=[RT[1]])
                                S.op("dve", lambda e, a=a, bb=bb, dst_t=dst_t: e.tensor_tensor(out=dst_t[:, a:bb], in0=RT_t[0][:, 0:512], in1=RT_t[1][:, 0:512], op=ALU.add), reads=[RT[0], RT[1]], writes=dst_g.R(0, a, bb))
                            else:
                                S.op("act", lambda e, a=a, bb=bb, n=n, b0=b0, dst_t=dst_t: e.activation(out=dst_t[:, a:bb], in_=PS_t[:, b0, 0:n], func=AF.Copy), reads=[PS[b0]], writes=dst_g.R(0, a, bb))
                    for kcnk in range(18):
                        bank = psb()
                        a = kcnk * 128
                        mm_group(bank, 128, [(H_t[:, kc, a:a + 128], wv2[:, kc, :]) for kc in range(8)], H.RC(range(8), a, a + 128) + [wb2])
                        S.op("act", lambda e, kcnk=kcnk, bank=bank: e.activation(out=VC_t[:, kcnk, :], in_=PS_t[:, bank, 0:128], func=AF.Copy), reads=[PS[bank]], writes=[VC])
                    def qz0(a, bb):
                        S.op("act", lambda e: e.activation(out=QZ_t[0][0:64, 0:bb - a], in_=QC_t[0:64, a:bb], func=AF.Copy), reads=QC.R(0, a, bb), writes=[QZ[0]])
                    for tix, (a, bb) in enumerate(TILES_ALL):
                        n = bb - a
                        kchunks = list(range(18)) if a < NL else [16, 17]
                        kpairs = [(kchunks[i], kchunks[i + 1]) for i in range(0, len(kchunks), 2)]
                        its = [(m, pr) for m in range(2) for pr in kpairs]
                        sb_of = {}
                        if tix == 0:
                            qz0(a, bb)
                        S.op("dve", lambda e, a=a, bb=bb, n=n: e.tensor_copy(out=QZ_t[1][64:128, 0:n], in_=QC_t[64:128, a:bb]), reads=QC.R(0, a, bb), writes=[QZ[1]])

                        def emit_s(j):
                            m, pr = its[j]
                            b0 = 2 * (sp_rr[0] % 2)
                            sp_rr[0] += 1
                            sb_of[j] = b0
                            for q_, kk in enumerate(pr):
                                mm_group(b0 + q_, n, [(KC_t[:, kk * 128:(kk + 1) * 128], QZ_t[m][:, 0:n])], KC.R(0, kk * 128, (kk + 1) * 128) + [QZ[m]])
                        emit_s(0)
                        for j, (m, pr) in enumerate(its):
                            if j + 1 < len(its):
                                emit_s(j + 1)
                            if j == len(kpairs) and tix + 1 < len(TILES_ALL):
                                qz0(*TILES_ALL[tix + 1])
                            if j == 6 and pend_tail[0] is not None:
                                pend_tail[0](7)
                                pend_tail[0] = None
                            b0 = sb_of[j]
                            pk = j % 3
                            S.op("act", lambda e, pk=pk, b0=b0, n=n: e.activation(out=PT_t[pk][:, :, 0:n], in_=PS_t[:, b0:b0 + 2, 0:n], func=AF.Exp, scale=0.125), reads=[PS[b0], PS[b0 + 1]], writes=[PT[pk]])
                            first, last = (pr == kpairs[0]), (pr == kpairs[-1])

                            def pv(e, m=m, pr=pr, pk=pk, n=n, first=first, last=last):
                                e.matmul(PS_t[:, 4 + m, 0:n], lhsT=VC_t[:, pr[0], :], rhs=PT_t[pk][:, 0, 0:n], start=first, stop=False)
                                return e.matmul(PS_t[:, 4 + m, 0:n], lhsT=VC_t[:, pr[1], :], rhs=PT_t[pk][:, 1, 0:n], start=False, stop=last)
                            if first:
                                S.op("pe", pv, reads=[PT[pk], VC], writes=[PS[4 + m]])
                                S.op("dve", lambda e, m=m, pk=pk, n=n: e.tensor_tensor(out=ZA_t[m][:, 0:n], in0=PT_t[pk][:, 0, 0:n], in1=PT_t[pk][:, 1, 0:n], op=ALU.add), reads=[PT[pk]], writes=[ZA[m]])
                            else:
                                S.op("pe", pv, reads=[PT[pk], VC], acc=[PS[4 + m]])
                                for q_ in range(2):
                                    S.op("dve", lambda e, m=m, pk=pk, n=n, q_=q_: e.tensor_tensor(out=ZA_t[m][:, 0:n], in0=ZA_t[m][:, 0:n], in1=PT_t[pk][:, q_, 0:n], op=ALU.add), reads=[PT[pk], ZA[m]], writes=[ZA[m]])
                            if last:
                                mm_group(6 + m, n, [(ONES[:], ZA_t[m][:, 0:n])], [ZA[m], CONST])
                        if pend_tail[0] is not None:
                            pend_tail[0](psb())
                            pend_tail[0] = None
                        for m in range(2):
                            S.op("act", lambda e, m=m, n=n: e.activation(out=RT_t[m][:, 0:n], in_=PS_t[:, 6 + m, 0:n], func=AF.Ln), reads=[PS[6 + m]], writes=[RT[m]])
                            S.op("act", lambda e, m=m, n=n: e.activation(out=RT_t[m][:, 0:n], in_=RT_t[m][:, 0:n], func=AF.Exp, scale=-1.0), reads=[RT[m]], writes=[RT[m]])
                            S.op("dve", lambda e, m=m, n=n: e.tensor_tensor(out=RT_t[m][:, 0:n], in0=PS_t[:, 4 + m, 0:n], in1=RT_t[m][:, 0:n], op=ALU.mult), reads=[PS[4 + m], RT[m]], writes=[RT[m]])
                        S.op("dve", lambda e, n=n: e.scalar_tensor_tensor(out=YR_t[:, 0:n], in0=RT_t[1][:, 0:n], scalar=NEGLAM, in1=RT_t[0][:, 0:n], op0=ALU.mult, op1=ALU.add), reads=[RT[0], RT[1], CONST], writes=[YR])
                        S.op("dve", lambda e, n=n: e.tensor_tensor(out=SQ1_t[:, 0:n], in0=YR_t[:, 0:n], in1=YR_t[:, 0:n], op=ALU.mult), reads=[YR], writes=[SQ1])
                        def tail(bk, n=n, a=a, bb=bb, c=c):
                            mm_group(bk, n, [(ONES[:], SQ1_t[:, 0:n])], [SQ1, CONST])
                            S.op("act", lambda e: e.activation(out=RT_t[0][:, 0:n], in_=PS_t[:, bk, 0:n], func=AF.Ln, bias=EPS, scale=1.0 / 128), reads=[PS[bk]], writes=[RT[0]])
                            S.op("act", lambda e: e.activation(out=RT_t[0][:, 0:n], in_=RT_t[0][:, 0:n], func=AF.Exp, scale=-0.5), reads=[RT[0]], writes=[RT[0]])
                            S.op("dve", lambda e: e.scalar_tensor_tensor(out=YB_t[:, c, a:bb], in0=YR_t[:, 0:n], scalar=SUBW[:, 0:1], in1=RT_t[0][:, 0:n], op0=ALU.mult, op1=ALU.mult), reads=[YR, RT[0], CONST], writes=YB.R(c, a, bb))
                        pend_tail[0] = tail
                    pend_tail[0](psb())
                    pend_tail[0] = None
                    if c == 0 and b == 0:
                        dd("d_qc", QC_t[:], [128, NT], BF16)
                        dd("d_kc", KC_t[:], [128, NT], BF16)
                        dd("d_cos", COS[:], [128, NL], BF16)
                        dd("d_sin", SIN[:], [128, NL], BF16)

                def kfun(a, bb):
                    o = pc(a)
                    return ([YAC_t[:, c, o:o + (bb - a)] for c in range(4)] + [YB_t[:, c, a:bb] for c in range(4)],
                            YAC + YB.RC(range(4), a, bb))
                wout_stage(l, b, woutab_d, kfun, TILES_ALL)

        def mixer_cd(l, b):
            with contextlib.ExitStack() as st:
                S.barrier()
                with contextlib.ExitStack() as st2:
                    norm_stage(l, b, 0, 1, TILES_ALL, st2)
                S.barrier()
                COS, SIN, RB = load_rope(st)
                MASKS = sb("MASKS", [128, 2, 512], BF16, st)
                ONESAB = sb("ONESAB", [128, 2, 128], BF16, st)
                POOLW = sb("POOLW", [128, 4, 128], BF16, st)
                ICNT = sb("ICNT", [128, 4, 16], F32, st)
                CB = Buf("cdconst")
                S.sync_to("pool")
                S.dma("pool", [(MASKS[:], mask_d), (ONESAB[:], onesab_d), (POOLW[:], poolw_d)], None, CB, writes=[CB])
                CB2 = Buf("cdconst2")
                S.dma("sp", ICNT[:], icnt_d, CB2, writes=[CB2], nowait=True)
                YD_t = sb("YD", [128, 4, NL], BF16, st)
                YD = Grid("YD", YD_t)
                with contextlib.ExitStack() as st2:
                    PX_t = sb("PX", [128, 2080], F32, st2)
                    PA_t = sb("PA", [128, 2080], F32, st2)
                    PB_t = sb("PB", [128, 2080], F32, st2)
                    PX, PA, PB = Buf("PX"), Buf("PA"), Buf("PB")
                    ZG_t = sb("ZG", [128, NL], BF16, st2)
                    ZG = Buf("ZG")
                    ET_t = sb("ET", [128, 16], F32, st2)
                    ET = Buf("ET")
                    S.op("dve", lambda e: e.memset(PX_t[:], 0.0), writes=[PX])
                    S.op("dve", lambda e: e.memset(PA_t[:], 0.0), writes=[PA])
                    S.op("dve", lambda e: e.memset(PB_t[:], 0.0), writes=[PB])
                    for g in range(4):
                        wv, wb = load_w([wincd_d[:, 1408 + g * 128:1408 + (g + 1) * 128]], 8)
                        for (a, bb) in TILES_L:
                            rh, rb = ktile_h(a, bb)
                            bank = psb()
                            mm_group(bank, 512, [(wv[:, kc, :], rh[kc]) for kc in range(8)], rb + [wb])
                            S.op("act", lambda e, a=a, bank=bank: e.activation(out=PX_t[:, 16 + a:16 + a + 512], in_=PS_t[:, bank, 0:512], func=AF.Copy), reads=[PS[bank]], writes=[PX])
                        w = (2, 4, 8, 16)[g]
                        cur_t, cur = PX_t, PX
                        sh = 1
                        dsts = [(PA_t, PA), (PB_t, PB)]
                        di_ = 0
                        while sh < w:
                            dt_, db_ = dsts[di_ % 2]
                            di_ += 1
                            S.op("dve", lambda e, dt_=dt_, cur_t=cur_t, sh=sh: e.tensor_tensor(out=dt_[:, 15:2080], in0=cur_t[:, 15:2080], in1=cur_t[:, 15 - sh:2080 - sh], op=ALU.add), reads=[cur], writes=[db_])
                            cur_t, cur = dt_, db_
                            sh *= 2
                        o = 16 + w // 2 - 1
                        S.op("dve", lambda e, cur_t=cur_t, o=o, w=w: e.scalar_tensor_tensor(out=ZG_t[:], in0=cur_t[:, o:o + NL], scalar=1.0 / w, in1=PX_t[:, 16:16 + NL], op0=ALU.mult, op1=ALU.subtract), reads=[cur, PX], writes=[ZG])
                        for side in range(2):
                            t0 = 0 if side == 0 else NL - 8
                            S.op("dve", lambda e, cur_t=cur_t, o=o, t0=t0, g=g, side=side: e.tensor_tensor(out=ET_t[:, 0:8], in0=cur_t[:, o + t0:o + t0 + 8], in1=ICNT[:, g, side * 8:side * 8 + 8], op=ALU.mult), reads=[cur, CONST, CB, CB2, RB], writes=[ET])
                            S.op("dve", lambda e, t0=t0: e.tensor_tensor(out=ZG_t[:, t0:t0 + 8], in0=ET_t[:, 0:8], in1=PX_t[:, 16 + t0:16 + t0 + 8], op=ALU.subtract), reads=[ET, PX, ZG], writes=[ZG])
                        for (a, bb) in TILES_L:
                            bank = psb()
                            mm_group(bank, 512, [(POOLW[:, g, :], ZG_t[:, a:bb])], [ZG, CONST, CB, RB])
                            S.op("act", lambda e, a=a, bb=bb, g=g, bank=bank: e.activation(out=YD_t[:, g, a:bb], in_=PS_t[:, bank, 0:512], func=AF.Identity, scale=PSCALE[:, g:g + 1]), reads=[PS[bank], CONST, CB, RB], writes=YD.R(g, a, bb))
                    S.barrier()
                QT_t = sb("QT", [128, 4, NL], BF16, st)
                QT = Grid("QT", QT_t)
                KT_t = sb("KT", [128, 2, NT], BF16, st)
                KT = Grid("KT", KT_t)
                S.op("dve", lambda e: e.memset(KT_t[:], 0.0), writes=KT.RC(range(2), 0, NT))
                VAB_t = sb("VAB", [128, 2, 18, 128], BF16, st)
                VAB = Buf("VAB")
                YA1_t, YA1 = QT_t, QT
                RT_t = [sb("RTc%d" % i, [128, 512], F32, st) for i in range(2)]
                RT = [Buf("RTc%d" % i) for i in range(2)]
                PT_t = [sb("PTc%d" % i, [128, 512], BF16, st) for i in range(4)]
                PT = [Buf("PTc%d" % i) for i in range(4)]
                S.op("dve", lambda e: e.memset(VAB_t[:], 0.0), writes=[VAB])

                def rope_proj(wv, wb, ci, dst_fn, dst_bufs_fn, tiles):
                    for (a, bb) in tiles:
                        n = bb - a
                        rh, rb = ktile_h(a, bb)
                        b0 = psb()
                        mm_group(b0, n, [(wv[:, kc, ci * 128:(ci + 1) * 128], rh[kc]) for kc in range(8)], rb + [wb])
                        if a < NL:
                            b1 = psb()
                            mm_group(b1, n, [(wv[:, kc, (ci + 1) * 128:(ci + 2) * 128], rh[kc]) for kc in range(8)], rb + [wb])
                            S.op("dve", lambda e, a=a, bb=bb, b0=b0: e.tensor_tensor(out=RT_t[0][:, 0:512], in0=PS_t[:, b0, 0:512], in1=COS[:, a:bb], op=ALU.mult), reads=[PS[b0], CONST, CB, RB], writes=[RT[0]])
                            S.op("dve", lambda e, a=a, bb=bb, b1=b1: e.tensor_tensor(out=RT_t[1][:, 0:512], in0=PS_t[:, b1, 0:512], in1=SIN[:, a:bb], op=ALU.mult), reads=[PS[b1], CONST, CB, RB], writes=[RT[1]])
                            if dst_fn is not None:
                                S.op("dve", lambda e, a=a, bb=bb: e.tensor_tensor(out=dst_fn(a, bb), in0=RT_t[0][:, 0:512], in1=RT_t[1][:, 0:512], op=ALU.add), reads=[RT[0], RT[1]], writes=dst_bufs_fn(a, bb))
                            else:
                                for kv_ in range(2):
                                    S.op("dve", lambda e, a=a, bb=bb, kv_=kv_: e.tensor_tensor(out=KT_t[64 * kv_:64 * kv_ + 64, kv_, a:bb], in0=RT_t[0][64 * kv_:64 * kv_ + 64, 0:512], in1=RT_t[1][64 * kv_:64 * kv_ + 64, 0:512], op=ALU.add), reads=[RT[0], RT[1]], writes=dst_bufs_fn(a, bb))
                        else:
                            if dst_fn is not None:
                                S.op("act", lambda e, a=a, bb=bb, n=n, b0=b0: e.activation(out=dst_fn(a, bb), in_=PS_t[:, b0, 0:n], func=AF.Copy), reads=[PS[b0]], writes=dst_bufs_fn(a, bb))
                            else:
                                for kv_ in range(2):
                                    S.op("act", lambda e, a=a, bb=bb, n=n, b0=b0, kv_=kv_: e.activation(out=KT_t[64 * kv_:64 * kv_ + 64, kv_, a:bb], in_=PS_t[64 * kv_:64 * kv_ + 64, b0, 0:n], func=AF.Copy), reads=[PS[b0]], writes=dst_bufs_fn(a, bb))
                for g in range(4):
                    wv, wb = load_w([wincd_d[:, g * 128:(g + 1) * 128], wincd_d[:, 512 + g * 128:512 + (g + 1) * 128]], 8)
                    rope_proj(wv, wb, 0, lambda a, bb, g=g: QT_t[:, g, a:bb], lambda a, bb, g=g: QT.R(g, a, bb), TILES_L)
                wv, wb = load_w([wincd_d[:, 1024:1152], wincd_d[:, 1152:1280], wincd_d[:, 1280:1408]], 8)
                rope_proj(wv, wb, 0, None, lambda a, bb: KT.RC(range(2), a, bb), TILES_ALL)
                for kcnk in range(18):
                    bank = psb()
                    a = kcnk * 128
                    mm_group(bank, 128, [(H_t[:, kc, a:a + 128], wv[:, kc, 256:384]) for kc in range(8)], H.RC(range(8), a, a + 128) + [wb])
                    S.op("act", lambda e, kcnk=kcnk, bank=bank: e.activation(out=VAB_t[:, 0, kcnk, 0:64], in_=PS_t[:, bank, 0:64], func=AF.Copy), reads=[PS[bank]], writes=[VAB])
                    S.op("dve", lambda e, kcnk=kcnk, bank=bank: e.tensor_copy(out=VAB_t[:, 1, kcnk, 64:128], in_=PS_t[:, bank, 64:128]), reads=[PS[bank]], writes=[VAB])
                for i in range(16):
                    qa, qb = i * 128, (i + 1) * 128
                    kl = [(16, None), (17, None)]
                    if i > 0:
                        kl.append((i - 1, 0))
                    kl.append((i, None))
                    if i < 15:
                        kl.append((i + 1, 1))
                    its = [(kv, kk, mk) for kv in range(2) for (kk, mk) in kl]
                    sbanks = {}

                    def emit_s(j):
                        kv, kk, mk = its[j]
                        bk = psb()
                        sbanks[j] = bk
                        mm_group(bk, 512, [(KT_t[:, kv, kk * 128:(kk + 1) * 128], QT_t[:, :, qa:qb])], KT.R(kv, kk * 128, (kk + 1) * 128) + QT.RC(range(4), qa, qb))
                    emit_s(0)
                    emit_s(1)
                    for j, (kv, kk, mk) in enumerate(its):
                        if j + 2 < len(its):
                            emit_s(j + 2)
                        bk = sbanks[j]
                        pk = j % 4
                        S.op("act", lambda e, pk=pk, bk=bk: e.activation(out=PT_t[pk][:], in_=PS_t[:, bk, :], func=AF.Exp, scale=0.125), reads=[PS[bk]], writes=[PT[pk]])
                        if mk is not None:
                            S.op("dve", lambda e, pk=pk, mk=mk: e.tensor_tensor(out=PT_t[pk][:], in0=PT_t[pk][:], in1=MASKS[:, mk, :], op=ALU.mult), reads=[PT[pk], CONST, CB, RB], writes=[PT[pk]])
                        first, last = (j == 0), (j == len(its) - 1)

                        ob = 4 + 2 * (i % 2)

                        def pv(e, kv=kv, kk=kk, pk=pk, first=first, last=last, ob=ob):
                            e.matmul(PS_t[:, ob, :], lhsT=VAB_t[:, kv, kk, :], rhs=PT_t[pk][:], start=first, stop=last)
                            return e.matmul(PS_t[:, ob + 1, :], lhsT=ONESAB[:, kv, :], rhs=PT_t[pk][:], start=first, stop=last)
                        if first:
                            S.op("pe", pv, reads=[PT[pk], VAB, CONST, CB, RB], writes=[PS[ob], PS[ob + 1]])
                        else:
                            S.op("pe", pv, reads=[PT[pk], VAB, CONST, CB, RB], acc=[PS[ob], PS[ob + 1]])
                    ob = 4 + 2 * (i % 2)
                    rk = i % 2
                    for g in range(4):
                        S.op("act", lambda e, g=g, ob=ob, rk=rk: e.activation(out=RT_t[rk][:, g * 128:(g + 1) * 128], in_=PS_t[:, ob + 1, g * 128:(g + 1) * 128], func=AF.Ln, bias=SINKE[:, g:g + 1]), reads=[PS[ob + 1], CONST, CB, RB], writes=[RT[rk]])
                    S.op("act", lambda e, rk=rk: e.activation(out=RT_t[rk][:], in_=RT_t[rk][:], func=AF.Exp, scale=-1.0), reads=[RT[rk]], writes=[RT[rk]])
                    S.op("dve", lambda e, qa=qa, qb=qb, ob=ob, rk=rk: e.tensor_tensor(out=YA1_t[:, :, qa:qb], in0=PS_t[:, ob, :].rearrange("p (g n) -> p g n", g=4), in1=RT_t[rk][:].rearrange("p (g n) -> p g n", g=4), op=ALU.mult), reads=[PS[ob], RT[rk]], writes=YA1.RC(range(4), qa, qb))

                def kfun(a, bb):
                    return ([YA1_t[:, g, a:bb] for g in range(4)] + [YD_t[:, g, a:bb] for g in range(4)],
                            YA1.RC(range(4), a, bb) + YD.RC(range(4), a, bb))
                wout_stage(l, b, woutcd_d, kfun, TILES_L)

        def final_stage(b):
            with contextlib.ExitStack() as st:
                S.barrier()
                SQ_t = sb("SQf", [128, 8, 512], BF16, st)
                SQ = [Buf("SQf%d" % c) for c in range(8)]
                RS_t = sb("RSf", [128, 512], F32, st)
                RS = Buf("RSf")
                YF_t = sb("YF", [128, 8, 512], F32, st)
                YF = [Buf("YF%d" % c) for c in range(8)]
                OS_t = [sb("OS%d" % i, [128, D], F32, st) for i in range(2)]
                OS = [Buf("OS%d" % i) for i in range(2)]
                for (a, bb) in TILES_L:
                    for c in range(8):
                        if c < 4:
                            S.op("act", lambda e, c=c, a=a, bb=bb: e.activation(out=SQ_t[:, c, :], in_=XR_t[:, c, a:bb], func=AF.Square), reads=XR.R(c, a, bb), writes=[SQ[c]])
                        else:
                            S.op("dve", lambda e, c=c, a=a, bb=bb: e.tensor_tensor(out=SQ_t[:, c, :], in0=XR_t[:, c, a:bb], in1=XR_t[:, c, a:bb], op=ALU.mult), reads=XR.R(c, a, bb), writes=[SQ[c]])
                    bank = psb()
                    mm_group(bank, 512, [(ONES[:], SQ_t[:, c, :]) for c in range(8)], SQ + [CONST])
                    S.op("act", lambda e, bank=bank: e.activation(out=RS_t[:], in_=PS_t[:, bank, :], func=AF.Ln, bias=EPS, scale=1.0 / D), reads=[PS[bank]], writes=[RS])
                    S.op("act", lambda e: e.activation(out=RS_t[:], in_=RS_t[:], func=AF.Exp, scale=-0.5), reads=[RS], writes=[RS])
                    for c in range(8):
                        S.op("dve", lambda e, c=c, a=a, bb=bb: e.scalar_tensor_tensor(out=YF_t[:, c, :], in0=XR_t[:, c, a:bb], scalar=FNW[:, c:c + 1], in1=RS_t[:], op0=ALU.mult, op1=ALU.mult), reads=XR.R(c, a, bb) + [RS, CONST], writes=[YF[c]])
                    for s in range(4):
                        tok = a + s * 128
                        k = (tok // 128) % 2
                        for hh in range(2):
                            bank = psb()

                            def tp(e, hh=hh, bank=bank, s=s):
                                ins = None
                                for q in range(4):
                                    c = hh * 4 + q
                                    ins = e.transpose(out=PS_t[:, bank, q * 128:(q + 1) * 128], in_=YF_t[:, c, s * 128:(s + 1) * 128], identity=IDENT[:])
                                return ins
                            S.op("pe", tp, reads=[YF[hh * 4 + q] for q in range(4)] + [CONST], writes=[PS[bank]])
                            if hh == 0:
                                S.op("act", lambda e, k=k, bank=bank: e.activation(out=OS_t[k][:, 0:512], in_=PS_t[:, bank, :], func=AF.Copy), reads=[PS[bank]], writes=[OS[k]])
                            else:
                                S.op("dve", lambda e, k=k, bank=bank: e.tensor_copy(out=OS_t[k][:, 512:1024], in_=PS_t[:, bank, :]), reads=[PS[bank]], writes=[OS[k]])
                        S.dma("sp", out_d[b, tok:tok + 128, :], OS_t[k][:], OS[k], reads=[OS[k]])
                S.barrier()
                for k in range(2):
                    if OS[k].dsem is not None:
                        nc.sync.wait_ge(OS[k].dsem, OS[k].dcnt)

        def dump_dbg():
            S.barrier()
            DB = Buf("dbg")
            S.dma("sp", dbg_d, XR_t[:], DB, reads=[bb_ for v in XR.b.values() for bb_ in [v]])
            nc.sync.wait_ge(DB.dsem, DB.dcnt)

        stages = [("mixer0", lambda b: mixer_ab(0, b)), ("ffn0", lambda b: ffn_stage(0, b, True)),
                  ("mixer1", lambda b: mixer_cd(1, b)), ("ffn1", lambda b: ffn_stage(1, b, False))]
        for b in range(BPC):
            load_x(b)
            stop = False
            for name, fn in stages:
                fn(b)
                if b == 0 and name == "mixer0":
                    compute_mods(1)
                if DEBUG_STOP == name:
                    stop = True
                    break
            if stop:
                dump_dbg()
                break
            final_stage(b)
    return nc


def _fm(v):
    v = np.asarray(v, np.float32)
    n = v.shape[-1] // 128
    return np.ascontiguousarray(np.moveaxis(v.reshape(v.shape[:-1] + (n, 128)), -1, 0))


def _rope_tables():
    pos = np.arange(NL)
    row = (pos // 64).astype(np.float32)
    col = (pos % 64).astype(np.float32)
    inv = (1.0 / (np.float32(10000.0) ** (np.arange(0, 32, 2, dtype=np.float32) / np.float32(32)))).astype(np.float32)
    cos_t = np.zeros((128, NL), np.float32)
    sin_t = np.zeros((128, NL), np.float32)
    for p in range(128):
        d = p % 64
        ax = row if d < 32 else col
        dd = d % 32
        ang = (ax * inv[dd % 16]).astype(np.float32)
        cos_t[p] = np.cos(ang)
        sin_t[p] = -np.sin(ang) if dd < 16 else np.sin(ang)
    return cos_t, sin_t


def _partner_perm(ncols):
    idx = np.arange(ncols)
    d = idx % 64
    dd = d % 32
    return np.where(dd < 16, idx + 16, idx - 16)


_PROG = {}


def _prepare(inputs):
    f = lambda k: np.asarray(inputs[k], np.float32)
    x, c, ctx, c_ctx = f("x"), f("c"), f("ctx"), f("c_ctx")
    shared = {}
    shared["w_mod"] = f("w_mod")
    shared["b_mod"] = np.ascontiguousarray(_fm(f("b_mod")))
    w = f("w_in_ab")[0]
    q, k = w[:, 1536:2048], w[:, 2048:2560]
    pp = _partner_perm(512)
    shared["w_in_ab"] = np.ascontiguousarray(np.concatenate([w, q[:, pp], k[:, pp]], axis=1))
    shared["conv_a"] = np.ascontiguousarray(np.transpose(f("conv_a")[0].reshape(3, 4, 128), (2, 1, 0)))
    shared["lam_qk"] = np.ascontiguousarray(np.broadcast_to(f("lam_qk")[0][None], (128, 4, 64)))
    shared["subln"] = np.ascontiguousarray(f("subln_b")[0].reshape(128, 1))
    shared["w_out_ab"] = f("w_out_ab")[0]
    w = f("w_in_cd")[0]
    qh = w[:, 0:512].reshape(D, 8, 64)
    qg = np.concatenate([np.concatenate([qh[:, g], qh[:, 4 + g]], axis=1) for g in range(4)], axis=1)
    kk = w[:, 512:640]
    shared["w_in_cd"] = np.ascontiguousarray(np.concatenate([qg, qg[:, _partner_perm(512)], kk, kk[:, _partner_perm(128)], w[:, 640:768], w[:, 768:1280]], axis=1))
    sk = f("sink_c")[0]
    shared["sink"] = np.ascontiguousarray(np.concatenate([np.broadcast_to(sk[None, 0:4], (64, 4)), np.broadcast_to(sk[None, 4:8], (64, 4))], axis=0))
    shared["pool_w"] = np.ascontiguousarray(np.transpose(f("pool_w")[0], (1, 0, 2)))
    shared["pool_scale"] = np.ascontiguousarray(f("pool_scale")[0].reshape(4, 128).T)
    wo = f("w_out_cd")[0]
    rows = []
    for g in range(4):
        rows += list(range(g * 64, g * 64 + 64)) + list(range((4 + g) * 64, (4 + g) * 64 + 64))
    rows += list(range(512, 1024))
    shared["w_out_cd"] = np.ascontiguousarray(wo[np.array(rows)])
    shared["w_up"] = f("w_up")
    shared["cfw"] = np.ascontiguousarray(np.transpose(f("conv_ffn_w").reshape(2, 3, 44, 128), (3, 0, 2, 1)))
    shared["cfb"] = np.ascontiguousarray(np.transpose(f("conv_ffn_b").reshape(2, 44, 128), (2, 0, 1)))
    shared["w_down"] = f("w_down")
    shared["fnw"] = np.ascontiguousarray(f("final_norm_w").reshape(8, 128).T)
    shared["ident"] = np.eye(128, dtype=np.float32)
    shared["cos_t"], shared["sin_t"] = _rope_tables()
    kk_, qq_ = np.meshgrid(np.arange(128), np.arange(128), indexing="ij")
    m_prev = (kk_ >= qq_).astype(np.float32)
    m_next = (kk_ <= qq_).astype(np.float32)
    shared["masks"] = np.ascontiguousarray(np.stack([np.tile(m_prev, (1, 4)), np.tile(m_next, (1, 4))], axis=1))
    ic = np.zeros((4, 16), np.float32)
    for g, wdt in enumerate((2, 4, 8, 16)):
        for j in range(8):
            t = j
            ic[g, j] = 1.0 / (min(t + wdt // 2, NL) - max(t - wdt // 2, 0))
            t = NL - 8 + j
            ic[g, 8 + j] = 1.0 / (min(t + wdt // 2, NL) - max(t - wdt // 2, 0))
    shared["icnt"] = np.ascontiguousarray(np.broadcast_to(ic[None], (128, 4, 16)))
    oab = np.zeros((128, 2, 128), np.float32)
    oab[:, 0, 0:64] = 1.0
    oab[:, 1, 64:128] = 1.0
    shared["onesab"] = oab
    in_maps = []
    for i in range(NCORES):
        m = dict(shared)
        m["x"] = np.ascontiguousarray(x[BPC * i:BPC * (i + 1)])
        m["ctx"] = np.ascontiguousarray(ctx[BPC * i:BPC * (i + 1)])
        cv = np.stack([c[BPC * i], c[BPC * i + 1], c_ctx], axis=0)
        m["cT"] = np.ascontiguousarray(np.transpose(cv.reshape(3, 8, 128), (2, 1, 0)))
        in_maps.append(m)
    return in_maps


def kernel(**inputs):
    if "nc" not in _PROG:
        _PROG["nc"] = build_program()
    in_maps = _prepare(inputs)
    res = run_bass_kernel_spmd(_PROG["nc"], in_maps, core_ids=list(range(NCORES)))
    return np.concatenate([r["out"] for r in res.results], axis=0).astype(np.float32)
```

```python
import contextlib
import math
import numpy as np
import concourse.bass as bass
import concourse.mybir as mybir
from concourse.bass_utils import run_bass_kernel_spmd

F32 = mybir.dt.float32
BF16 = mybir.dt.bfloat16
AF = mybir.ActivationFunctionType
ALU = mybir.AluOpType
AX = mybir.AxisListType

D = 1024
NL = 2048
NCX = 256
NT = NL + NCX
DFF = 2816
EPS = 1e-6
NCORES = 8
BPC = 2
SEG = 256

DEBUG_STOP = None


class Eng:
    def __init__(self, name, e, sem, sid):
        self.name, self.e, self.sem, self.sid, self.cnt, self.seen = name, e, sem, sid, 0, {}


class Buf:
    __slots__ = ("name", "w", "r", "dsem", "dsid", "dcnt")

    def __init__(self, name):
        self.name, self.w, self.r = name, {}, {}
        self.dsem = None
        self.dcnt = 0


def _merge(dst, src):
    for k, v in src.items():
        if dst.get(k, 0) < v:
            dst[k] = v


class Sched:
    def __init__(self, nc, es):
        self.nc, self.es = nc, es
        self.sems = {}
        self.nsem = 0
        self.eng = {}
        for name, e in (("pe", nc.tensor), ("act", nc.scalar), ("dve", nc.vector), ("pool", nc.gpsimd), ("sp", nc.sync)):
            sem, sid = self.new_sem("e_" + name)
            self.eng[name] = Eng(name, e, sem, sid)

    def new_sem(self, name):
        sem = self.es.enter_context(self.nc.semaphore("%s_%d" % (name, self.nsem)))
        sid = self.nsem
        self.nsem += 1
        self.sems[sid] = sem
        return sem, sid

    def _need(self, reads, writes, acc, own_sid):
        need = {}
        for b in reads:
            _merge(need, b.w)
        for b in writes:
            if b.r:
                _merge(need, b.r)
            else:
                _merge(need, b.w)
        for b in acc:
            if b.r:
                _merge(need, b.r)
            else:
                for k, v in b.w.items():
                    if k != own_sid and need.get(k, 0) < v:
                        need[k] = v
        return need

    def _wait(self, E, need):
        for sid, val in need.items():
            if E.seen.get(sid, 0) < val:
                E.e.wait_ge(self.sems[sid], val)
                E.seen[sid] = val

    def op(self, en, fn, reads=(), writes=(), acc=()):
        E = self.eng[en]
        self._wait(E, self._need(reads, writes, acc, E.sid))
        ins = fn(E.e)
        E.cnt += 1
        ins.then_inc(E.sem, 1)
        for b in reads:
            if b.r.get(E.sid, 0) < E.cnt:
                b.r[E.sid] = E.cnt
        for b in writes:
            b.w = {E.sid: E.cnt}
            b.r = {}
        for b in acc:
            b.w = {E.sid: E.cnt}
            b.r = {}

    def dma(self, qn, out_ap, in_ap, owner, reads=(), writes=(), nowait=False):
        E = self.eng[qn]
        if owner.dsem is None:
            owner.dsem, owner.dsid = self.new_sem("d_" + owner.name)
        if not nowait:
            self._wait(E, self._need(reads, writes, (), -1))
        pairs = out_ap if isinstance(out_ap, list) else [(out_ap, in_ap)]
        for (o_, i_) in pairs:
            ins = E.e.dma_start(out=o_, in_=i_)
            owner.dcnt += 16
            ins.then_inc(owner.dsem, 16)
        for b in reads:
            if b.r.get(owner.dsid, 0) < owner.dcnt:
                b.r[owner.dsid] = owner.dcnt
        for b in writes:
            b.w = {owner.dsid: owner.dcnt}
            b.r = {}

    def sync_to(self, qn, names=("pe", "act", "dve")):
        self._wait(self.eng[qn], {self.eng[n].sid: self.eng[n].cnt for n in names if self.eng[n].cnt > 0})

    def barrier(self, names=("pe", "act", "dve", "sp")):
        clock = {self.eng[n].sid: self.eng[n].cnt for n in names if self.eng[n].cnt > 0}
        for n in names:
            E = self.eng[n]
            self._wait(E, {k: v for k, v in clock.items() if k != E.sid})


class Grid:
    def __init__(self, name, t):
        self.name, self.t, self.b = name, t, {}

    def R(self, c, a, b):
        out = []
        for k in range(a // SEG, (b - 1) // SEG + 1):
            key = (c, k)
            if key not in self.b:
                self.b[key] = Buf("%s_%s_%d" % (self.name, c, k))
            out.append(self.b[key])
        return out

    def RC(self, cs, a, b):
        out = []
        for c in cs:
            out += self.R(c, a, b)
        return out


def pc(t):
    return 1 + t if t < NL else 3 + t


def lam_init(layer):
    return 0.8 - 0.6 * math.exp(-0.3 * layer)


def build_program():
    nc = bass.Bass("TRN2", target_bir_lowering=False)
    di = {}

    def din(name, shape, dt=F32):
        di[name] = nc.dram_tensor(name, list(shape), dt, kind="ExternalInput").ap()
        return di[name]

    x_d = din("x", [BPC, NL, D])
    ctx_d = din("ctx", [BPC, NCX, D])
    cT_d = din("cT", [128, 8, 3])
    wmod_d = din("w_mod", [2, D, 6 * D])
    bmod_d = din("b_mod", [128, 2, 48])
    winab_d = din("w_in_ab", [D, 4096])
    conva_d = din("conv_a", [128, 4, 3])
    lamqk_d = din("lam_qk", [128, 4, 64])
    subln_d = din("subln", [128, 1])
    woutab_d = din("w_out_ab", [D, D])
    wincd_d = din("w_in_cd", [D, 1920])
    sink_d = din("sink", [128, 4])
    poolw_d = din("pool_w", [128, 4, 128])
    pscale_d = din("pool_scale", [128, 4])
    woutcd_d = din("w_out_cd", [D, D])
    wup_d = din("w_up", [2, D, 2 * DFF])
    cfw_d = din("cfw", [128, 2, 44, 3])
    cfb_d = din("cfb", [128, 2, 44])
    wdown_d = din("w_down", [2, DFF, D])
    fnw_d = din("fnw", [128, 8])
    ident_d = din("ident", [128, 128])
    cos_d = din("cos_t", [128, NL])
    sin_d = din("sin_t", [128, NL])
    mask_d = din("masks", [128, 2, 512])
    icnt_d = din("icnt", [128, 4, 16])
    onesab_d = din("onesab", [128, 2, 128])
    out_d = nc.dram_tensor("out", [BPC, NL, D], F32, kind="ExternalOutput").ap()
    dbg_d = None
    if DEBUG_STOP is not None:
        dbg_d = nc.dram_tensor("dbg", [128, 8, NT], F32, kind="ExternalOutput").ap()

    with contextlib.ExitStack() as es:
        S = Sched(nc, es)

        uniq = [0]

        def sb(name, shape, dt, stack=es):
            uniq[0] += 1
            return stack.enter_context(nc.sbuf_tensor("%s_%d" % (name, uniq[0]), list(shape), dt))

        XR_t = sb("XR", [128, 8, NT], F32)
        H_t = sb("H", [128, 8, NT], BF16)
        XR = Grid("XR", XR_t)
        H = Grid("H", H_t)
        PS_t = es.enter_context(nc.psum_tensor("PS", [128, 8, 512], F32))
        PS = [Buf("ps%d" % i) for i in range(8)]
        IDENT = sb("IDENT", [128, 128], F32)
        ONES = sb("ONES", [128, 128], BF16)
        ONESF = sb("ONESF", [128, 128], F32)
        MODS = sb("MODS", [128, 2, 6, 8, 3], F32)
        CT = sb("CT", [128, 8, 3], F32)
        SC = sb("SC", [128, 8, 3], F32)
        SCB = sb("SCB", [128, 8, 3], BF16)
        BM = sb("BM", [128, 2, 48], F32)
        CONVA = sb("CONVA", [128, 4, 3], F32)
        LS = sb("LS", [128, 4], F32)
        SUBW = sb("SUBW", [128, 1], F32)
        SINKE = sb("SINKE", [128, 4], F32)
        PSCALE = sb("PSCALE", [128, 4], F32)
        CFW = sb("CFW", [128, 2, 44, 3], F32)
        CFB = sb("CFB", [128, 2, 44], F32)
        FNW = sb("FNW", [128, 8], F32)
        CONST = Buf("const")
        NSLOT = 4
        SLOT_t = [sb("WS%d" % i, [128, 3072], BF16) for i in range(NSLOT)]
        SLOT = [Buf("ws%d" % i) for i in range(NSLOT)]
        slot_rr = [0]
        ps_rr = [0]

        def psb(pool=(0, 1, 2, 3)):
            i = pool[ps_rr[0] % len(pool)]
            ps_rr[0] += 1
            return i

        def cload(dst_ap, src_ap, q="sp"):
            S.dma(q, dst_ap, src_ap, CONST, writes=[CONST], nowait=True)

        cload(IDENT[:], ident_d)
        cload(CT[:], cT_d)
        cload(BM[:], bmod_d)
        cload(CONVA[:], conva_d)
        cload(SUBW[:], subln_d)
        cload(SINKE[:], sink_d)
        cload(PSCALE[:], pscale_d)
        cload(CFW[:], cfw_d)
        cload(CFB[:], cfb_d)
        cload(FNW[:], fnw_d)
        S.op("dve", lambda e: e.memset(ONES[:], 1.0), writes=[CONST])
        S.op("dve", lambda e: e.memset(ONESF[:], 1.0), writes=[CONST])

        def load_w(srcs, kc):
            i = slot_rr[0] % NSLOT
            slot_rr[0] += 1
            tot = sum(s.shape[1] for s in srcs)
            assert kc * tot <= 3072
            view = SLOT_t[i][:, 0:kc * tot].rearrange("p (k n) -> p k n", k=kc)
            off = 0
            pairs = []
            for s in srcs:
                n = s.shape[1]
                pairs.append((view[:, :, off:off + n], s.rearrange("(k p) n -> p k n", p=128)))
                off += n
            S.dma("pool", pairs, None, SLOT[i], writes=[SLOT[i]])
            return view, SLOT[i]

        def mm_group(bank, n, pairs, reads, m=128):
            def f(e):
                ins = None
                last = len(pairs) - 1
                for i, (lt, rh) in enumerate(pairs):
                    ins = e.matmul(PS_t[0:m, bank, 0:n], lhsT=lt, rhs=rh, start=(i == 0), stop=(i == last))
                return ins
            S.op("pe", f, reads=reads, writes=[PS[bank]])

        S.op("act", lambda e: e.activation(out=SC[:], in_=CT[:], func=AF.Silu), reads=[CONST], writes=[CONST])
        S.op("act", lambda e: e.activation(out=SCB[:], in_=CT[:], func=AF.Silu), reads=[CONST], writes=[CONST])

        def compute_mods(l):
            with contextlib.ExitStack() as st:
                S.barrier()
                MR_t = sb("MR", [3, 6144], F32, st)
                MR = Buf("MR")
                for blk in range(16):
                    wv, wb = load_w([wmod_d[l][:, blk * 384:(blk + 1) * 384]], 8)
                    bank = psb()
                    mm_group(bank, 384, [(SCB[:, kc, :], wv[:, kc, :]) for kc in range(8)], [wb, CONST], m=3)
                    S.op("act", lambda e, blk=blk, bank=bank: e.activation(out=MR_t[0:3, blk * 384:(blk + 1) * 384], in_=PS_t[0:3, bank, 0:384], func=AF.Copy), reads=[PS[bank]], writes=[MR])
                bank = psb()

                def tp(e, bank=bank):
                    ins = None
                    for j in range(48):
                        ins = e.transpose(out=PS_t[:, bank, j * 3:j * 3 + 3], in_=MR_t[0:3, j * 128:(j + 1) * 128], identity=IDENT[0:3, 0:3])
                    return ins
                S.op("pe", tp, reads=[MR, CONST], writes=[PS[bank]])
                for r in range(3):
                    S.op("dve", lambda e, r=r, l=l, bank=bank: e.tensor_tensor(
                        out=MODS[:, l].rearrange("p m c r -> p (m c) r")[:, :, r], in0=PS_t[:, bank, 0:144].rearrange("p (j r) -> p j r", r=3)[:, :, r], in1=BM[:, l, :], op=ALU.add),
                        reads=[PS[bank], CONST], writes=[CONST])
                for m in (1, 4):
                    S.op("dve", lambda e, l=l, m=m: e.tensor_scalar_add(out=MODS[:, l, m], in0=MODS[:, l, m], scalar1=1.0), reads=[CONST], writes=[CONST])
                S.barrier()
        compute_mods(0)
        with contextlib.ExitStack() as st:
            LQ = sb("LQ", [128, 4, 64], F32, st)
            LTMP = sb("LTMP", [128, 2, 64], F32, st)
            cload(LQ[:], lamqk_d)
            for q_ in range(2):
                S.op("dve", lambda e, q_=q_: e.tensor_tensor(out=LTMP[:, q_, :], in0=LQ[:, 2 * q_, :], in1=LQ[:, 2 * q_ + 1, :], op=ALU.mult), reads=[CONST], writes=[CONST])
            S.op("dve", lambda e: e.reduce_sum(out=LS[:, 0:2], in_=LTMP[:], axis=AX.X), reads=[CONST], writes=[CONST])
            S.op("act", lambda e: e.activation(out=LS[:, 0:2], in_=LS[:, 0:2], func=AF.Exp), reads=[CONST], writes=[CONST])
            S.op("dve", lambda e: e.tensor_tensor(out=LS[:, 2:3], in0=LS[:, 1:2], in1=LS[:, 0:1], op=ALU.subtract), reads=[CONST], writes=[CONST])
            S.op("dve", lambda e: e.tensor_scalar_add(out=LS[:, 2:3], in0=LS[:, 2:3], scalar1=-lam_init(0)), reads=[CONST], writes=[CONST])
            S.barrier()

        ddc = [0]

        def dd(name, src_ap, shape, dt):
            if DEBUG_STOP is None:
                return
            ddc[0] += 1
            dst = nc.dram_tensor(name, list(shape), dt, kind="ExternalOutput").ap()
            S.barrier()
            DB = Buf("dd%d" % ddc[0])
            S.dma("sp", dst, src_ap, DB, nowait=True)
            nc.sync.wait_ge(DB.dsem, DB.dcnt)
            for n_ in ("pe", "act", "dve"):
                S._wait(S.eng[n_], {DB.dsid: DB.dcnt})

        def mod(l, m, c, r):
            return MODS[:, l, m, c, r:r + 1]

        NEGLAM = LS[:, 2:3]
        S.op("dve", lambda e: e.tensor_scalar_mul(out=SUBW[:], in0=SUBW[:], scalar1=1.0 - lam_init(0)), reads=[CONST], writes=[CONST])
        S.op("act", lambda e: e.activation(out=SINKE[:], in_=SINKE[:], func=AF.Exp), reads=[CONST], writes=[CONST])

        def load_rope(st):
            COS = sb("COS", [128, NL], BF16, st)
            SIN = sb("SIN", [128, NL], BF16, st)
            RB = Buf("rope")
            S.sync_to("pool")
            S.dma("pool", [(COS[:], cos_d), (SIN[:], sin_d)], None, RB, writes=[RB])
            return COS, SIN, RB

        TILES_L = [(i * 512, (i + 1) * 512) for i in range(4)]
        TILES_ALL = TILES_L + [(NL, NT)]

        def load_x(b):
          with contextlib.ExitStack() as st:
            S.barrier()
            XS_t = [sb("XS%d" % i, [128, D], F32, st) for i in range(4)]
            XS = [Buf("xs%d" % i) for i in range(4)]
            for j in range(NT // 128):
                k = j % 4
                src = x_d[b, j * 128:(j + 1) * 128, :] if j < 16 else ctx_d[b, (j - 16) * 128:(j - 15) * 128, :]
                S.dma("sp", XS_t[k][:], src, XS[k], writes=[XS[k]])
                for hh in range(2):
                    bank = psb()

                    def tp(e, k=k, hh=hh, bank=bank):
                        ins = None
                        for q in range(4):
                            c = hh * 4 + q
                            ins = e.transpose(out=PS_t[:, bank, q * 128:(q + 1) * 128], in_=XS_t[k][:, c * 128:(c + 1) * 128], identity=IDENT[:])
                        return ins
                    S.op("pe", tp, reads=[XS[k], CONST], writes=[PS[bank]])
                    en = "act" if hh == 0 else "dve"
                    dst = XR_t[:, hh * 4:hh * 4 + 4, j * 128:(j + 1) * 128]
                    srcp = PS_t[:, bank, :].rearrange("p (q n) -> p q n", q=4)
                    wr = XR.RC(range(hh * 4, hh * 4 + 4), j * 128, (j + 1) * 128)
                    if en == "act":
                        S.op("act", lambda e, dst=dst, srcp=srcp: e.activation(out=dst, in_=srcp, func=AF.Copy), reads=[PS[bank]], writes=wr)
                    else:
                        S.op("dve", lambda e, dst=dst, srcp=srcp: e.tensor_copy(out=dst, in_=srcp), reads=[PS[bank]], writes=wr)

        def norm_stage(l, b, m_shift, m_scale, tiles, st):
            SQ_t = sb("SQ", [128, 8, 512], BF16, st)
            SQ = Buf("SQ")
            RS_t = [sb("RS%d" % i, [128, 512], F32, st) for i in range(2)]
            RS = [Buf("RS%d" % i) for i in range(2)]
            TM_t = [sb("TM%d" % i, [128, 512], F32, st) for i in range(2)]
            TM = [Buf("TM%d" % i) for i in range(2)]
            SQB = [Buf("SQ%d" % c) for c in range(8)]

            def sq(ti, a, bb, c):
                n = bb - a
                if c < 4:
                    S.op("act", lambda e: e.activation(out=SQ_t[:, c, 0:n], in_=XR_t[:, c, a:bb], func=AF.Square), reads=XR.R(c, a, bb), writes=[SQB[c]])
                else:
                    S.op("dve", lambda e: e.tensor_tensor(out=SQ_t[:, c, 0:n], in0=XR_t[:, c, a:bb], in1=XR_t[:, c, a:bb], op=ALU.mult), reads=XR.R(c, a, bb), writes=[SQB[c]])

            def stat(ti, a, bb):
                n = bb - a
                bank = psb()
                mm_group(bank, n, [(ONES[:], SQ_t[:, c, 0:n]) for c in range(8)], SQB + [CONST])
                k = ti % 2
                S.op("act", lambda e: e.activation(out=RS_t[k][:, 0:n], in_=PS_t[:, bank, 0:n], func=AF.Ln, bias=EPS, scale=1.0 / D), reads=[PS[bank]], writes=[RS[k]])
                S.op("act", lambda e: e.activation(out=RS_t[k][:, 0:n], in_=RS_t[k][:, 0:n], func=AF.Exp, scale=-0.5), reads=[RS[k]], writes=[RS[k]])

            def modl(ti, a, bb, c):
                n = bb - a
                r = b if a < NL else 2
                k = ti % 2
                kk = c % 2
                S.op("dve", lambda e: e.tensor_tensor(out=TM_t[kk][:, 0:n], in0=XR_t[:, c, a:bb], in1=RS_t[k][:, 0:n], op=ALU.mult), reads=XR.R(c, a, bb) + [RS[k]], writes=[TM[kk]])
                S.op("act", lambda e: e.activation(out=H_t[:, c, a:bb], in_=TM_t[kk][:, 0:n], func=AF.Identity, scale=mod(l, m_scale, c, r), bias=mod(l, m_shift, c, r)), reads=[TM[kk], CONST], writes=H.R(c, a, bb))
            for c in range(8):
                sq(0, *tiles[0], c)
            stat(0, *tiles[0])
            for ti in range(len(tiles)):
                nxt = ti + 1 < len(tiles)
                for c in range(8):
                    if nxt:
                        sq(ti + 1, *tiles[ti + 1], c)
                    modl(ti, *tiles[ti], c)
                if nxt:
                    stat(ti + 1, *tiles[ti + 1])

        def ktile_h(a, bb):
            return [H_t[:, k, a:bb] for k in range(8)], H.RC(range(8), a, bb)

        def ffn_stage(l, b, with_ctx):
            with contextlib.ExitStack() as st:
                S.barrier()
                tiles = TILES_ALL if with_ctx else TILES_L
                with contextlib.ExitStack() as st2:
                    norm_stage(l, b, 3, 4, tiles, st2)
                S.barrier()
                ACTS_t = sb("ACTS", [128, 22, 1024], BF16, st)
                ACTS = [Buf("acts%d" % i) for i in range(22)]
                U_t = [[sb("U%d%d" % (p, i), [128, 1028], BF16, st) for i in range(2)] for p in range(2)]
                U = [[Buf("U%d%d" % (p, i)) for i in range(2)] for p in range(2)]
                TT_t = [[sb("TT%d%d" % (p, i), [128, 1024], BF16, st) for i in range(2)] for p in range(2)]
                TT = [[Buf("TT%d%d" % (p, i)) for i in range(2)] for p in range(2)]
                wins = [(0, 1024, 0, NL), (1024, 2048, 0, NL)] + ([(NL, NT, NL, NT)] if with_ctx else [])
                for (a, bb, s0, s1) in wins:
                    W = bb - a
                    r = b if a < NL else 2
                    ua, ub = max(a - 1, s0), min(bb + 1, s1)
                    ncol = ub - ua
                    nt = (ncol + 511) // 512
                    base = (ncol + nt - 1) // nt
                    ctiles = [(ua + i * base, min(ua + (i + 1) * base, ub)) for i in range(nt)]
                    pend = [None]
                    for p_ in range(2):
                        for k_ in range(2):
                            if ua == a:
                                S.op("dve", lambda e, p_=p_, k_=k_: e.memset(U_t[p_][k_][:, 0:1], 0.0), writes=[U[p_][k_]])
                            if ub == bb:
                                S.op("dve", lambda e, p_=p_, k_=k_, W=W: e.memset(U_t[p_][k_][:, W + 1:W + 2], 0.0), writes=[U[p_][k_]])
                    for i in range(22):
                        wv, wb = load_w([wup_d[l][:, i * 128:(i + 1) * 128], wup_d[l][:, DFF + i * 128:DFF + (i + 1) * 128]], 8)
                        k = i % 2
                        for p in range(2):
                            ch = p * 22 + i
                            Ut, Ub = U_t[p][k], U[p][k]
                            for (ca, cb) in ctiles:
                                n = cb - ca
                                bank = psb()
                                rh, rb = ktile_h(ca, cb)
                                mm_group(bank, n, [(wv[:, kc, p * 128:(p + 1) * 128], rh[kc]) for kc in range(8)], rb + [wb])
                                o = 1 + ca - a
                                S.op("act", lambda e, Ut=Ut, o=o, n=n, bank=bank: e.activation(out=Ut[:, o:o + n], in_=PS_t[:, bank, 0:n], func=AF.Copy), reads=[PS[bank]], writes=[Ub])
                            Tt, Tb = TT_t[p][k], TT[p][k]
                            S.op("dve", lambda e, Tt=Tt, Ut=Ut, W=W, ch=ch: e.tensor_scalar(out=Tt[:, 0:W], in0=Ut[:, 1:W + 1], scalar1=CFW[:, l, ch, 1:2], scalar2=CFB[:, l, ch:ch + 1], op0=ALU.mult, op1=ALU.add), reads=[Ub, CONST], writes=[Tb])
                            S.op("dve", lambda e, Tt=Tt, Ut=Ut, W=W, ch=ch: e.scalar_tensor_tensor(out=Tt[:, 0:W], in0=Ut[:, 0:W], scalar=CFW[:, l, ch, 0:1], in1=Tt[:, 0:W], op0=ALU.mult, op1=ALU.add), reads=[Ub, Tb, CONST], writes=[Tb])
                            S.op("dve", lambda e, Tt=Tt, Ut=Ut, W=W, ch=ch: e.scalar_tensor_tensor(out=Tt[:, 0:W], in0=Ut[:, 2:W + 2], scalar=CFW[:, l, ch, 2:3], in1=Tt[:, 0:W], op0=ALU.mult, op1=ALU.add), reads=[Ub, Tb, CONST], writes=[Tb])
                        def fin(i=i, k=k, W=W):
                            S.op("act", lambda e: e.activation(out=TT_t[1][k][:, 0:W], in_=TT_t[1][k][:, 0:W], func=AF.Silu), reads=[TT[1][k]], writes=[TT[1][k]])
                            S.op("dve", lambda e: e.tensor_tensor(out=ACTS_t[:, i, 0:W], in0=TT_t[1][k][:, 0:W], in1=TT_t[0][k][:, 0:W], op=ALU.mult), reads=[TT[0][k], TT[1][k]], writes=[ACTS[i]])
                        if pend[0] is not None:
                            pend[0]()
                        pend[0] = fin
                    pend[0]()
                    pend[0] = None
                    otiles = [(a + i * 512, min(a + (i + 1) * 512, bb)) for i in range((W + 511) // 512)]
                    groups = [(oc, ca, cb) for oc in range(8) for (ca, cb) in otiles]
                    wts = {}
                    KS = 20
                    for gi in range(0, len(groups), 4):
                        batch = []
                        for (oc, ca, cb) in groups[gi:gi + 4]:
                            if oc not in wts:
                                wts[oc] = load_w([wdown_d[l][:, oc * 128:(oc + 1) * 128]], 22)
                            wv, wb = wts[oc]
                            n = cb - ca
                            bank = psb()

                            def fa(e, wv=wv, ca=ca, cb=cb, n=n, bank=bank):
                                ins = None
                                for kc in range(KS):
                                    ins = e.matmul(PS_t[:, bank, 0:n], lhsT=wv[:, kc, :], rhs=ACTS_t[:, kc, ca - a:cb - a], start=(kc == 0), stop=False)
                                return ins
                            S.op("pe", fa, reads=ACTS[0:KS] + [wb], writes=[PS[bank]])
                            batch.append((oc, ca, cb, n, bank, wv, wb))
                        for (oc, ca, cb, n, bank, wv, wb) in batch:
                            def fb(e, wv=wv, ca=ca, cb=cb, n=n, bank=bank):
                                ins = None
                                for kc in range(KS, 22):
                                    ins = e.matmul(PS_t[:, bank, 0:n], lhsT=wv[:, kc, :], rhs=ACTS_t[:, kc, ca - a:cb - a], start=False, stop=(kc == 21))
                                return ins
                            S.op("pe", fb, reads=ACTS[KS:22] + [wb], acc=[PS[bank]])
                            S.op("dve", lambda e, oc=oc, ca=ca, cb=cb, n=n, bank=bank, r=r: e.scalar_tensor_tensor(out=XR_t[:, oc, ca:cb], in0=PS_t[:, bank, 0:n], scalar=mod(l, 5, oc, r), in1=XR_t[:, oc, ca:cb], op0=ALU.mult, op1=ALU.add), reads=[PS[bank], CONST] + XR.R(oc, ca, cb), writes=XR.R(oc, ca, cb))

        def wout_stage(l, b, w_d, kfun, tiles):
            for oc in range(8):
                wv, wb = load_w([w_d[:, oc * 128:(oc + 1) * 128]], 8)
                for (a, bb) in tiles:
                    n = bb - a
                    r = b if a < NL else 2
                    bank = psb()
                    rh, rb = kfun(a, bb)
                    mm_group(bank, n, [(wv[:, kc, :], rh[kc]) for kc in range(8)], rb + [wb])
                    S.op("dve", lambda e, oc=oc, a=a, bb=bb, n=n, bank=bank, r=r: e.scalar_tensor_tensor(out=XR_t[:, oc, a:bb], in0=PS_t[:, bank, 0:n], scalar=mod(l, 2, oc, r), in1=XR_t[:, oc, a:bb], op0=ALU.mult, op1=ALU.add), reads=[PS[bank], CONST] + XR.R(oc, a, bb), writes=XR.R(oc, a, bb))

        def mixer_ab(l, b):
            with contextlib.ExitStack() as st:
                S.barrier()
                with contextlib.ExitStack() as st2:
                    norm_stage(l, b, 0, 1, TILES_ALL, st2)
                S.barrier()
                COS, SIN, RB = load_rope(st)
                YAC_t = sb("YAC", [128, 4, 2312], BF16, st)
                YAC = [Buf("yac%d" % c) for c in range(4)]
                with contextlib.ExitStack() as st2:
                    PP2_t = [sb("PP%d" % i, [128, 2312], BF16, st2) for i in range(2)]
                    PP2 = [Buf("PP%d" % i) for i in range(2)]
                    GBB2_t = [sb("GBB%d" % i, [128, 2312], BF16, st2) for i in range(2)]
                    GBB2 = [Buf("GBB%d" % i) for i in range(2)]
                    T_t = sb("T", [128, 2306], F32, st2)
                    T = Buf("T")
                    GCS_t = [sb("GCS%d" % i, [128, 512], BF16, st2) for i in range(2)]
                    GCS = [Buf("GCS%d" % i) for i in range(2)]
                    for i_ in range(2):
                        S.op("dve", lambda e, i_=i_: e.memset(PP2_t[i_][:], 0.0), writes=[PP2[i_]])
                        S.op("dve", lambda e, i_=i_: e.memset(GBB2_t[i_][:], 0.0), writes=[GBB2[i_]])
                    for c in range(4):
                        PP_t, PP, GBB_t, GBB = PP2_t[c % 2], PP2[c % 2], GBB2_t[c % 2], GBB2[c % 2]
                        wv, wb = load_w([winab_d[:, g * 512 + c * 128:g * 512 + (c + 1) * 128] for g in range(3)], 8)
                        for ti, (a, bb) in enumerate(TILES_ALL):
                            n = bb - a
                            rh, rb = ktile_h(a, bb)
                            banks = [psb() for _ in range(3)]
                            for g in range(3):
                                mm_group(banks[g], n, [(wv[:, kc, g * 128:(g + 1) * 128], rh[kc]) for kc in range(8)], rb + [wb])
                            k = ti % 2
                            o = pc(a)
                            S.op("act", lambda e, k=k, n=n, bk=banks[1]: e.activation(out=GCS_t[k][:, 0:n], in_=PS_t[:, bk, 0:n], func=AF.Copy), reads=[PS[banks[1]]], writes=[GCS[k]])
                            S.op("dve", lambda e, k=k, n=n, o=o, bk=banks[2]: e.tensor_tensor(out=PP_t[:, o:o + n], in0=PS_t[:, bk, 0:n], in1=GCS_t[k][:, 0:n], op=ALU.mult), reads=[PS[banks[2]], GCS[k]], writes=[PP])
                            S.op("act", lambda e, n=n, o=o, bk=banks[0]: e.activation(out=GBB_t[:, o:o + n], in_=PS_t[:, bk, 0:n], func=AF.Copy), reads=[PS[banks[0]]], writes=[GBB])
                        S.op("dve", lambda e, c=c: e.tensor_scalar(out=T_t[:], in0=PP_t[:, 1:2307], scalar1=CONVA[:, c, 1:2], scalar2=None, op0=ALU.mult), reads=[PP, CONST], writes=[T])
                        S.op("dve", lambda e, c=c: e.scalar_tensor_tensor(out=T_t[:], in0=PP_t[:, 0:2306], scalar=CONVA[:, c, 0:1], in1=T_t[:], op0=ALU.mult, op1=ALU.add), reads=[PP, T, CONST], writes=[T])
                        S.op("dve", lambda e, c=c: e.scalar_tensor_tensor(out=T_t[:], in0=PP_t[:, 2:2308], scalar=CONVA[:, c, 2:3], in1=T_t[:], op0=ALU.mult, op1=ALU.add), reads=[PP, T, CONST], writes=[T])
                        S.op("dve", lambda e, c=c: e.tensor_tensor(out=YAC_t[:, c, 1:2307], in0=T_t[:], in1=GBB_t[:, 1:2307], op=ALU.mult), reads=[T, GBB], writes=[YAC[c]])
                    S.barrier()
                YB_t = sb("YB", [128, 4, NT], BF16, st)
                YB = Grid("YB", YB_t)
                QC_t = sb("QC", [128, NT], BF16, st)
                KC_t = sb("KC", [128, NT], BF16, st)
                VC_t = sb("VC", [128, 18, 128], BF16, st)
                QC, KC, VC = Grid("QC", QC_t), Grid("KC", KC_t), Buf("VC")
                PT_t = [sb("PT%d" % i, [128, 2, 512], BF16, st) for i in range(3)]
                PT = [Buf("PT%d" % i) for i in range(3)]
                RT_t = [sb("RT0", [128, 512], BF16, st), None]
                RT = [Buf("RT0"), None]
                ZA_t = [sb("ZA%d" % i, [128, 512], BF16, st) for i in range(2)]
                ZA = [Buf("ZA%d" % i) for i in range(2)]
                RT_t[1] = sb("RT1", [128, 512], BF16, st)
                RT[1] = Buf("RT1")
                QZ_t = [sb("QZ%d" % i, [128, 512], BF16, st) for i in range(2)]
                QZ = [Buf("QZ%d" % i) for i in range(2)]
                for m_ in range(2):
                    S.op("dve", lambda e, m_=m_: e.memset(QZ_t[m_][:], 0.0), writes=[QZ[m_]])
                YR_t, YR = RT_t[1], RT[1]
                SQ1_t = sb("SQ1", [128, 512], BF16, st)
                SQ1 = Buf("SQ1")
                pend_tail = [None]
                sp_rr = [0]
                for c in range(4):
                    wvq, wbq = load_w([winab_d[:, 1536 + c * 128:1536 + (c + 1) * 128], winab_d[:, 3072 + c * 128:3072 + (c + 1) * 128]], 8)
                    wvk, wbk = load_w([winab_d[:, 2048 + c * 128:2048 + (c + 1) * 128], winab_d[:, 3584 + c * 128:3584 + (c + 1) * 128]], 8)
                    wv2, wb2 = load_w([winab_d[:, 2560 + c * 128:2560 + (c + 1) * 128]], 8)
                    for (a, bb) in TILES_ALL:
                        n = bb - a
                        rh, rb = ktile_h(a, bb)
                        for qi, (dst_t, dst_g, wv, wb) in enumerate(((QC_t, QC, wvq, wbq), (KC_t, KC, wvk, wbk))):
                            b0 = psb()
                            mm_group(b0, n, [(wv[:, kc, 0:128], rh[kc]) for kc in range(8)], rb + [wb])
                            if a < NL:
                                b1 = psb()
                                mm_group(b1, n, [(wv[:, kc, 128:256], rh[kc]) for kc in range(8)], rb + [wb])
                                S.op("dve", lambda e, a=a, bb=bb, b0=b0: e.tensor_tensor(out=RT_t[0][:, 0:512], in0=PS_t[:, b0, 0:512], in1=COS[:, a:bb], op=ALU.mult), reads=[PS[b0], RB], writes=[RT[0]])
                                S.op("dve", lambda e, a=a, bb=bb, b1=b1: e.tensor_tensor(out=RT_t[1][:, 0:512], in0=PS_t[:, b1, 0:512], in1=SIN[:, a:bb], op=ALU.mult), reads=[PS[b1], RB], writes=[RT[1]])
                                S.op("dve", lambda e, a=a, bb=bb, dst_t=dst_t: e.tensor_tensor(out=dst_t[:, a:bb], in0=RT_t[0][:, 0:512], in1=RT_t[1][:, 0:512], op=ALU.add), reads=[RT[0], RT[1]], writes=dst_g.R(0, a, bb))
                            else:
                                S.op("act", lambda e, a=a, bb=bb, n=n, b0=b0, dst_t=dst_t: e.activation(out=dst_t[:, a:bb], in_=PS_t[:, b0, 0:n], func=AF.Copy), reads=[PS[b0]], writes=dst_g.R(0, a, bb))
                    for kcnk in range(18):
                        bank = psb()
                        a = kcnk * 128
                        mm_group(bank, 128, [(H_t[:, kc, a:a + 128], wv2[:, kc, :]) for kc in range(8)], H.RC(range(8), a, a + 128) + [wb2])
                        S.op("act", lambda e, kcnk=kcnk, bank=bank: e.activation(out=VC_t[:, kcnk, :], in_=PS_t[:, bank, 0:128], func=AF.Copy), reads=[PS[bank]], writes=[VC])
                    def qz0(a, bb):
                        S.op("act", lambda e: e.activation(out=QZ_t[0][0:64, 0:bb - a], in_=QC_t[0:64, a:bb], func=AF.Copy), reads=QC.R(0, a, bb), writes=[QZ[0]])
                    for tix, (a, bb) in enumerate(TILES_ALL):
                        n = bb - a
                        kchunks = list(range(18)) if a < NL else [16, 17]
                        kpairs = [(kchunks[i], kchunks[i + 1]) for i in range(0, len(kchunks), 2)]
                        its = [(m, pr) for m in range(2) for pr in kpairs]
                        sb_of = {}
                        if tix == 0:
                            qz0(a, bb)
                        S.op("dve", lambda e, a=a, bb=bb, n=n: e.tensor_copy(out=QZ_t[1][64:128, 0:n], in_=QC_t[64:128, a:bb]), reads=QC.R(0, a, bb), writes=[QZ[1]])

                        def emit_s(j):
                            m, pr = its[j]
                            b0 = 2 * (sp_rr[0] % 2)
                            sp_rr[0] += 1
                            sb_of[j] = b0
                            for q_, kk in enumerate(pr):
                                mm_group(b0 + q_, n, [(KC_t[:, kk * 128:(kk + 1) * 128], QZ_t[m][:, 0:n])], KC.R(0, kk * 128, (kk + 1) * 128) + [QZ[m]])
                        emit_s(0)
                        for j, (m, pr) in enumerate(its):
                            if j + 1 < len(its):
                                emit_s(j + 1)
                            if j == len(kpairs) and tix + 1 < len(TILES_ALL):
                                qz0(*TILES_ALL[tix + 1])
                            if j == 6 and pend_tail[0] is not None:
                                pend_tail[0](7)
                                pend_tail[0] = None
                            b0 = sb_of[j]
                            pk = j % 3
                            S.op("act", lambda e, pk=pk, b0=b0, n=n: e.activation(out=PT_t[pk][:, :, 0:n], in_=PS_t[:, b0:b0 + 2, 0:n], func=AF.Exp, scale=0.125), reads=[PS[b0], PS[b0 + 1]], writes=[PT[pk]])
                            first, last = (pr == kpairs[0]), (pr == kpairs[-1])

                            def pv(e, m=m, pr=pr, pk=pk, n=n, first=first, last=last):
                                e.matmul(PS_t[:, 4 + m, 0:n], lhsT=VC_t[:, pr[0], :], rhs=PT_t[pk][:, 0, 0:n], start=first, stop=False)
                                return e.matmul(PS_t[:, 4 + m, 0:n], lhsT=VC_t[:, pr[1], :], rhs=PT_t[pk][:, 1, 0:n], start=False, stop=last)
                            if first:
                                S.op("pe", pv, reads=[PT[pk], VC], writes=[PS[4 + m]])
                                S.op("dve", lambda e, m=m, pk=pk, n=n: e.tensor_tensor(out=ZA_t[m][:, 0:n], in0=PT_t[pk][:, 0, 0:n], in1=PT_t[pk][:, 1, 0:n], op=ALU.add), reads=[PT[pk]], writes=[ZA[m]])
                            else:
                                S.op("pe", pv, reads=[PT[pk], VC], acc=[PS[4 + m]])
                                for q_ in range(2):
                                    S.op("dve", lambda e, m=m, pk=pk, n=n, q_=q_: e.tensor_tensor(out=ZA_t[m][:, 0:n], in0=ZA_t[m][:, 0:n], in1=PT_t[pk][:, q_, 0:n], op=ALU.add), reads=[PT[pk], ZA[m]], writes=[ZA[m]])
                            if last:
                                mm_group(6 + m, n, [(ONES[:], ZA_t[m][:, 0:n])], [ZA[m], CONST])
                        if pend_tail[0] is not None:
                            pend_tail[0](psb())
                            pend_tail[0] = None
                        for m in range(2):
                            S.op("act", lambda e, m=m, n=n: e.activation(out=RT_t[m][:, 0:n], in_=PS_t[:, 6 + m, 0:n], func=AF.Ln), reads=[PS[6 + m]], writes=[RT[m]])
                            S.op("act", lambda e, m=m, n=n: e.activation(out=RT_t[m][:, 0:n], in_=RT_t[m][:, 0:n], func=AF.Exp, scale=-1.0), reads=[RT[m]], writes=[RT[m]])
                            S.op("dve", lambda e, m=m, n=n: e.tensor_tensor(out=RT_t[m][:, 0:n], in0=PS_t[:, 4 + m, 0:n], in1=RT_t[m][:, 0:n], op=ALU.mult), reads=[PS[4 + m], RT[m]], writes=[RT[m]])
                        S.op("dve", lambda e, n=n: e.scalar_tensor_tensor(out=YR_t[:, 0:n], in0=RT_t[1][:, 0:n], scalar=NEGLAM, in1=RT_t[0][:, 0:n], op0=ALU.mult, op1=ALU.add), reads=[RT[0], RT[1], CONST], writes=[YR])
                        S.op("dve", lambda e, n=n: e.tensor_tensor(out=SQ1_t[:, 0:n], in0=YR_t[:, 0:n], in1=YR_t[:, 0:n], op=ALU.mult), reads=[YR], writes=[SQ1])
                        def tail(bk, n=n, a=a, bb=bb, c=c):
                            mm_group(bk, n, [(ONES[:], SQ1_t[:, 0:n])], [SQ1, CONST])
                            S.op("act", lambda e: e.activation(out=RT_t[0][:, 0:n], in_=PS_t[:, bk, 0:n], func=AF.Ln, bias=EPS, scale=1.0 / 128), reads=[PS[bk]], writes=[RT[0]])
                            S.op("act", lambda e: e.activation(out=RT_t[0][:, 0:n], in_=RT_t[0][:, 0:n], func=AF.Exp, scale=-0.5), reads=[RT[0]], writes=[RT[0]])
                            S.op("dve", lambda e: e.scalar_tensor_tensor(out=YB_t[:, c, a:bb], in0=YR_t[:, 0:n], scalar=SUBW[:, 0:1], in1=RT_t[0][:, 0:n], op0=ALU.mult, op1=ALU.mult), reads=[YR, RT[0], CONST], writes=YB.R(c, a, bb))
                        pend_tail[0] = tail
                    pend_tail[0](psb())
                    pend_tail[0] = None
                    if c == 0 and b == 0:
                        dd("d_qc", QC_t[:], [128, NT], BF16)
                        dd("d_kc", KC_t[:], [128, NT], BF16)
                        dd("d_cos", COS[:], [128, NL], BF16)
                        dd("d_sin", SIN[:], [128, NL], BF16)

                def kfun(a, bb):
                    o = pc(a)
                    return ([YAC_t[:, c, o:o + (bb - a)] for c in range(4)] + [YB_t[:, c, a:bb] for c in range(4)],
                            YAC + YB.RC(range(4), a, bb))
                wout_stage(l, b, woutab_d, kfun, TILES_ALL)

        def mixer_cd(l, b):
            with contextlib.ExitStack() as st:
                S.barrier()
                with contextlib.ExitStack() as st2:
                    norm_stage(l, b, 0, 1, TILES_ALL, st2)
                S.barrier()
                COS, SIN, RB = load_rope(st)
                MASKS = sb("MASKS", [128, 2, 512], BF16, st)
                ONESAB = sb("ONESAB", [128, 2, 128], BF16, st)
                POOLW = sb("POOLW", [128, 4, 128], BF16, st)
                ICNT = sb("ICNT", [128, 4, 16], F32, st)
                CB = Buf("cdconst")
                S.sync_to("pool")
                S.dma("pool", [(MASKS[:], mask_d), (ONESAB[:], onesab_d), (POOLW[:], poolw_d)], None, CB, writes=[CB])
                CB2 = Buf("cdconst2")
                S.dma("sp", ICNT[:], icnt_d, CB2, writes=[CB2], nowait=True)
                YD_t = sb("YD", [128, 4, NL], BF16, st)
                YD = Grid("YD", YD_t)
                with contextlib.ExitStack() as st2:
                    PX_t = sb("PX", [128, 2080], F32, st2)
                    PA_t = sb("PA", [128, 2080], F32, st2)
                    PB_t = sb("PB", [128, 2080], F32, st2)
                    PX, PA, PB = Buf("PX"), Buf("PA"), Buf("PB")
                    ZG_t = sb("ZG", [128, NL], BF16, st2)
                    ZG = Buf("ZG")
                    ET_t = sb("ET", [128, 16], F32, st2)
                    ET = Buf("ET")
                    S.op("dve", lambda e: e.memset(PX_t[:], 0.0), writes=[PX])
                    S.op("dve", lambda e: e.memset(PA_t[:], 0.0), writes=[PA])
                    S.op("dve", lambda e: e.memset(PB_t[:], 0.0), writes=[PB])
                    for g in range(4):
                        wv, wb = load_w([wincd_d[:, 1408 + g * 128:1408 + (g + 1) * 128]], 8)
                        for (a, bb) in TILES_L:
                            rh, rb = ktile_h(a, bb)
                            bank = psb()
                            mm_group(bank, 512, [(wv[:, kc, :], rh[kc]) for kc in range(8)], rb + [wb])
                            S.op("act", lambda e, a=a, bank=bank: e.activation(out=PX_t[:, 16 + a:16 + a + 512], in_=PS_t[:, bank, 0:512], func=AF.Copy), reads=[PS[bank]], writes=[PX])
                        w = (2, 4, 8, 16)[g]
                        cur_t, cur = PX_t, PX
                        sh = 1
                        dsts = [(PA_t, PA), (PB_t, PB)]
                        di_ = 0
                        while sh < w:
                            dt_, db_ = dsts[di_ % 2]
                            di_ += 1
                            S.op("dve", lambda e, dt_=dt_, cur_t=cur_t, sh=sh: e.tensor_tensor(out=dt_[:, 15:2080], in0=cur_t[:, 15:2080], in1=cur_t[:, 15 - sh:2080 - sh], op=ALU.add), reads=[cur], writes=[db_])
                            cur_t, cur = dt_, db_
                            sh *= 2
                        o = 16 + w // 2 - 1
                        S.op("dve", lambda e, cur_t=cur_t, o=o, w=w: e.scalar_tensor_tensor(out=ZG_t[:], in0=cur_t[:, o:o + NL], scalar=1.0 / w, in1=PX_t[:, 16:16 + NL], op0=ALU.mult, op1=ALU.subtract), reads=[cur, PX], writes=[ZG])
                        for side in range(2):
                            t0 = 0 if side == 0 else NL - 8
                            S.op("dve", lambda e, cur_t=cur_t, o=o, t0=t0, g=g, side=side: e.tensor_tensor(out=ET_t[:, 0:8], in0=cur_t[:, o + t0:o + t0 + 8], in1=ICNT[:, g, side * 8:side * 8 + 8], op=ALU.mult), reads=[cur, CONST, CB, CB2, RB], writes=[ET])
                            S.op("dve", lambda e, t0=t0: e.tensor_tensor(out=ZG_t[:, t0:t0 + 8], in0=ET_t[:, 0:8], in1=PX_t[:, 16 + t0:16 + t0 + 8], op=ALU.subtract), reads=[ET, PX, ZG], writes=[ZG])
                        for (a, bb) in TILES_L:
                            bank = psb()
                            mm_group(bank, 512, [(POOLW[:, g, :], ZG_t[:, a:bb])], [ZG, CONST, CB, RB])
                            S.op("act", lambda e, a=a, bb=bb, g=g, bank=bank: e.activation(out=YD_t[:, g, a:bb], in_=PS_t[:, bank, 0:512], func=AF.Identity, scale=PSCALE[:, g:g + 1]), reads=[PS[bank], CONST, CB, RB], writes=YD.R(g, a, bb))
                    S.barrier()
                QT_t = sb("QT", [128, 4, NL], BF16, st)
                QT = Grid("QT", QT_t)
                KT_t = sb("KT", [128, 2, NT], BF16, st)
                KT = Grid("KT", KT_t)
                S.op("dve", lambda e: e.memset(KT_t[:], 0.0), writes=KT.RC(range(2), 0, NT))
                VAB_t = sb("VAB", [128, 2, 18, 128], BF16, st)
                VAB = Buf("VAB")
                YA1_t, YA1 = QT_t, QT
                RT_t = [sb("RTc%d" % i, [128, 512], F32, st) for i in range(2)]
                RT = [Buf("RTc%d" % i) for i in range(2)]
                PT_t = [sb("PTc%d" % i, [128, 512], BF16, st) for i in range(4)]
                PT = [Buf("PTc%d" % i) for i in range(4)]
                S.op("dve", lambda e: e.memset(VAB_t[:], 0.0), writes=[VAB])

                def rope_proj(wv, wb, ci, dst_fn, dst_bufs_fn, tiles):
                    for (a, bb) in tiles:
                        n = bb - a
                        rh, rb = ktile_h(a, bb)
                        b0 = psb()
                        mm_group(b0, n, [(wv[:, kc, ci * 128:(ci + 1) * 128], rh[kc]) for kc in range(8)], rb + [wb])
                        if a < NL:
                            b1 = psb()
                            mm_group(b1, n, [(wv[:, kc, (ci + 1) * 128:(ci + 2) * 128], rh[kc]) for kc in range(8)], rb + [wb])
                            S.op("dve", lambda e, a=a, bb=bb, b0=b0: e.tensor_tensor(out=RT_t[0][:, 0:512], in0=PS_t[:, b0, 0:512], in1=COS[:, a:bb], op=ALU.mult), reads=[PS[b0], CONST, CB, RB], writes=[RT[0]])
                            S.op("dve", lambda e, a=a, bb=bb, b1=b1: e.tensor_tensor(out=RT_t[1][:, 0:512], in0=PS_t[:, b1, 0:512], in1=SIN[:, a:bb], op=ALU.mult), reads=[PS[b1], CONST, CB, RB], writes=[RT[1]])
                            if dst_fn is not None:
                                S.op("dve", lambda e, a=a, bb=bb: e.tensor_tensor(out=dst_fn(a, bb), in0=RT_t[0][:, 0:512], in1=RT_t[1][:, 0:512], op=ALU.add), reads=[RT[0], RT[1]], writes=dst_bufs_fn(a, bb))
                            else:
                                for kv_ in range(2):
                                    S.op("dve", lambda e, a=a, bb=bb, kv_=kv_: e.tensor_tensor(out=KT_t[64 * kv_:64 * kv_ + 64, kv_, a:bb], in0=RT_t[0][64 * kv_:64 * kv_ + 64, 0:512], in1=RT_t[1][64 * kv_:64 * kv_ + 64, 0:512], op=ALU.add), reads=[RT[0], RT[1]], writes=dst_bufs_fn(a, bb))
                        else:
                            if dst_fn is not None:
                                S.op("act", lambda e, a=a, bb=bb, n=n, b0=b0: e.activation(out=dst_fn(a, bb), in_=PS_t[:, b0, 0:n], func=AF.Copy), reads=[PS[b0]], writes=dst_bufs_fn(a, bb))
                            else:
                                for kv_ in range(2):
                                    S.op("act", lambda e, a=a, bb=bb, n=n, b0=b0, kv_=kv_: e.activation(out=KT_t[64 * kv_:64 * kv_ + 64, kv_, a:bb], in_=PS_t[64 * kv_:64 * kv_ + 64, b0, 0:n], func=AF.Copy), reads=[PS[b0]], writes=dst_bufs_fn(a, bb))
                for g in range(4):
                    wv, wb = load_w([wincd_d[:, g * 128:(g + 1) * 128], wincd_d[:, 512 + g * 128:512 + (g + 1) * 128]], 8)
                    rope_proj(wv, wb, 0, lambda a, bb, g=g: QT_t[:, g, a:bb], lambda a, bb, g=g: QT.R(g, a, bb), TILES_L)
                wv, wb = load_w([wincd_d[:, 1024:1152], wincd_d[:, 1152:1280], wincd_d[:, 1280:1408]], 8)
                rope_proj(wv, wb, 0, None, lambda a, bb: KT.RC(range(2), a, bb), TILES_ALL)
                for kcnk in range(18):
                    bank = psb()
                    a = kcnk * 128
                    mm_group(bank, 128, [(H_t[:, kc, a:a + 128], wv[:, kc, 256:384]) for kc in range(8)], H.RC(range(8), a, a + 128) + [wb])
                    S.op("act", lambda e, kcnk=kcnk, bank=bank: e.activation(out=VAB_t[:, 0, kcnk, 0:64], in_=PS_t[:, bank, 0:64], func=AF.Copy), reads=[PS[bank]], writes=[VAB])
                    S.op("dve", lambda e, kcnk=kcnk, bank=bank: e.tensor_copy(out=VAB_t[:, 1, kcnk, 64:128], in_=PS_t[:, bank, 64:128]), reads=[PS[bank]], writes=[VAB])
                for i in range(16):
                    qa, qb = i * 128, (i + 1) * 128
                    kl = [(16, None), (17, None)]
                    if i > 0:
                        kl.append((i - 1, 0))
                    kl.append((i, None))
                    if i < 15:
                        kl.append((i + 1, 1))
                    its = [(kv, kk, mk) for kv in range(2) for (kk, mk) in kl]
                    sbanks = {}

                    def emit_s(j):
                        kv, kk, mk = its[j]
                        bk = psb()
                        sbanks[j] = bk
                        mm_group(bk, 512, [(KT_t[:, kv, kk * 128:(kk + 1) * 128], QT_t[:, :, qa:qb])], KT.R(kv, kk * 128, (kk + 1) * 128) + QT.RC(range(4), qa, qb))
                    emit_s(0)
                    emit_s(1)
                    for j, (kv, kk, mk) in enumerate(its):
                        if j + 2 < len(its):
                            emit_s(j + 2)
                        bk = sbanks[j]
                        pk = j % 4
                        S.op("act", lambda e, pk=pk, bk=bk: e.activation(out=PT_t[pk][:], in_=PS_t[:, bk, :], func=AF.Exp, scale=0.125), reads=[PS[bk]], writes=[PT[pk]])
                        if mk is not None:
                            S.op("dve", lambda e, pk=pk, mk=mk: e.tensor_tensor(out=PT_t[pk][:], in0=PT_t[pk][:], in1=MASKS[:, mk, :], op=ALU.mult), reads=[PT[pk], CONST, CB, RB], writes=[PT[pk]])
                        first, last = (j == 0), (j == len(its) - 1)

                        ob = 4 + 2 * (i % 2)

                        def pv(e, kv=kv, kk=kk, pk=pk, first=first, last=last, ob=ob):
                            e.matmul(PS_t[:, ob, :], lhsT=VAB_t[:, kv, kk, :], rhs=PT_t[pk][:], start=first, stop=last)
                            return e.matmul(PS_t[:, ob + 1, :], lhsT=ONESAB[:, kv, :], rhs=PT_t[pk][:], start=first, stop=last)
                        if first:
                            S.op("pe", pv, reads=[PT[pk], VAB, CONST, CB, RB], writes=[PS[ob], PS[ob + 1]])
                        else:
                            S.op("pe", pv, reads=[PT[pk], VAB, CONST, CB, RB], acc=[PS[ob], PS[ob + 1]])
                    ob = 4 + 2 * (i % 2)
                    rk = i % 2
                    for g in range(4):
                        S.op("act", lambda e, g=g, ob=ob, rk=rk: e.activation(out=RT_t[rk][:, g * 128:(g + 1) * 128], in_=PS_t[:, ob + 1, g * 128:(g + 1) * 128], func=AF.Ln, bias=SINKE[:, g:g + 1]), reads=[PS[ob + 1], CONST, CB, RB], writes=[RT[rk]])
                    S.op("act", lambda e, rk=rk: e.activation(out=RT_t[rk][:], in_=RT_t[rk][:], func=AF.Exp, scale=-1.0), reads=[RT[rk]], writes=[RT[rk]])
                    S.op("dve", lambda e, qa=qa, qb=qb, ob=ob, rk=rk: e.tensor_tensor(out=YA1_t[:, :, qa:qb], in0=PS_t[:, ob, :].rearrange("p (g n) -> p g n", g=4), in1=RT_t[rk][:].rearrange("p (g n) -> p g n", g=4), op=ALU.mult), reads=[PS[ob], RT[rk]], writes=YA1.RC(range(4), qa, qb))

                def kfun(a, bb):
                    return ([YA1_t[:, g, a:bb] for g in range(4)] + [YD_t[:, g, a:bb] for g in range(4)],
                            YA1.RC(range(4), a, bb) + YD.RC(range(4), a, bb))
                wout_stage(l, b, woutcd_d, kfun, TILES_L)

        def final_stage(b):
            with contextlib.ExitStack() as st:
                S.barrier()
                SQ_t = sb("SQf", [128, 8, 512], BF16, st)
                SQ = [Buf("SQf%d" % c) for c in range(8)]
                RS_t = sb("RSf", [128, 512], F32, st)
                RS = Buf("RSf")
                YF_t = sb("YF", [128, 8, 512], F32, st)
                YF = [Buf("YF%d" % c) for c in range(8)]
                OS_t = [sb("OS%d" % i, [128, D], F32, st) for i in range(2)]
                OS = [Buf("OS%d" % i) for i in range(2)]
                for (a, bb) in TILES_L:
                    for c in range(8):
                        if c < 4:
                            S.op("act", lambda e, c=c, a=a, bb=bb: e.activation(out=SQ_t[:, c, :], in_=XR_t[:, c, a:bb], func=AF.Square), reads=XR.R(c, a, bb), writes=[SQ[c]])
                        else:
                            S.op("dve", lambda e, c=c, a=a, bb=bb: e.tensor_tensor(out=SQ_t[:, c, :], in0=XR_t[:, c, a:bb], in1=XR_t[:, c, a:bb], op=ALU.mult), reads=XR.R(c, a, bb), writes=[SQ[c]])
                    bank = psb()
                    mm_group(bank, 512, [(ONES[:], SQ_t[:, c, :]) for c in range(8)], SQ + [CONST])
                    S.op("act", lambda e, bank=bank: e.activation(out=RS_t[:], in_=PS_t[:, bank, :], func=AF.Ln, bias=EPS, scale=1.0 / D), reads=[PS[bank]], writes=[RS])
                    S.op("act", lambda e: e.activation(out=RS_t[:], in_=RS_t[:], func=AF.Exp, scale=-0.5), reads=[RS], writes=[RS])
                    for c in range(8):
                        S.op("dve", lambda e, c=c, a=a, bb=bb: e.scalar_tensor_tensor(out=YF_t[:, c, :], in0=XR_t[:, c, a:bb], scalar=FNW[:, c:c + 1], in1=RS_t[:], op0=ALU.mult, op1=ALU.mult), reads=XR.R(c, a, bb) + [RS, CONST], writes=[YF[c]])
                    for s in range(4):
                        tok = a + s * 128
                        k = (tok // 128) % 2
                        for hh in range(2):
                            bank = psb()

                            def tp(e, hh=hh, bank=bank, s=s):
                                ins = None
                                for q in range(4):
                                    c = hh * 4 + q
                                    ins = e.transpose(out=PS_t[:, bank, q * 128:(q + 1) * 128], in_=YF_t[:, c, s * 128:(s + 1) * 128], identity=IDENT[:])
                                return ins
                            S.op("pe", tp, reads=[YF[hh * 4 + q] for q in range(4)] + [CONST], writes=[PS[bank]])
                            if hh == 0:
                                S.op("act", lambda e, k=k, bank=bank: e.activation(out=OS_t[k][:, 0:512], in_=PS_t[:, bank, :], func=AF.Copy), reads=[PS[bank]], writes=[OS[k]])
                            else:
                                S.op("dve", lambda e, k=k, bank=bank: e.tensor_copy(out=OS_t[k][:, 512:1024], in_=PS_t[:, bank, :]), reads=[PS[bank]], writes=[OS[k]])
                        S.dma("sp", out_d[b, tok:tok + 128, :], OS_t[k][:], OS[k], reads=[OS[k]])
                S.barrier()
                for k in range(2):
                    if OS[k].dsem is not None:
                        nc.sync.wait_ge(OS[k].dsem, OS[k].dcnt)

        def dump_dbg():
            S.barrier()
            DB = Buf("dbg")
            S.dma("sp", dbg_d, XR_t[:], DB, reads=[bb_ for v in XR.b.values() for bb_ in [v]])
            nc.sync.wait_ge(DB.dsem, DB.dcnt)

        stages = [("mixer0", lambda b: mixer_ab(0, b)), ("ffn0", lambda b: ffn_stage(0, b, True)),
                  ("mixer1", lambda b: mixer_cd(1, b)), ("ffn1", lambda b: ffn_stage(1, b, False))]
        for b in range(BPC):
            load_x(b)
            stop = False
            for name, fn in stages:
                fn(b)
                if b == 0 and name == "mixer0":
                    compute_mods(1)
                if DEBUG_STOP == name:
                    stop = True
                    break
            if stop:
                dump_dbg()
                break
            final_stage(b)
    return nc


def _fm(v):
    v = np.asarray(v, np.float32)
    n = v.shape[-1] // 128
    return np.ascontiguousarray(np.moveaxis(v.reshape(v.shape[:-1] + (n, 128)), -1, 0))


def _rope_tables():
    pos = np.arange(NL)
    row = (pos // 64).astype(np.float32)
    col = (pos % 64).astype(np.float32)
    inv = (1.0 / (np.float32(10000.0) ** (np.arange(0, 32, 2, dtype=np.float32) / np.float32(32)))).astype(np.float32)
    cos_t = np.zeros((128, NL), np.float32)
    sin_t = np.zeros((128, NL), np.float32)
    for p in range(128):
        d = p % 64
        ax = row if d < 32 else col
        dd = d % 32
        ang = (ax * inv[dd % 16]).astype(np.float32)
        cos_t[p] = np.cos(ang)
        sin_t[p] = -np.sin(ang) if dd < 16 else np.sin(ang)
    return cos_t, sin_t


def _partner_perm(ncols):
    idx = np.arange(ncols)
    d = idx % 64
    dd = d % 32
    return np.where(dd < 16, idx + 16, idx - 16)


_PROG = {}


def _prepare(inputs):
    f = lambda k: np.asarray(inputs[k], np.float32)
    x, c, ctx, c_ctx = f("x"), f("c"), f("ctx"), f("c_ctx")
    shared = {}
    shared["w_mod"] = f("w_mod")
    shared["b_mod"] = np.ascontiguousarray(_fm(f("b_mod")))
    w = f("w_in_ab")[0]
    q, k = w[:, 1536:2048], w[:, 2048:2560]
    pp = _partner_perm(512)
    shared["w_in_ab"] = np.ascontiguousarray(np.concatenate([w, q[:, pp], k[:, pp]], axis=1))
    shared["conv_a"] = np.ascontiguousarray(np.transpose(f("conv_a")[0].reshape(3, 4, 128), (2, 1, 0)))
    shared["lam_qk"] = np.ascontiguousarray(np.broadcast_to(f("lam_qk")[0][None], (128, 4, 64)))
    shared["subln"] = np.ascontiguousarray(f("subln_b")[0].reshape(128, 1))
    shared["w_out_ab"] = f("w_out_ab")[0]
    w = f("w_in_cd")[0]
    qh = w[:, 0:512].reshape(D, 8, 64)
    qg = np.concatenate([np.concatenate([qh[:, g], qh[:, 4 + g]], axis=1) for g in range(4)], axis=1)
    kk = w[:, 512:640]
    shared["w_in_cd"] = np.ascontiguousarray(np.concatenate([qg, qg[:, _partner_perm(512)], kk, kk[:, _partner_perm(128)], w[:, 640:768], w[:, 768:1280]], axis=1))
    sk = f("sink_c")[0]
    shared["sink"] = np.ascontiguousarray(np.concatenate([np.broadcast_to(sk[None, 0:4], (64, 4)), np.broadcast_to(sk[None, 4:8], (64, 4))], axis=0))
    shared["pool_w"] = np.ascontiguousarray(np.transpose(f("pool_w")[0], (1, 0, 2)))
    shared["pool_scale"] = np.ascontiguousarray(f("pool_scale")[0].reshape(4, 128).T)
    wo = f("w_out_cd")[0]
    rows = []
    for g in range(4):
        rows += list(range(g * 64, g * 64 + 64)) + list(range((4 + g) * 64, (4 + g) * 64 + 64))
    rows += list(range(512, 1024))
    shared["w_out_cd"] = np.ascontiguousarray(wo[np.array(rows)])
    shared["w_up"] = f("w_up")
    shared["cfw"] = np.ascontiguousarray(np.transpose(f("conv_ffn_w").reshape(2, 3, 44, 128), (3, 0, 2, 1)))
    shared["cfb"] = np.ascontiguousarray(np.transpose(f("conv_ffn_b").reshape(2, 44, 128), (2, 0, 1)))
    shared["w_down"] = f("w_down")
    shared["fnw"] = np.ascontiguousarray(f("final_norm_w").reshape(8, 128).T)
    shared["ident"] = np.eye(128, dtype=np.float32)
    shared["cos_t"], shared["sin_t"] = _rope_tables()
    kk_, qq_ = np.meshgrid(np.arange(128), np.arange(128), indexing="ij")
    m_prev = (kk_ >= qq_).astype(np.float32)
    m_next = (kk_ <= qq_).astype(np.float32)
    shared["masks"] = np.ascontiguousarray(np.stack([np.tile(m_prev, (1, 4)), np.tile(m_next, (1, 4))], axis=1))
    ic = np.zeros((4, 16), np.float32)
    for g, wdt in enumerate((2, 4, 8, 16)):
        for j in range(8):
            t = j
            ic[g, j] = 1.0 / (min(t + wdt // 2, NL) - max(t - wdt // 2, 0))
            t = NL - 8 + j
            ic[g, 8 + j] = 1.0 / (min(t + wdt // 2, NL) - max(t - wdt // 2, 0))
    shared["icnt"] = np.ascontiguousarray(np.broadcast_to(ic[None], (128, 4, 16)))
    oab = np.zeros((128, 2, 128), np.float32)
    oab[:, 0, 0:64] = 1.0
    oab[:, 1, 64:128] = 1.0
    shared["onesab"] = oab
    in_maps = []
    for i in range(NCORES):
        m = dict(shared)
        m["x"] = np.ascontiguousarray(x[BPC * i:BPC * (i + 1)])
        m["ctx"] = np.ascontiguousarray(ctx[BPC * i:BPC * (i + 1)])
        cv = np.stack([c[BPC * i], c[BPC * i + 1], c_ctx], axis=0)
        m["cT"] = np.ascontiguousarray(np.transpose(cv.reshape(3, 8, 128), (2, 1, 0)))
        in_maps.append(m)
    return in_maps


def kernel(**inputs):
    if "nc" not in _PROG:
        _PROG["nc"] = build_program()
    in_maps = _prepare(inputs)
    res = run_bass_kernel_spmd(_PROG["nc"], in_maps, core_ids=list(range(NCORES)))
    return np.concatenate([r["out"] for r in res.results], axis=0).astype(np.float32)
```
